# Optimizing a Trainium2 kernel written in Bass

```python
import math
import jax, jax.numpy as jnp
from jax import lax
import numpy as np

D_MODEL = 1024
BATCH = 16
SEQ = 4096
DEPTH = 2

CONV_WIDTH = D_MODEL // 2
CONV_K = 31
DN_DK = 128
DN_DV = 128
DN_HEADS = D_MODEL // 128
DN_SHORT_K = 4
DN_CHUNK = 64
SG_WIDTH = D_MODEL // 2
SG_GROUPS = 4
SG_CHUNK = 128
N_BRANCH = 3
NORM_EPS = 1e-6

SPLIT_SIZES = (
    CONV_WIDTH,
    CONV_WIDTH,
    CONV_WIDTH,
    DN_HEADS * (2 * DN_DK + DN_DV),
    DN_HEADS * DN_DV,
    DN_HEADS,
    DN_HEADS,
    SG_WIDTH,
    SG_WIDTH,
    SG_WIDTH,
    N_BRANCH * D_MODEL,
)
N_IN_COLS = sum(SPLIT_SIZES)

kernel_name = "hybrid_conv_deltanet_gmlp_gated_merge"


def _rmsnorm(x, g):
    xf = x.astype(jnp.float32)
    y = xf * lax.rsqrt(jnp.mean(xf * xf, axis=-1, keepdims=True) + NORM_EPS)
    return (y * g.astype(jnp.float32)).astype(x.dtype)


def _layernorm(x, g, b):
    xf = x.astype(jnp.float32)
    mu = jnp.mean(xf, axis=-1, keepdims=True)
    var = jnp.mean(jnp.square(xf - mu), axis=-1, keepdims=True)
    y = (xf - mu) * lax.rsqrt(var + NORM_EPS)
    return (y * g.astype(jnp.float32) + b.astype(jnp.float32)).astype(x.dtype)


def _l2norm(x):
    xf = x.astype(jnp.float32)
    return xf * lax.rsqrt(jnp.sum(xf * xf, axis=-1, keepdims=True) + NORM_EPS)


def _causal_dwconv(x, w):
    k, c = w.shape
    xp = jnp.pad(x, ((0, 0), (k - 1, 0), (0, 0)))
    return lax.conv_general_dilated(
        xp, w.astype(x.dtype)[:, None, :], window_strides=(1,), padding="VALID",
        dimension_numbers=("NWC", "WIO", "NWC"), feature_group_count=c)


def _gated_delta_rule(q, k, v, beta, g):
    bsz, t, h, dk = q.shape
    dv = v.shape[-1]
    c = DN_CHUNK
    n = t // c

    def chunks(a):
        a = a.reshape((bsz, n, c, h) + a.shape[3:])
        return jnp.moveaxis(a, 3, 1)

    q, k, v, beta, g = chunks(q), chunks(k), chunks(v), chunks(beta), chunks(g)
    gc = jnp.cumsum(g, axis=-1)
    diff = gc[..., :, None] - gc[..., None, :]
    incl = jnp.tril(jnp.ones((c, c), dtype=bool))
    strict = jnp.tril(jnp.ones((c, c), dtype=bool), k=-1)
    gamma_incl = jnp.exp(jnp.where(incl, diff, -jnp.inf))
    gamma_strict = jnp.exp(jnp.where(strict, diff, -jnp.inf))

    kk = jnp.einsum("bhnid,bhnjd->bhnij", k, k)
    a_mat = jnp.eye(c, dtype=jnp.float32) + beta[..., :, None] * kk * gamma_strict
    rhs = jnp.concatenate([v * beta[..., None],
                           k * (beta * jnp.exp(gc))[..., None]], axis=-1)
    sol = lax.linalg.triangular_solve(a_mat, rhs, left_side=True, lower=True,
                                      unit_diagonal=True)
    u, w = sol[..., :dv], sol[..., dv:]

    qk = jnp.einsum("bhnid,bhnjd->bhnij", q, k) * gamma_incl
    q_dec = q * jnp.exp(gc)[..., None]
    k_dec = k * jnp.exp(gc[..., -1:] - gc)[..., None]
    d_last = jnp.exp(gc[..., -1])

    def step(s, xs):
        u_c, w_c, qd_c, qk_c, kd_c, dl_c = xs
        v_new = u_c - jnp.einsum("bhcd,bhde->bhce", w_c, s)
        o_c = (jnp.einsum("bhcd,bhde->bhce", qd_c, s)
               + jnp.einsum("bhij,bhje->bhie", qk_c, v_new))
        s = dl_c[..., None, None] * s + jnp.einsum("bhcd,bhce->bhde", kd_c, v_new)
        return s, o_c

    xs = tuple(jnp.moveaxis(a, 2, 0) for a in (u, w, q_dec, qk, k_dec, d_last))
    s0 = jnp.zeros((bsz, h, dk, dv), jnp.float32)
    _, o = lax.scan(step, s0, xs)
    o = jnp.transpose(o, (1, 0, 3, 2, 4))
    return o.reshape(bsz, t, h, dv)


def _layer(x, norm_g, w_in, a_dw, a_dw_b, a_ln_g, a_ln_b, a_proj,
           b_conv, b_a_log, b_dt_bias, b_onorm_g, b_proj,
           c_ln_g, c_ln_b, c_ws, c_bs, c_proj, w_out):
    bsz, t, _ = x.shape
    h = _rmsnorm(x, norm_g)
    proj = h @ w_in
    idx = [int(i) for i in np.cumsum(SPLIT_SIZES)[:-1]]
    (a_val, a_glu, a_z, qkv, b_z, b_beta, b_alpha,
     c_u, c_v, c_z, gate_logits) = jnp.split(proj, idx, axis=-1)

    a = a_val * jax.nn.sigmoid(a_glu)
    a = _causal_dwconv(a, a_dw) + a_dw_b
    a = _layernorm(a, a_ln_g, a_ln_b)
    y_a = jax.nn.silu(a) * jax.nn.silu(a_z)

    qkv = jax.nn.silu(_causal_dwconv(qkv, b_conv))
    q, k, v = jnp.split(qkv, [DN_HEADS * DN_DK, 2 * DN_HEADS * DN_DK], axis=-1)
    q = _l2norm(q.reshape(bsz, t, DN_HEADS, DN_DK)) * (DN_DK ** -0.5)
    k = _l2norm(k.reshape(bsz, t, DN_HEADS, DN_DK))
    v = v.reshape(bsz, t, DN_HEADS, DN_DV).astype(jnp.float32)
    beta = jax.nn.sigmoid(b_beta.astype(jnp.float32))
    g = -jnp.exp(b_a_log.astype(jnp.float32)) * jax.nn.softplus(
        b_alpha.astype(jnp.float32) + b_dt_bias.astype(jnp.float32))
    o = _gated_delta_rule(q, k, v, beta, g)
    o = _rmsnorm(o, b_onorm_g)
    z = jax.nn.silu(b_z.reshape(bsz, t, DN_HEADS, DN_DV).astype(jnp.float32))
    y_b = (o * z).reshape(bsz, t, DN_HEADS * DN_DV).astype(x.dtype)

    u = jax.nn.gelu(c_u)
    vs = _layernorm(jax.nn.gelu(c_v), c_ln_g, c_ln_b)
    vs = vs.reshape(bsz, t // SG_CHUNK, SG_CHUNK, SG_GROUPS, SG_WIDTH // SG_GROUPS)
    ws = jnp.tril(c_ws)
    mixed = jnp.einsum("gij,bnjgc->bnigc", ws, vs) + c_bs.T[:, :, None]
    y_c = u * mixed.reshape(bsz, t, SG_WIDTH) * jax.nn.silu(c_z)

    gates = jax.nn.sigmoid(gate_logits).reshape(bsz, t, N_BRANCH, D_MODEL)
    merged = (gates[..., 0, :] * (y_a @ a_proj)
              + gates[..., 1, :] * (y_b @ b_proj)
              + gates[..., 2, :] * (y_c @ c_proj))
    return x + merged @ w_out


def setup_inputs(seed: int = 0) -> dict:
    key = jax.random.key(seed)
    ks = jax.random.split(key, 24)
    f32 = jnp.float32
    L = DEPTH

    def nrm(k, shape, fan_in):
        return jax.random.normal(k, shape, f32) * (fan_in ** -0.5)

    def gain(k, shape):
        return 1.0 + 0.02 * jax.random.normal(k, shape, f32)

    def bias(k, shape):
        return 0.02 * jax.random.normal(k, shape, f32)

    dt = jnp.exp(jax.random.uniform(ks[10], (L, DN_HEADS), f32,
                                    math.log(1e-3), math.log(1e-1)))
    return {
        "x": jax.random.normal(ks[0], (BATCH, SEQ, D_MODEL), f32),
        "norm_g": gain(ks[1], (L, D_MODEL)),
        "w_in": nrm(ks[2], (L, D_MODEL, N_IN_COLS), D_MODEL),
        "a_dw": nrm(ks[3], (L, CONV_K, CONV_WIDTH), CONV_K),
        "a_dw_b": bias(ks[4], (L, CONV_WIDTH)),
        "a_ln_g": gain(ks[5], (L, CONV_WIDTH)),
        "a_ln_b": bias(ks[6], (L, CONV_WIDTH)),
        "a_proj": nrm(ks[7], (L, CONV_WIDTH, D_MODEL), CONV_WIDTH),
        "b_conv": nrm(ks[8], (L, DN_SHORT_K, DN_HEADS * (2 * DN_DK + DN_DV)), DN_SHORT_K),
        "b_a_log": jnp.log(jax.random.uniform(ks[9], (L, DN_HEADS), f32, 1.0, 16.0)),
        "b_dt_bias": dt + jnp.log(-jnp.expm1(-dt)),
        "b_onorm_g": gain(ks[11], (L, DN_DV)),
        "b_proj": nrm(ks[12], (L, DN_HEADS * DN_DV, D_MODEL), DN_HEADS * DN_DV),
        "c_ln_g": gain(ks[13], (L, SG_WIDTH)),
        "c_ln_b": bias(ks[14], (L, SG_WIDTH)),
        "c_ws": nrm(ks[15], (L, SG_GROUPS, SG_CHUNK, SG_CHUNK), SG_CHUNK),
        "c_bs": gain(ks[16], (L, SG_GROUPS, SG_CHUNK)),
        "c_proj": nrm(ks[17], (L, SG_WIDTH, D_MODEL), SG_WIDTH),
        "w_out": nrm(ks[18], (L, D_MODEL, D_MODEL), D_MODEL),
        "final_g": gain(ks[19], (D_MODEL,)),
    }


def reference(x, norm_g, w_in, a_dw, a_dw_b, a_ln_g, a_ln_b, a_proj,
              b_conv, b_a_log, b_dt_bias, b_onorm_g, b_proj,
              c_ln_g, c_ln_b, c_ws, c_bs, c_proj, w_out, final_g):
    for l in range(DEPTH):
        x = _layer(x, norm_g[l], w_in[l], a_dw[l], a_dw_b[l], a_ln_g[l], a_ln_b[l],
                   a_proj[l], b_conv[l], b_a_log[l], b_dt_bias[l], b_onorm_g[l],
                   b_proj[l], c_ln_g[l], c_ln_b[l], c_ws[l], c_bs[l], c_proj[l],
                   w_out[l])
    return _rmsnorm(x, final_g)
```

```python
import numpy as np
import concourse.bass as bass
import concourse.mybir as mybir
from concourse.bass_utils import run_bass_kernel_spmd

F32 = mybir.dt.float32
BF16 = mybir.dt.bfloat16
AF = mybir.ActivationFunctionType
ALU = mybir.AluOpType

P = 128
D = 1024
KC = 8
H = 8
NIN = 10256
EPS = 1e-6
CK = 31
NCORES = 8


class Buf:
    __slots__ = ("name", "w", "r", "dsem", "dcnt")

    def __init__(self, name):
        self.name = name
        self.w = None
        self.r = {}
        self.dsem = None
        self.dcnt = 0


class Sched:
    EPOCH = 8000

    def __init__(self, nc):
        self.nc = nc
        self.eng = {"pe": nc.tensor, "act": nc.scalar, "dve": nc.vector, "pool": nc.gpsimd, "sp": nc.sync}
        self.sems = {}
        self.cnt = {}
        self.epoch = {}
        self.seen = {e: {} for e in self.eng}
        self.nsem = 0
        for e in self.eng:
            self.epoch[e] = 0
            self._new_epoch_sem(e)
        self.n_ops = {e: 0 for e in self.eng}
        self.dry = False

    def _new_sem(self):
        self.nsem += 1
        return self.nc.alloc_semaphore("s%d" % self.nsem)

    def _new_epoch_sem(self, e):
        key = (e, self.epoch[e])
        self.sems[key] = self._new_sem()
        self.cnt[e] = 0

    def buf(self, name="b"):
        return Buf(name)

    def _deps(self, reads, writes):
        deps = {}
        for b in reads:
            if b.w is not None:
                k, v = b.w
                if deps.get(k, 0) < v:
                    deps[k] = v
        for b in writes:
            if b.w is not None:
                k, v = b.w
                if deps.get(k, 0) < v:
                    deps[k] = v
            for k, v in b.r.items():
                if deps.get(k, 0) < v:
                    deps[k] = v
        return deps

    def _waits(self, e, deps):
        seen = self.seen[e]
        waits = []
        for k, v in deps.items():
            if e == "pe" and k[0] == "pe":
                continue
            if seen.get(k, 0) >= v:
                continue
            waits.append((k, v))
            seen[k] = v
        return waits

    def _mark(self, me, reads, writes):
        k, v = me
        for b in writes:
            b.w = me
            b.r = {}
        for b in reads:
            if b not in writes:
                if b.r.get(k, 0) < v:
                    b.r[k] = v

    def op(self, e, fn, reads=(), writes=(), sig=True):
        if self.dry:
            return None
        waits = self._waits(e, self._deps(reads, writes))
        eng = self.eng[e]
        for (k, v) in waits[:-1]:
            eng.wait_ge(self.sems[k], v)
        ins = fn(eng)
        if waits:
            k, v = waits[-1]
            ins._wait_ge(self.sems[k], v)
        key = (e, self.epoch[e])
        if sig:
            self.cnt[e] += 1
            ins.then_inc(self.sems[key], 1)
            me = (key, self.cnt[e])
            if self.cnt[e] >= self.EPOCH:
                self.epoch[e] += 1
                self._new_epoch_sem(e)
        else:
            me = (key, self.cnt[e] + 1)
        self._mark(me, reads, writes)
        self.n_ops[e] += 1
        return ins

    def dma(self, q, fn, reads=(), writes=(), track=None):
        if self.dry:
            return None
        waits = self._waits(q, self._deps(reads, writes))
        eng = self.eng[q]
        for (k, v) in waits[:-1]:
            eng.wait_ge(self.sems[k], v)
        ins = fn(eng)
        if waits:
            k, v = waits[-1]
            ins._wait_ge(self.sems[k], v)
        tb = track if track is not None else (writes[0] if writes else reads[0])
        if tb.dsem is None:
            tb.dsem = ("dma", id(tb))
            self.sems[tb.dsem] = self._new_sem()
        tb.dcnt += 16
        ins.then_inc(self.sems[tb.dsem], 16)
        self._mark((tb.dsem, tb.dcnt), reads, writes)
        return ins

    def wait_all(self, e, bufs):
        waits = self._waits(e, self._deps(bufs, bufs))
        for (k, v) in waits:
            self.eng[e].wait_ge(self.sems[k], v)


class Rot:
    def __init__(self, nc, S, name, shape, dt, n):
        self.t = [nc.alloc_sbuf_tensor("rot_%s%d" % (name, i), shape, dt) for i in range(n)]
        self.b = [S.buf("%s%d" % (name, i)) for i in range(n)]
        self.i = 0

    def next(self):
        i = self.i
        self.i = (i + 1) % len(self.t)
        return self.t[i], self.b[i]


SPLIT = dict(a_val=0, a_glu=512, a_z=1024, q=1536, k=2560, v=3584, b_z=4608, beta=5632, alpha=5640,
             c_u=5648, c_v=6160, c_z=6672, gate=7184)
SLOT_TILES = 32


def group_defs():
    gs = []

    def win_chunks(name, col0, n=4):
        tiles = []
        for j in range(n):
            for kc in range(KC):
                tiles.append(("w_in", kc, col0 + 128 * j))
        gs.append((name, tiles))

    for g2 in range(2):
        tiles = []
        for j in (2 * g2, 2 * g2 + 1):
            for kc in range(KC):
                tiles.append(("w_in", kc, SPLIT["a_glu"] + 128 * j))
            for kc in range(KC):
                tiles.append(("w_in", kc, SPLIT["a_val"] + 128 * j))
        gs.append(("ag%d" % g2, tiles))
        for j in (2 * g2, 2 * g2 + 1):
            gs.append(("ca%d" % j, [("diag", j, k) for k in range(CK)]))
    win_chunks("az", SPLIT["a_z"])
    win_chunks("cu", SPLIT["c_u"])
    tiles = []
    for kc in range(KC):
        for j in range(4):
            tiles.append(("w_in", kc, SPLIT["c_v"] + 128 * j))
    gs.append(("cv", tiles))
    win_chunks("cz", SPLIT["c_z"])
    for nm in ("q", "k", "v"):
        win_chunks(nm + "0", SPLIT[nm])
        win_chunks(nm + "1", SPLIT[nm] + 512)
    win_chunks("bz0", SPLIT["b_z"])
    win_chunks("bz1", SPLIT["b_z"] + 512)
    for m in range(8):
        tiles = []
        for br in range(3):
            for kc in range(KC):
                tiles.append(("w_in", kc, SPLIT["gate"] + br * 1024 + m * 128))
        gs.append(("mg%d" % m, tiles))
        tiles = []
        for kc in range(4):
            tiles.append(("a_proj", kc, m * 128))
        for kc in range(8):
            tiles.append(("b_proj", kc, m * 128))
        for kc in range(4):
            tiles.append(("c_proj", kc, m * 128))
        gs.append(("pj%d" % m, tiles))
    for g in range(2):
        tiles = []
        for mm_ in range(4):
            for kc in range(KC):
                tiles.append(("w_out", kc, (4 * g + mm_) * 128))
        gs.append(("wo%d" % g, tiles))
    return gs


GROUPS = group_defs()
GOFF = {}
_o = 0
for _n, _t in GROUPS:
    if not _n.startswith("ca"):
        GOFF[_n] = (_o, len(_t))
        _o += len(_t) * 128
TOT = _o
for _n, _t in GROUPS:
    if _n.startswith("ca"):
        GOFF[_n] = (_o, len(_t))
        _o += len(_t) * 128
TOT2 = _o

PP_NG, PP_ADW, PP_ADWB, PP_ALNG, PP_ALNB, PP_BCONV, PP_ONG = 0, 8, 8 + 124, 136, 140, 144, 240
NPP = 241
PR_ALOG, PR_DTB, PR_CLNG, PR_CLNB, PR_CBS = 0, 8, 16, 528, 1040
NPR = 1552


def host_layout(inputs, L):
    wbig = np.empty((L, P, TOT), np.float32)
    for l in range(L):
        mats = {"w_in": inputs["w_in"][l], "a_proj": inputs["a_proj"][l], "b_proj": inputs["b_proj"][l],
                "c_proj": inputs["c_proj"][l], "w_out": inputs["w_out"][l]}
        for name, tiles in GROUPS:
            if name.startswith("ca"):
                continue
            off = GOFF[name][0]
            for ti, (mat, kc, c0) in enumerate(tiles):
                wbig[l, :, off + ti * 128: off + (ti + 1) * 128] = mats[mat][kc * 128:(kc + 1) * 128, c0:c0 + 128]
    wba = np.empty((L, P, KC, 16), np.float32)
    for l in range(L):
        w = inputs["w_in"][l][:, SPLIT["beta"]:SPLIT["beta"] + 16]
        wba[l] = w.reshape(KC, P, 16).transpose(1, 0, 2)
    pp = np.zeros((L, P, NPP), np.float32)
    pr = np.zeros((L, 1, NPR), np.float32)
    wsT = np.empty((L, P, 4, P), np.float32)
    for l in range(L):
        pp[l, :, PP_NG:PP_NG + 8] = inputs["norm_g"][l].reshape(KC, P).T
        pp[l, :, PP_ADW:PP_ADW + 124] = inputs["a_dw"][l].reshape(CK, 4, P).transpose(2, 1, 0).reshape(P, 124)
        pp[l, :, PP_ADWB:PP_ADWB + 4] = inputs["a_dw_b"][l].reshape(4, P).T
        pp[l, :, PP_ALNG:PP_ALNG + 4] = inputs["a_ln_g"][l].reshape(4, P).T
        pp[l, :, PP_ALNB:PP_ALNB + 4] = inputs["a_ln_b"][l].reshape(4, P).T
        pp[l, :, PP_BCONV:PP_BCONV + 96] = inputs["b_conv"][l].reshape(4, 24, P).transpose(2, 1, 0).reshape(P, 96)
        pp[l, :, PP_ONG] = inputs["b_onorm_g"][l]
        pr[l, 0, PR_ALOG:PR_ALOG + 8] = inputs["b_a_log"][l]
        pr[l, 0, PR_DTB:PR_DTB + 8] = inputs["b_dt_bias"][l]
        pr[l, 0, PR_CLNG:PR_CLNG + 512] = inputs["c_ln_g"][l]
        pr[l, 0, PR_CLNB:PR_CLNB + 512] = inputs["c_ln_b"][l]
        pr[l, 0, PR_CBS:PR_CBS + 512] = inputs["c_bs"][l].reshape(512)
        wsT[l] = inputs["c_ws"][l].transpose(2, 0, 1)
    fg = np.ascontiguousarray(inputs["final_g"].reshape(KC, P).T)
    return dict(wbig=wbig, wba=wba, pp=pp, pr=pr, wsT=wsT, fg=fg)


def build(nseq, seqlen, L, T=256, nslot=3):
    NB = T // P
    ntile = seqlen // T
    NTOK = nseq * seqlen
    nc = bass.Bass("TRN2", target_bir_lowering=False)
    S = Sched(nc)

    x_d = nc.dram_tensor("x", [NTOK, D], F32, kind="ExternalInput").ap()
    wbig_d = nc.dram_tensor("wbig", [L, P, TOT], F32, kind="ExternalInput").ap()
    wba_d = nc.dram_tensor("wba", [L, P, KC, 16], F32, kind="ExternalInput").ap()
    pp_d = nc.dram_tensor("pp", [L, P, NPP], F32, kind="ExternalInput").ap()
    pr_d = nc.dram_tensor("pr", [L, 1, NPR], F32, kind="ExternalInput").ap()
    wsT_d = nc.dram_tensor("wsT", [L, P, 4, P], F32, kind="ExternalInput").ap()
    fg_d = nc.dram_tensor("fg", [P, KC], F32, kind="ExternalInput").ap()
    out_d = nc.dram_tensor("out", [NTOK, D], F32, kind="ExternalOutput").ap()
    wbf_d = nc.dram_tensor("wbf", [L, P, TOT2], BF16, kind="Internal").ap()

    def sb(name, shape, dt=F32):
        return nc.alloc_sbuf_tensor("sb_" + name, shape, dt)

    ident_bf = sb("ident_bf", [P, P], BF16)
    ident4_bf = sb("ident4_bf", [P, 4, P], BF16)
    ident_f = sb("ident_f", [P, P])
    ones_f = sb("ones_f", [P, P])
    onesD_bf = sb("onesD_bf", [P, P], BF16)
    ones512_f = sb("ones512_f", [P, P])
    onescol_bf = sb("onescol_bf", [P, 1], BF16)
    Lincl = sb("Lincl", [P, P])
    Ustr = sb("Ustr", [P, P])
    maskS = sb("maskS", [P, 4, P], BF16)
    maskIT = sb("maskIT", [P, 4, P], BF16)
    bd4 = sb("bd4", [P, 4, P], BF16)
    nbd4 = sb("nbd4", [P, 4, P], BF16)
    E4 = sb("E4", [4, P])
    CB = S.buf("consts")

    def pool(fn, reads=(), writes=()):
        return S.op("pool", fn, reads, writes)

    def dve(fn, reads=(), writes=()):
        return S.op("dve", fn, reads, writes)

    def act(fn, reads=(), writes=()):
        return S.op("act", fn, reads, writes)

    def pe(fn, reads=(), writes=(), sig=True):
        return S.op("pe", fn, reads, writes, sig=sig)

    def asel(t, pattern, cm, op, fill, base=0):
        pool(lambda e: e.affine_select(out=t, in_=t, pattern=pattern, compare_op=op, fill=fill, base=base,
                                       channel_multiplier=cm), [CB], [CB])

    pool(lambda e: e.memset(ident_f[:], 0.0), (), [CB])
    asel(ident_f[:], [[-1, P]], 1, ALU.not_equal, 1.0)
    pool(lambda e: e.tensor_copy(out=ident_bf[:], in_=ident_f[:]), [CB], [CB])
    for hh in range(4):
        pool(lambda e, hh=hh: e.tensor_copy(out=ident4_bf[:, hh, :], in_=ident_f[:]), [CB], [CB])
    pool(lambda e: e.memset(ones_f[:], 1.0), (), [CB])
    pool(lambda e: e.memset(onesD_bf[:], 1.0 / D), (), [CB])
    pool(lambda e: e.memset(ones512_f[:], 1.0 / 512), (), [CB])
    pool(lambda e: e.memset(onescol_bf[:], 1.0), (), [CB])
    pool(lambda e: e.memset(Lincl[:], 1.0), (), [CB])
    asel(Lincl[:], [[1, P]], -1, ALU.is_ge, 0.0)
    pool(lambda e: e.memset(Ustr[:], 1.0), (), [CB])
    asel(Ustr[:], [[-1, P]], 1, ALU.is_gt, 0.0)
    pool(lambda e: e.memset(maskS[:], 0.0), (), [CB])
    asel(maskS[:], [[0, 4], [-1, P]], 1, ALU.is_gt, -10000.0)
    pool(lambda e: e.memset(maskIT[:], 0.0), (), [CB])
    asel(maskIT[:], [[0, 4], [1, P]], -1, ALU.is_ge, -10000.0)
    pool(lambda e: e.memset(E4[:], 1.0), (), [CB])
    asel(E4[:], [[1, P]], -32, ALU.is_ge, 0.0)
    asel(E4[:], [[-1, P]], 32, ALU.is_ge, 0.0, base=31)

    NPS = 7
    psum_t = [nc.alloc_psum_tensor("ps%d" % i, [P, 512], F32) for i in range(NPS)]
    psum_b = [S.buf("ps%d" % i) for i in range(NPS)]
    NPSB = 1
    psbf_t = [nc.alloc_psum_tensor("psbf%d" % i, [P, 4, P], BF16) for i in range(NPSB)]
    psbf_b = [S.buf("psbf%d" % i) for i in range(NPSB)]
    pst = {"i": 0, "j": 0}

    ps_held = set()

    def PS(hold=False):
        i = pst["i"]
        while i in ps_held:
            i = (i + 1) % NPS
        pst["i"] = (i + 1) % NPS
        if hold:
            ps_held.add(i)
        return psum_t[i], psum_b[i]

    def PS_release(pb):
        ps_held.discard(psum_b.index(pb))

    def PSB():
        i = pst["j"]
        pst["j"] = (i + 1) % NPSB
        return psbf_t[i], psbf_b[i]

    pt, pb = PS()
    pe(lambda e: e.matmul(pt[:, 0:P], lhsT=E4[:], rhs=E4[:], start=True, stop=True), [CB], [pb])
    for hh in range(4):
        dve(lambda e, hh=hh: e.tensor_copy(out=bd4[:, hh, :], in_=pt[:, 0:P]), [pb], [CB])
    dve(lambda e: e.tensor_scalar(out=nbd4[:], in0=bd4[:], scalar1=-1.0, scalar2=1.0, op0=ALU.mult, op1=ALU.add), [CB], [CB])

    pp = [sb("pp%d" % l, [P, NPP]) for l in range(L)]
    fg = sb("fg", [P, KC])
    negA = [sb("negA%d" % l, [P, 8]) for l in range(L)]
    dtb = [sb("dtb%d" % l, [P, 8]) for l in range(L)]
    clng = [sb("clng%d" % l, [P, 512]) for l in range(L)]
    clnb = [sb("clnb%d" % l, [P, 512]) for l in range(L)]
    bsrow = [sb("bsrow%d" % l, [1, 512]) for l in range(L)]
    onesrow_f = sb("onesrow_f", [1, P])
    wsT = [sb("wsT%d" % l, [P, 4, P], BF16) for l in range(L)]
    wba = [sb("wba%d" % l, [P, KC, 16], BF16) for l in range(L)]
    vrot = Rot(nc, S, "vrot", [P, 512], F32, 2)
    ldtmp = vrot.t[0][:, :].rearrange("p (g i) -> p g i", g=4)
    ldtmp2 = vrot.t[1][:, 0:KC * 16].rearrange("p (k c) -> p k c", k=KC)
    PB_ = S.buf("params")
    LT = vrot.b[0]
    LT2 = vrot.b[1]
    S.dma("sp", lambda e: e.dma_start(out=fg[:], in_=fg_d), (), [PB_])
    pool(lambda e: e.memset(onesrow_f[:], 1.0), (), [PB_])
    for l in range(L):
        S.dma("sp", lambda e, l=l: e.dma_start(out=pp[l][:], in_=pp_d[l]), (), [PB_])
        S.dma("sp", lambda e, l=l: e.dma_start(out=negA[l][:], in_=pr_d[l, :, PR_ALOG:PR_ALOG + 8].partition_broadcast(P)), (), [PB_])
        S.dma("sp", lambda e, l=l: e.dma_start(out=dtb[l][:], in_=pr_d[l, :, PR_DTB:PR_DTB + 8].partition_broadcast(P)), (), [PB_])
        S.dma("sp", lambda e, l=l: e.dma_start(out=clng[l][:], in_=pr_d[l, :, PR_CLNG:PR_CLNG + 512].partition_broadcast(P)), (), [PB_])
        S.dma("sp", lambda e, l=l: e.dma_start(out=clnb[l][:], in_=pr_d[l, :, PR_CLNB:PR_CLNB + 512].partition_broadcast(P)), (), [PB_])
        S.dma("sp", lambda e, l=l: e.dma_start(out=bsrow[l][:], in_=pr_d[l, :, PR_CBS:PR_CBS + 512]), (), [PB_])
        act(lambda e, l=l: e.activation(out=negA[l][:], in_=negA[l][:], func=AF.Exp), [PB_], [PB_])
        dve(lambda e, l=l: e.tensor_scalar(out=negA[l][:], in0=negA[l][:], scalar1=-1.0, scalar2=None, op0=ALU.mult), [PB_], [PB_])
        S.dma("sp", lambda e, l=l: e.dma_start(out=ldtmp, in_=wsT_d[l]), (), [LT])
        dve(lambda e, l=l: e.tensor_tensor(out=wsT[l][:], in0=ldtmp, in1=Lincl[:].unsqueeze(1).to_broadcast([P, 4, P]), op=ALU.mult),
            [LT, CB], [PB_])
        S.dma("sp", lambda e, l=l: e.dma_start(out=ldtmp2, in_=wba_d[l]), (), [LT2])
        dve(lambda e, l=l: e.tensor_copy(out=wba[l][:], in_=ldtmp2), [LT2], [PB_])

    ring = [sb("ring%d" % i, [P, SLOT_TILES * 128], BF16) for i in range(nslot)]
    ring_b = [S.buf("ring%d" % i) for i in range(nslot)]
    wbfL = [S.buf("wbf%d" % l) for l in range(L)]
    wdgL = [S.buf("wdg%d" % l) for l in range(L)]
    wbfB = [[(wdgL[l] if n.startswith("ca") else wbfL[l]) for (n, _) in GROUPS] for l in range(L)]
    for l in range(L):
        for gi, (n, tiles) in enumerate(GROUPS):
            off, nt = GOFF[n]
            if n.startswith("ca"):
                j = int(n[2:])
                dgt, dgb = ring[0][:, 0:CK * P].rearrange("p (k c) -> p k c", k=CK), ring_b[0]
                for k in range(CK):
                    dve(lambda e, l=l, j=j, k=k, dgt=dgt: e.tensor_scalar(out=dgt[:, k, :], in0=ident_f[:], scalar1=pp[l][:, PP_ADW + j * CK + k:PP_ADW + j * CK + k + 1],
                                                                          scalar2=None, op0=ALU.mult), [CB, PB_], [dgb])
                S.dma("sp", lambda e, l=l, off=off, nt=nt, dgt=dgt: e.dma_start(out=wbf_d[l, :, off:off + nt * 128], in_=ring[0][:, 0:CK * P]),
                      [dgb], [wbfB[l][gi]], track=wbfB[l][gi])
            else:
                S.dma("pool", lambda e, l=l, off=off, nt=nt: e.dma_start(out=wbf_d[l, :, off:off + nt * 128],
                                                                          in_=wbig_d[l, :, off:off + nt * 128]),
                      (), [wbfB[l][gi]])

    xT = sb("xT", [P, KC, T])
    xT_b = [S.buf("xT%d" % k) for k in range(KC)]
    hT = sb("hT", [P, KC, T], BF16)
    hT_b = [S.buf("hT%d" % k) for k in range(KC)]
    iobuf = Rot(nc, S, "io", [P, D], F32, 2)
    stA = [sb("stA%d" % l, [P, 4, CK - 1], BF16) for l in range(L)]
    stA_b = [S.buf("stA%d" % l) for l in range(L)]
    stB = [sb("stB%d" % l, [P, 24, 3]) for l in range(L)]
    stB_b = [S.buf("stB%d" % l) for l in range(L)]
    Sf = [sb("Sf%d" % l, [P, H, P]) for l in range(L)]
    Sbf = [sb("Sbf%d" % l, [P, H, P], BF16) for l in range(L)]
    S_b = [[S.buf("S%d_%d" % (l, hf)) for hf in range(2)] for l in range(L)]
    Sbf_b = [[S.buf("Sbf%d_%d" % (l, hf)) for hf in range(2)] for l in range(L)]

    f32rot = Rot(nc, S, "f32r", [P, T], F32, 4)
    sq16rot = Rot(nc, S, "sq16r", [P, T], BF16, 4)
    workA = Rot(nc, S, "workA", [P, T + CK - 1], BF16, 2)
    aT = sb("aT", [P, 4, T])
    aT_b = [S.buf("aT%d" % j) for j in range(4)]
    azT = sb("azT", [P, 4, T], BF16)
    azT_b = [S.buf("azT%d" % j) for j in range(4)]
    meanA = sb("meanA", [P, T])
    rstdA = sb("rstdA", [P, T])
    mrA_b = S.buf("mrA")
    yaT = sb("yaT", [P, 4, T], BF16)
    yaT_b = [S.buf("yaT%d" % j) for j in range(4)]
    bfrot = Rot(nc, S, "bfr", [P, T], BF16, 2)
    cuT = sb("cuT", [P, 4, T], BF16)
    cuT_b = [S.buf("cuT%d" % j) for j in range(4)]
    uczT = sb("uczT", [P, 4, T], BF16)
    uczT_b = [S.buf("uczT%d" % j) for j in range(4)]
    vs = [sb("vs%d" % b, [P, 512], BF16) for b in range(NB)]
    vs_b = [S.buf("vs%d" % b) for b in range(NB)]
    small = Rot(nc, S, "small", [P, 16], F32, 44)
    smallc = Rot(nc, S, "smallc", [P, 16], F32, 4)
    ycT = sb("ycT", [P, 4, T], BF16)
    ycT_b = [S.buf("ycT%d" % j) for j in range(4)]
    workB = Rot(nc, S, "workB", [P, T + 3], F32, 4)
    qkvT = sb("qkvT", [P, 24, T], BF16)
    qkvT_b = [S.buf("qkvT%d" % c) for c in range(24)]
    bzT = sb("bzT", [P, H, T], BF16)
    bzT_b = [S.buf("bzT%d" % h) for h in range(H)]
    ybT = sb("ybT", [P, H, T], BF16)
    ybT_b = [S.buf("ybT%d" % h) for h in range(H)]
    sqrot = Rot(nc, S, "sqT", [P, 16, P], BF16, 1)
    dn_f = [{n: Rot(nc, S, "dn%d_" % hf + n, [P, 4, P], F32, 1) for n in ("gU", "gL")} for hf in range(2)]
    dn_h = [{n: Rot(nc, S, "dn%d_" % hf + n, [P, 4, P], BF16, 1) for n in
             ("Gs", "GiT", "egR", "Mf", "Mbd", "MbdT", "Moff", "ktok", "Vb", "PmT", "qTs", "R", "Y0", "U", "on")} for hf in range(2)]
    dn_x = [Rot(nc, S, "dn%d_X" % hf, [P, 4, P], BF16, 2) for hf in range(2)]
    dn_xt = [Rot(nc, S, "dn%d_XT" % hf, [P, 4, P], BF16, 2) for hf in range(2)]
    dn_p = [Rot(nc, S, "dn%d_PT" % hf, [P, 4, P], BF16, 2) for hf in range(2)]
    dn_y = [Rot(nc, S, "dn%d_Y" % hf, [P, 4, P], BF16, 2) for hf in range(2)]
    mergedT = sb("mergedT", [P, KC, T], BF16)
    mergedT_b = [S.buf("mergedT%d" % k) for k in range(KC)]

    ring_state = {"next_load": 0, "order": [], "slot_of": {}}

    def ring_plan(order):
        ring_state["order"] = order
        ring_state["next_load"] = 0
        ring_state["slot_of"] = {}

    def ring_issue(upto):
        o = ring_state["order"]
        while ring_state["next_load"] < min(upto, len(o)):
            i = ring_state["next_load"]
            l, gi = o[i]
            n = GROUPS[gi][0]
            off, nt = GOFF[n]
            s = i % nslot
            S.dma("sp", lambda e, l=l, off=off, nt=nt, s=s: e.dma_start(out=ring[s][:, 0:nt * 128], in_=wbf_d[l, :, off:off + nt * 128]),
                  [wbfB[l][gi]], [ring_b[s]])
            ring_state["slot_of"][i] = s
            ring_state["next_load"] += 1

    ring_pos = {"i": 0}

    def ring_get(l, name):
        if S.dry:
            ring_state.setdefault("dry_list", []).append((l, name))
            return ring[0], ring_b[0]
        i = ring_pos["i"]
        o = ring_state["order"]
        assert GROUPS[o[i][1]][0] == name and o[i][0] == l, (o[i], l, name)
        ring_issue(i + nslot)
        ring_pos["i"] = i + 1
        s = ring_state["slot_of"][i]
        return ring[s], ring_b[s]

    def wtile(slot, ti):
        return slot[:, ti * 128:(ti + 1) * 128]

    evac_rr = {"i": 0}

    def evac(out, in_, reads, writes, eng=None):
        if eng is None:
            eng = "act" if evac_rr["i"] % 2 == 0 else "dve"
            evac_rr["i"] += 1
        if eng == "act":
            act(lambda e: e.copy(out=out, in_=in_), reads, writes)
        else:
            dve(lambda e: e.tensor_copy(out=out, in_=in_), reads, writes)

    def rstd_from(out, in_, reads, writes, n_tmp=None):
        act(lambda e: e.activation(out=out, in_=in_, func=AF.Ln, bias=EPS), reads, writes)
        act(lambda e: e.activation(out=out, in_=out, func=AF.Exp, scale=-0.5), writes, writes)

    def proj_chunk(slot, sbuf_, j, ntile8=KC):
        pt, pb = PS()
        for kc in range(KC):
            pe(lambda e, kc=kc: e.matmul(pt[:, 0:T], lhsT=wtile(slot, j * KC + kc), rhs=hT[:, kc, :], start=(kc == 0), stop=(kc == KC - 1)),
               [sbuf_, hT_b[kc]], [pb], sig=(kc == KC - 1))
        return pt, pb

    def rms_to(dst, dst_bufs, gcol, dst_is_h):
        pt, pb = PS()
        for kc in range(KC):
            sq, sqb = sq16rot.next()
            act(lambda e, kc=kc, sq=sq: e.activation(out=sq[:], in_=xT[:, kc, :], func=AF.Square), [xT_b[kc]], [sqb])
            pe(lambda e, kc=kc, sq=sq: e.matmul(pt[:, 0:T], lhsT=onesD_bf[:], rhs=sq[:], start=(kc == 0), stop=(kc == KC - 1)),
               [sqb, CB], [pb])
        rs, rsb = f32rot.next()
        rstd_from(rs[:], pt[:, 0:T], [pb], [rsb])
        for kc in range(KC):
            dve(lambda e, kc=kc: e.scalar_tensor_tensor(out=dst[:, kc, :], in0=xT[:, kc, :], scalar=gcol(kc), in1=rs[:], op0=ALU.mult, op1=ALU.mult),
                [xT_b[kc], rsb, PB_], [dst_bufs[kc] if isinstance(dst_bufs, list) else dst_bufs])

    def tile_layer(l):
        ppl = pp[l]
        rms_to(hT, hT_b, lambda kc: ppl[:, PP_NG + kc:PP_NG + kc + 1], True)

        pending = []

        def flush_silu():
            for (c, wk, wkb, acc, accb, wcol) in pending:
                act(lambda e, c=c, acc=acc: e.activation(out=qkvT[:, c, :], in_=acc[:], func=AF.Silu), [accb], [qkvT_b[c]])
            del pending[:]

        for gi, nm in enumerate(("q0", "q1", "k0", "k1", "v0", "v1")):
            slot, sbuf_ = ring_get(l, nm)
            for jp in range(2):
                items = []
                for jj in (2 * jp, 2 * jp + 1):
                    c = gi * 4 + jj
                    pt, pb = proj_chunk(slot, sbuf_, jj)
                    wk, wkb = workB.next()
                    wcol = lambda k, c=c: ppl[:, PP_BCONV + c * 4 + k:PP_BCONV + c * 4 + k + 1]
                    pool(lambda e, c=c, wk=wk: e.tensor_copy(out=wk[:, 0:3], in_=stB[l][:, c, :]), [stB_b[l]], [wkb])
                    act(lambda e, wk=wk, pt=pt: e.copy(out=wk[:, 3:3 + T], in_=pt[:, 0:T]), [pb], [wkb])
                    acc, accb = f32rot.next()
                    act(lambda e, acc=acc, pt=pt, wcol=wcol: e.activation(out=acc[:], in_=pt[:, 0:T], func=AF.Copy, scale=wcol(3)), [pb, PB_], [accb])
                    pool(lambda e, c=c, wk=wk: e.tensor_copy(out=stB[l][:, c, :], in_=wk[:, T:T + 3]), [wkb], [stB_b[l]])
                    items.append((c, wk, wkb, acc, accb, wcol))
                flush_silu()
                for k in range(3):
                    for (c, wk, wkb, acc, accb, wcol) in items:
                        dve(lambda e, wk=wk, acc=acc, k=k, wcol=wcol: e.scalar_tensor_tensor(out=acc[:], in0=wk[:, k:k + T], scalar=wcol(k), in1=acc[:],
                                                                                             op0=ALU.mult, op1=ALU.add), [wkb, accb, PB_], [accb])
                pending.extend(items)
        flush_silu()
        for gi, nm in enumerate(("bz0", "bz1")):
            slot, sbuf_ = ring_get(l, nm)
            for jj in range(4):
                h = gi * 4 + jj
                pt, pb = proj_chunk(slot, sbuf_, jj)
                act(lambda e, h=h, pt=pt: e.activation(out=bzT[:, h, :], in_=pt[:, 0:T], func=AF.Silu), [pb], [bzT_b[h]])

        gate_box = []

        def merge_gates(m):
            slot, sbuf_ = ring_get(l, "mg%d" % m)
            acc, accb = f32rot.next()
            gts = []
            for br in range(3):
                pt, pb = proj_chunk(slot, sbuf_, br)
                gt, gtb = f32rot.next()
                act(lambda e, pt=pt, gt=gt: e.activation(out=gt[:], in_=pt[:, 0:T], func=AF.Sigmoid), [pb], [gtb])
                gts.append((gt, gtb))
            return acc, accb, gts

        def ac_gen():
            for g2 in range(2):
                slot, sbuf_ = ring_get(l, "ag%d" % g2)
                wks = []
                for jj in range(2):
                    j = 2 * g2 + jj
                    pt, pb = proj_chunk(slot, sbuf_, 2 * jj)
                    sg, sgb = f32rot.next()
                    act(lambda e, pt=pt, sg=sg: e.activation(out=sg[:], in_=pt[:, 0:T], func=AF.Sigmoid), [pb], [sgb])
                    pt, pb = proj_chunk(slot, sbuf_, 2 * jj + 1)
                    wk, wkb = workA.next()
                    pool(lambda e, j=j, wk=wk: e.tensor_copy(out=wk[:, 0:CK - 1], in_=stA[l][:, j, :]), [stA_b[l]], [wkb])
                    dve(lambda e, wk=wk, pt=pt, sg=sg: e.tensor_tensor(out=wk[:, CK - 1:CK - 1 + T], in0=pt[:, 0:T], in1=sg[:], op=ALU.mult), [pb, sgb], [wkb])
                    pool(lambda e, j=j, wk=wk: e.tensor_copy(out=stA[l][:, j, :], in_=wk[:, T:T + CK - 1]), [wkb], [stA_b[l]])
                    wks.append((j, wk, wkb))
                    yield
                for (j, wk, wkb) in wks:
                    slot, sbuf_ = ring_get(l, "ca%d" % j)
                    pt, pb = PS()
                    for k in range(CK):
                        pe(lambda e, k=k, pt=pt, wk=wk, slot=slot: e.matmul(pt[:, 0:T], lhsT=wtile(slot, k), rhs=wk[:, k:k + T], start=(k == 0), stop=(k == CK - 1)),
                           [sbuf_, wkb], [pb], sig=(k == CK - 1))
                    act(lambda e, j=j, pt=pt: e.activation(out=aT[:, j, :], in_=pt[:, 0:T], func=AF.Identity, bias=ppl[:, PP_ADWB + j:PP_ADWB + j + 1]), [pb, PB_], [aT_b[j]])
                    yield
            slot, sbuf_ = ring_get(l, "az")
            for j in range(4):
                pt, pb = proj_chunk(slot, sbuf_, j)
                act(lambda e, j=j, pt=pt: e.activation(out=azT[:, j, :], in_=pt[:, 0:T], func=AF.Silu), [pb], [azT_b[j]])
                yield
            slot, sbuf_ = ring_get(l, "cu")
            for j in range(4):
                pt, pb = proj_chunk(slot, sbuf_, j)
                act(lambda e, j=j, pt=pt: e.activation(out=cuT[:, j, :], in_=pt[:, 0:T], func=AF.Gelu_apprx_tanh), [pb], [cuT_b[j]])
                yield
            slot, sbuf_ = ring_get(l, "cv")
            for b in range(NB):
                pt, pb = PS()
                for kc in range(KC):
                    pe(lambda e, kc=kc, b=b, pt=pt: e.matmul(pt[:, 0:512], lhsT=hT[:, kc, b * P:(b + 1) * P], rhs=slot[:, kc * 512:(kc + 1) * 512],
                                                             start=(kc == 0), stop=(kc == KC - 1)), [sbuf_, hT_b[kc]], [pb], sig=(kc == KC - 1))
                vg, vgb = vrot.next()
                act(lambda e, pt=pt, vg=vg: e.activation(out=vg[:], in_=pt[:, 0:512], func=AF.Gelu_apprx_tanh), [pb], [vgb])
                st, stb = smallc.next()
                mv, mvb = smallc.next()
                dve(lambda e, vg=vg, st=st: e.bn_stats(out=st[:, 0:6], in_=vg[:]), [vgb], [stb])
                dve(lambda e, st=st, mv=mv: e.bn_aggr(out=mv[:, 0:2], in_=st[:, 0:6]), [stb], [mvb])
                rstd_from(mv[:, 2:3], mv[:, 1:2], [mvb], [mvb])
                dve(lambda e, vg=vg, mv=mv: e.tensor_scalar(out=vg[:], in0=vg[:], scalar1=mv[:, 0:1], scalar2=mv[:, 2:3], op0=ALU.subtract, op1=ALU.mult),
                    [vgb, mvb], [vgb])
                dve(lambda e, vg=vg: e.tensor_tensor(out=vg[:], in0=vg[:], in1=clng[l][:], op=ALU.mult), [vgb, PB_], [vgb])
                dve(lambda e, vg=vg, b=b: e.tensor_tensor(out=vs[b][:], in0=vg[:], in1=clnb[l][:], op=ALU.add), [vgb, PB_], [vs_b[b]])
                yield
            slot, sbuf_ = ring_get(l, "cz")
            for j in range(4):
                pt, pb = proj_chunk(slot, sbuf_, j)
                t2, t2b = bfrot.next()
                act(lambda e, pt=pt, t2=t2: e.activation(out=t2[:], in_=pt[:, 0:T], func=AF.Silu), [pb], [t2b])
                dve(lambda e, j=j, t2=t2: e.tensor_tensor(out=uczT[:, j, :], in0=t2[:], in1=cuT[:, j, :], op=ALU.mult), [t2b, cuT_b[j]], [uczT_b[j]])
                yield
            pm, pmb = PS()
            p2, p2b = PS()
            for j in range(4):
                sq, sqb = f32rot.next()
                act(lambda e, j=j, sq=sq: e.activation(out=sq[:], in_=aT[:, j, :], func=AF.Square), [aT_b[j]], [sqb])
                pe(lambda e, j=j: e.matmul(pm[:, 0:T], lhsT=ones512_f[:], rhs=aT[:, j, :], start=(j == 0), stop=(j == 3)), [aT_b[j], CB], [pmb])
                pe(lambda e, j=j, sq=sq: e.matmul(p2[:, 0:T], lhsT=ones512_f[:], rhs=sq[:], start=(j == 0), stop=(j == 3)), [sqb, CB], [p2b])
            act(lambda e: e.copy(out=meanA[:], in_=pm[:, 0:T]), [pmb], [mrA_b])
            msq, msqb = f32rot.next()
            dve(lambda e: e.tensor_tensor(out=msq[:], in0=meanA[:], in1=meanA[:], op=ALU.mult), [mrA_b], [msqb])
            dve(lambda e: e.tensor_tensor(out=rstdA[:], in0=p2[:, 0:T], in1=msq[:], op=ALU.subtract), [p2b, msqb], [mrA_b])
            rstd_from(rstdA[:], rstdA[:], [mrA_b], [mrA_b])
            yield
            for j in range(4):
                t1, t1b = f32rot.next()
                dve(lambda e, j=j, t1=t1: e.tensor_tensor(out=t1[:], in0=aT[:, j, :], in1=meanA[:], op=ALU.subtract), [aT_b[j], mrA_b], [t1b])
                dve(lambda e, t1=t1: e.tensor_tensor(out=t1[:], in0=t1[:], in1=rstdA[:], op=ALU.mult), [t1b, mrA_b], [t1b])
                t2, t2b = bfrot.next()
                act(lambda e, j=j, t1=t1, t2=t2: e.activation(out=t2[:], in_=t1[:], func=AF.Silu, scale=ppl[:, PP_ALNG + j:PP_ALNG + j + 1],
                                                              bias=ppl[:, PP_ALNB + j:PP_ALNB + j + 1]), [t1b, PB_], [t2b])
                dve(lambda e, j=j, t2=t2: e.tensor_tensor(out=yaT[:, j, :], in0=t2[:], in1=azT[:, j, :], op=ALU.mult), [t2b, azT_b[j]], [yaT_b[j]])
                yield

            for g in range(4):
                pt, pb = PS()
                for b in range(NB):
                    pe(lambda e, g=g, b=b, pt=pt: e.matmul(pt[:, b * P:(b + 1) * P], lhsT=vs[b][:, g * P:(g + 1) * P], rhs=wsT[l][:, g, :], start=True, stop=False),
                       [vs_b[b], PB_], [pb], sig=False)
                    pe(lambda e, g=g, b=b, pt=pt: e.matmul(pt[:, b * P:(b + 1) * P], lhsT=onesrow_f[:], rhs=bsrow[l][:, g * P:(g + 1) * P], start=False, stop=True),
                       [PB_], [pb], sig=(b == NB - 1))
                dve(lambda e, g=g, pt=pt: e.tensor_tensor(out=ycT[:, g, :], in0=pt[:, 0:T], in1=uczT[:, g, :], op=ALU.mult), [pb, uczT_b[g]], [ycT_b[g]])
                yield


            gate_box.append(merge_gates(0))
            yield

        def dn_gen():
            def dn_prep(b):
                t0 = b * P
                tsl = slice(t0, t0 + P)
                sqT, sqT_b = sqrot.next()
                pba, pbab = PS()
                for kc in range(KC):
                    pe(lambda e, kc=kc: e.matmul(pba[:, 0:16], lhsT=hT[:, kc, tsl], rhs=wba[l][:, kc, :], start=(kc == 0), stop=(kc == KC - 1)),
                       [hT_b[kc], PB_], [pbab], sig=(kc == KC - 1))
                act(lambda e: e.activation(out=sqT[:, :, :], in_=qkvT[:, 0:16, tsl], func=AF.Square), [qkvT_b[c16] for c16 in range(16)], [sqT_b])
                for c16 in range(16):
                    pe(lambda e, c16=c16: e.matmul(pba[:, 16 + c16:17 + c16], lhsT=sqT[:, c16, :], rhs=onescol_bf[:], start=True, stop=True),
                       [sqT_b, CB], [pbab], sig=(c16 == 15))

                def sm():
                    t, bb = small.next()
                    return t, bb

                ba, bab = sm()
                act(lambda e: e.copy(out=ba[:, 0:16], in_=pba[:, 0:16]), [pbab], [bab])
                z, zb = sm()
                dve(lambda e: e.tensor_tensor(out=z[:, 0:8], in0=ba[:, 8:16], in1=dtb[l][:], op=ALU.add), [bab, PB_], [zb])
                act(lambda e: e.activation(out=z[:, 0:8], in_=z[:, 0:8], func=AF.Exp), [zb], [zb])
                act(lambda e: e.activation(out=z[:, 0:8], in_=z[:, 0:8], func=AF.Ln, bias=1.0), [zb], [zb])
                g_, gb = sm()
                dve(lambda e: e.tensor_tensor(out=g_[:, 0:8], in0=z[:, 0:8], in1=negA[l][:], op=ALU.mult), [zb, PB_], [gb])
                beta, betab = sm()
                act(lambda e: e.activation(out=beta[:, 0:8], in_=ba[:, 0:8], func=AF.Exp, scale=-1.0), [bab], [betab])
                dve(lambda e: e.tensor_scalar(out=beta[:, 0:8], in0=beta[:, 0:8], scalar1=1.0, scalar2=None, op0=ALU.add), [betab], [betab])
                dve(lambda e: e.reciprocal(out=beta[:, 0:8], in_=beta[:, 0:8]), [betab], [betab])
                rinv, rinvb = sm()
                rstd_from(rinv[:, 0:16], pba[:, 16:32], [pbab], [rinvb])
                a2, a2b = sm()
                dve(lambda e: e.tensor_tensor(out=a2[:, 0:8], in0=rinv[:, 8:16], in1=rinv[:, 8:16], op=ALU.mult), [rinvb], [a2b])
                dve(lambda e: e.scalar_tensor_tensor(out=a2[:, 0:8], in0=beta[:, 0:8], scalar=-1.0, in1=a2[:, 0:8], op0=ALU.mult, op1=ALU.mult), [betab, a2b], [a2b])
                cV, cVb = sm()
                dve(lambda e: e.tensor_tensor(out=cV[:, 0:8], in0=beta[:, 0:8], in1=rinv[:, 8:16], op=ALU.mult), [betab, rinvb], [cVb])
                osc, oscb = sm()
                dve(lambda e: e.tensor_scalar(out=osc[:, 0:8], in0=rinv[:, 0:8], scalar1=float(P) ** -0.5, scalar2=None, op0=ALU.mult), [rinvb], [oscb])
                pg, pgb = PS()
                pe(lambda e: e.matmul(pg[:, 0:8], lhsT=Lincl[:], rhs=g_[:, 0:8], start=True, stop=True), [gb, CB], [pgb], sig=False)
                pe(lambda e: e.matmul(pg[:, 8:16], lhsT=ones_f[:], rhs=g_[:, 0:8], start=True, stop=True), [gb, CB], [pgb])
                gc, gcb = sm()
                act(lambda e: e.copy(out=gc[:, 0:16], in_=pg[:, 0:16]), [pgb], [gcb])
                egc, egcb = sm()
                act(lambda e: e.activation(out=egc[:, 0:8], in_=pg[:, 0:8], func=AF.Exp), [pgb], [egcb])
                edl, edlb = sm()
                act(lambda e: e.activation(out=edl[:, 0:8], in_=pg[:, 8:16], func=AF.Exp), [pgb], [edlb])
                edec, edecb = sm()
                dve(lambda e: e.tensor_tensor(out=edec[:, 0:8], in0=gc[:, 8:16], in1=gc[:, 0:8], op=ALU.subtract), [gcb], [edecb])
                act(lambda e: e.activation(out=edec[:, 0:8], in_=edec[:, 0:8], func=AF.Exp), [edecb], [edecb])
                cKS, cKSb = sm()
                dve(lambda e: e.tensor_tensor(out=cKS[:, 0:8], in0=a2[:, 0:8], in1=egc[:, 0:8], op=ALU.mult), [a2b, egcb], [cKSb])

                return dict(locals())

            sc_next = dn_prep(0)
            yield
            for b in range(NB):
                sc = sc_next
                if b + 1 < NB:
                    sc_next = dn_prep(b + 1)
                (tsl, g_, gb, a2, a2b, cV, cVb, osc, oscb, edl, edlb, edec, edecb, cKS, cKSb) = [sc[k] for k in (
                    "tsl", "g_", "gb", "a2", "a2b", "cV", "cVb", "osc", "oscb", "edl", "edlb", "edec", "edecb", "cKS", "cKSb")]

                def sm():
                    t, bb = small.next()
                    return t, bb

                def dn_unit(hf):
                    h0 = 4 * hf
                    gU, gUb = dn_f[hf]["gU"].next()
                    gL, gLb = dn_f[hf]["gL"].next()
                    dve(lambda e, h0=h0, gU=gU: e.tensor_tensor(out=gU[:], in0=Ustr[:].unsqueeze(1).to_broadcast([P, 4, P]),
                                                                in1=g_[:, h0:h0 + 4].unsqueeze(2).to_broadcast([P, 4, P]), op=ALU.mult), [gb, CB], [gUb])
                    pool(lambda e, h0=h0, gL=gL: e.tensor_tensor(out=gL[:], in0=Lincl[:].unsqueeze(1).to_broadcast([P, 4, P]),
                                                                 in1=g_[:, h0:h0 + 4].unsqueeze(2).to_broadcast([P, 4, P]), op=ALU.mult), [gb, CB], [gLb])
                    flat = lambda t: t[:].rearrange("p h j -> p (h j)")
                    Gs, Gsb = dn_h[hf]["Gs"].next()
                    GiT, GiTb = dn_h[hf]["GiT"].next()
                    egR, egRb = dn_h[hf]["egR"].next()
                    pt, pb = PS()
                    pe(lambda e, pt=pt, gU=gU: e.matmul(pt[:, :], lhsT=Lincl[:], rhs=flat(gU), start=True, stop=False), [gUb, CB], [pb], sig=False)
                    pe(lambda e, pt=pt: e.matmul(pt[:, :], lhsT=ident_bf[:], rhs=flat(maskS), start=False, stop=True), [CB], [pb])
                    act(lambda e, pt=pt, Gs=Gs: e.activation(out=flat(Gs), in_=pt[:, :], func=AF.Exp), [pb], [Gsb])
                    pt, pb = PS()
                    pe(lambda e, pt=pt, gL=gL: e.matmul(pt[:, :], lhsT=Ustr[:], rhs=flat(gL), start=True, stop=False), [gLb, CB], [pb], sig=False)
                    pe(lambda e, pt=pt: e.matmul(pt[:, :], lhsT=ident_bf[:], rhs=flat(maskIT), start=False, stop=True), [CB], [pb])
                    act(lambda e, pt=pt, GiT=GiT: e.activation(out=flat(GiT), in_=pt[:, :], func=AF.Exp), [pb], [GiTb])
                    pt, pb = PS()
                    pe(lambda e, pt=pt, gL=gL: e.matmul(pt[:, :], lhsT=ones_f[:], rhs=flat(gL), start=True, stop=True), [gLb, CB], [pb])
                    act(lambda e, pt=pt, egR=egR: e.activation(out=flat(egR), in_=pt[:, :], func=AF.Exp), [pb], [egRb])
                    yield
                    qTs, qTsb = dn_h[hf]["qTs"].next()
                    dve(lambda e, qTs=qTs, egR=egR, h0=h0: e.tensor_tensor(out=qTs[:], in0=qkvT[:, h0:h0 + 4, tsl], in1=egR[:], op=ALU.mult),
                        [qkvT_b[h0 + i] for i in range(4)] + [egRb], [qTsb])
                    ktok, ktokb = dn_h[hf]["ktok"].next()
                    Vb, Vbb = dn_h[hf]["Vb"].next()
                    ptb, pbb = PSB()
                    for hh in range(4):
                        pe(lambda e, hh=hh, ptb=ptb: e.transpose(ptb[:, hh, :], qkvT[:, 8 + h0 + hh, tsl], ident_bf[:]), [qkvT_b[8 + h0 + hh], CB], [pbb], sig=(hh == 3))
                    dve(lambda e, ptb=ptb, ktok=ktok: e.tensor_tensor(out=ktok[:], in0=ptb[:], in1=edec[:, h0:h0 + 4].unsqueeze(2).to_broadcast([P, 4, P]), op=ALU.mult),
                        [pbb, edecb], [ktokb])
                    yield
                    ptb, pbb = PSB()
                    for hh in range(4):
                        pe(lambda e, hh=hh, ptb=ptb: e.transpose(ptb[:, hh, :], qkvT[:, 16 + h0 + hh, tsl], ident_bf[:]), [qkvT_b[16 + h0 + hh], CB], [pbb], sig=(hh == 3))
                    dve(lambda e, ptb=ptb, Vb=Vb: e.tensor_tensor(out=Vb[:], in0=ptb[:], in1=cV[:, h0:h0 + 4].unsqueeze(2).to_broadcast([P, 4, P]), op=ALU.mult),
                        [pbb, cVb], [Vbb])
                    yield
                    Mf, Mfb = dn_h[hf]["Mf"].next()
                    PmT, PmTb = dn_h[hf]["PmT"].next()
                    pt, pb = PS()
                    for hh in range(4):
                        kT = qkvT[:, 8 + h0 + hh, tsl]
                        pe(lambda e, hh=hh, pt=pt, kT=kT: e.matmul(pt[:, hh * P:(hh + 1) * P], lhsT=kT, rhs=kT, start=True, stop=True), [qkvT_b[8 + h0 + hh]], [pb], sig=(hh == 3))
                    for hh in range(4):
                        dve(lambda e, hh=hh, pt=pt, Mf=Mf, Gs=Gs: e.scalar_tensor_tensor(out=Mf[:, hh, :], in0=pt[:, hh * P:(hh + 1) * P], scalar=a2[:, h0 + hh:h0 + hh + 1],
                                                                                         in1=Gs[:, hh, :], op0=ALU.mult, op1=ALU.mult), [pb, a2b, Gsb], [Mfb])
                    pt, pb = PS()
                    for hh in range(4):
                        pe(lambda e, hh=hh, pt=pt: e.matmul(pt[:, hh * P:(hh + 1) * P], lhsT=qkvT[:, 8 + h0 + hh, tsl], rhs=qkvT[:, h0 + hh, tsl], start=True, stop=True),
                           [qkvT_b[8 + h0 + hh], qkvT_b[h0 + hh]], [pb], sig=(hh == 3))
                    dve(lambda e, pt=pt, PmT=PmT, GiT=GiT: e.tensor_tensor(out=flat(PmT), in0=pt[:, :], in1=flat(GiT), op=ALU.mult), [pb, GiTb], [PmTb])
                    yield
                    Mbd, Mbdb = dn_h[hf]["Mbd"].next()
                    MbdT, MbdTb = dn_h[hf]["MbdT"].next()
                    pool(lambda e, Mbd=Mbd, Mf=Mf: e.tensor_tensor(out=Mbd[:], in0=Mf[:], in1=bd4[:], op=ALU.mult), [Mfb, CB], [Mbdb])
                    ptb, pbb = PSB()
                    for hh in range(4):
                        pe(lambda e, hh=hh, ptb=ptb, Mf=Mf: e.transpose(ptb[:, hh, :], Mf[:, hh, :], ident_bf[:]), [Mfb, CB], [pbb], sig=(hh == 3))
                    dve(lambda e, ptb=ptb, MbdT=MbdT: e.tensor_tensor(out=MbdT[:], in0=ptb[:], in1=bd4[:], op=ALU.mult), [pbb, CB], [MbdTb])
                    yield
                    Moff, Moffb = dn_h[hf]["Moff"].next()
                    pool(lambda e, Moff=Moff, Mf=Mf: e.tensor_tensor(out=Moff[:], in0=Mf[:], in1=nbd4[:], op=ALU.mult), [Mfb, CB], [Moffb])

                    def grp(lh, lhb, rh, rhb, extra=None, exb=None):
                        pt_, pb_ = PS()
                        for hh in range(4):
                            if extra is None:
                                pe(lambda e, hh=hh: e.matmul(pt_[:, hh * P:(hh + 1) * P], lhsT=lh[:, hh, :], rhs=rh[:, hh, :], start=True, stop=True),
                                   [lhb, rhb], [pb_], sig=(hh == 3))
                            else:
                                pe(lambda e, hh=hh: e.matmul(pt_[:, hh * P:(hh + 1) * P], lhsT=lh[:, hh, :], rhs=rh[:, hh, :], start=True, stop=False),
                                   [lhb, rhb], [pb_], sig=False)
                                pe(lambda e, hh=hh: e.matmul(pt_[:, hh * P:(hh + 1) * P], lhsT=ident_bf[:], rhs=extra[:, hh, :], start=False, stop=True),
                                   [exb, CB], [pb_], sig=(hh == 3))
                        return pt_, pb_

                    PT, PTb = dn_p[hf].next()
                    pool(lambda e, PT=PT, MbdT=MbdT: e.tensor_tensor(out=PT[:], in0=MbdT[:], in1=ident4_bf[:], op=ALU.add), [MbdTb, CB], [PTb])
                    X, Xb, XT, XTb = Mbd, Mbdb, MbdT, MbdTb
                    X2, X2b = dn_x[hf].next()
                    X2T, X2Tb = dn_xt[hf].next()
                    p1, p1b = grp(XT, XTb, X, Xb)
                    p2, p2b = grp(X, Xb, XT, XTb)
                    evac(flat(X2), p1[:, :], [p1b], [X2b], eng="act")
                    evac(flat(X2T), p2[:, :], [p2b], [X2Tb], eng="dve")
                    yield
                    X, Xb, XT, XTb = X2, X2b, X2T, X2Tb
                    for k in range(3):
                        PT2, PT2b = dn_p[hf].next()
                        p3, p3b = grp(X, Xb, PT, PTb, extra=PT, exb=PTb)
                        Xn, Xnb = dn_x[hf].next()
                        p1, p1b = grp(XT, XTb, X, Xb)
                        if k < 2:
                            XnT, XnTb = dn_xt[hf].next()
                            p2, p2b = grp(X, Xb, XT, XTb)
                        evac(flat(PT2), p3[:, :], [p3b], [PT2b], eng="act")
                        evac(flat(Xn), p1[:, :], [p1b], [Xnb], eng="dve")
                        if k < 2:
                            evac(flat(XnT), p2[:, :], [p2b], [XnTb], eng="act")
                        else:
                            XnT, XnTb = None, None
                        yield
                        PT, PTb, X, Xb, XT, XTb = PT2, PT2b, Xn, Xnb, XnT, XnTb
                    PT2, PT2b = dn_p[hf].next()
                    p3, p3b = grp(X, Xb, PT, PTb, extra=PT, exb=PTb)
                    evac(flat(PT2), p3[:, :], [p3b], [PT2b])
                    yield
                    TbdT, TbdTb = PT2, PT2b
                    Nm, Nmb = dn_x[hf].next()
                    NT, NTb = dn_xt[hf].next()
                    NT1, NT1b = dn_x[hf].next()
                    p1, p1b = grp(TbdT, TbdTb, Moff, Moffb)
                    p2, p2b = grp(Moff, Moffb, TbdT, TbdTb)
                    evac(flat(Nm), p1[:, :], [p1b], [Nmb], eng="act")
                    evac(flat(NT), p2[:, :], [p2b], [NTb], eng="act")
                    pool(lambda e, NT=NT, NT1=NT1: e.tensor_tensor(out=NT1[:], in0=NT[:], in1=ident4_bf[:], op=ALU.add), [NTb, CB], [NT1b])
                    yield
                    N2T1, N2T1b = dn_xt[hf].next()
                    p1, p1b = grp(Nm, Nmb, NT, NTb, extra=ident4_bf, exb=CB)
                    evac(flat(N2T1), p1[:, :], [p1b], [N2T1b])
                    yield
                    R, Rb = dn_h[hf]["R"].next()
                    pt, pb = PS()
                    for hh in range(4):
                        pe(lambda e, hh=hh, pt=pt: e.matmul(pt[:, hh * P:(hh + 1) * P], lhsT=qkvT[:, 8 + h0 + hh, tsl], rhs=Sbf[l][:, h0 + hh, :], start=True, stop=True),
                           [qkvT_b[8 + h0 + hh], Sbf_b[l][hf]], [pb], sig=(hh == 3))
                    for hh in range(4):
                        dve(lambda e, hh=hh, pt=pt, R=R, Vb=Vb: e.scalar_tensor_tensor(out=R[:, hh, :], in0=pt[:, hh * P:(hh + 1) * P], scalar=cKS[:, h0 + hh:h0 + hh + 1],
                                                                                       in1=Vb[:, hh, :], op0=ALU.mult, op1=ALU.add), [pb, cKSb, Vbb], [Rb])
                    yield
                    Y0, Y0b = dn_h[hf]["Y0"].next()
                    p1, p1b = grp(TbdT, TbdTb, R, Rb)
                    evac(flat(Y0), p1[:, :], [p1b], [Y0b])
                    yield
                    Y1, Y1b = dn_h[hf]["U"].next()
                    p1, p1b = grp(N2T1, N2T1b, Y0, Y0b)
                    evac(flat(Y1), p1[:, :], [p1b], [Y1b])
                    yield
                    W, Wb = dn_y[hf].next()
                    p1, p1b = grp(NT1, NT1b, Y1, Y1b)
                    evac(flat(W), p1[:, :], [p1b], [Wb])
                    yield
                    po, pob = PS(hold=True)
                    for hh in range(4):
                        pe(lambda e, hh=hh, qTs=qTs: e.matmul(po[:, hh * P:(hh + 1) * P], lhsT=qTs[:, hh, :], rhs=Sbf[l][:, h0 + hh, :], start=True, stop=False),
                           [qTsb, Sbf_b[l][hf]], [pob], sig=False)
                        pe(lambda e, hh=hh, PmT=PmT, W=W: e.matmul(po[:, hh * P:(hh + 1) * P], lhsT=PmT[:, hh, :], rhs=W[:, hh, :], start=False, stop=True),
                           [PmTb, Wb], [pob], sig=(hh == 3))
                    pt, pb = PS()
                    for hh in range(4):
                        pe(lambda e, hh=hh, pt=pt, ktok=ktok, W=W: e.matmul(pt[:, hh * P:(hh + 1) * P], lhsT=ktok[:, hh, :], rhs=W[:, hh, :], start=True, stop=True),
                           [ktokb, Wb], [pb], sig=(hh == 3))
                    for hh in range(4):
                        dve(lambda e, hh=hh, pt=pt: e.scalar_tensor_tensor(out=Sf[l][:, h0 + hh, :], in0=Sf[l][:, h0 + hh, :], scalar=edl[:, h0 + hh:h0 + hh + 1],
                                                                           in1=pt[:, hh * P:(hh + 1) * P], op0=ALU.mult, op1=ALU.add), [pb, edlb, S_b[l][hf]], [S_b[l][hf]])
                    act(lambda e: e.copy(out=Sbf[l][:, h0:h0 + 4, :], in_=Sf[l][:, h0:h0 + 4, :]), [S_b[l][hf]], [Sbf_b[l][hf]])
                    ss, ssb = sm()
                    junk, junkb = vrot.next()
                    act(lambda e, junk=junk: e.activation(out=junk[:, :], in_=po[:, :], func=AF.Square), [pob], [junkb])
                    dve(lambda e, junk=junk: e.reduce_sum(out=ss[:, 0:4], in_=junk[:, :].rearrange("p (h e) -> p h e", h=4), axis=mybir.AxisListType.X), [junkb], [ssb])
                    dve(lambda e: e.tensor_tensor(out=ss[:, 4:8], in0=osc[:, h0:h0 + 4], in1=osc[:, h0:h0 + 4], op=ALU.mult), [oscb, ssb], [ssb])
                    dve(lambda e: e.scalar_tensor_tensor(out=ss[:, 0:4], in0=ss[:, 0:4], scalar=1.0 / P, in1=ss[:, 4:8], op0=ALU.mult, op1=ALU.mult), [ssb], [ssb])
                    rstd_from(ss[:, 0:4], ss[:, 0:4], [ssb], [ssb])
                    dve(lambda e: e.tensor_tensor(out=ss[:, 8:12], in0=ss[:, 0:4], in1=osc[:, h0:h0 + 4], op=ALU.mult), [ssb, oscb], [ssb])
                    on, onb = dn_h[hf]["on"].next()
                    dve(lambda e, on=on: e.tensor_tensor(out=on[:], in0=po[:, :].rearrange("p (h e) -> p h e", h=4), in1=ss[:, 8:12].unsqueeze(2).to_broadcast([P, 4, P]), op=ALU.mult),
                        [pob, ssb], [onb])
                    PS_release(pob)
                    yield
                    ptb, pbb = PSB()
                    for hh in range(4):
                        pe(lambda e, hh=hh, ptb=ptb, on=on: e.transpose(ptb[:, hh, :], on[:, hh, :], ident_bf[:]), [onb, CB], [pbb], sig=(hh == 3))
                    dve(lambda e, ptb=ptb: e.scalar_tensor_tensor(out=ybT[:, h0:h0 + 4, tsl], in0=ptb[:], scalar=ppl[:, PP_ONG:PP_ONG + 1], in1=bzT[:, h0:h0 + 4, tsl],
                                                                  op0=ALU.mult, op1=ALU.mult), [pbb, PB_] + [bzT_b[h0 + i] for i in range(4)], [ybT_b[h0 + i] for i in range(4)])

                gens = [dn_unit(0), dn_unit(1)]
                while gens:
                    nxt = []
                    for gen in gens:
                        try:
                            next(gen)
                            nxt.append(gen)
                        except StopIteration:
                            pass
                    gens = nxt
                    yield


        alive = [dn_gen(), ac_gen()]
        while alive:
            nxt = []
            for gen_ in alive:
                try:
                    next(gen_)
                    nxt.append(gen_)
                except StopIteration:
                    pass
            alive = nxt

        br_src = [(yaT, yaT_b, 4, 0), (ybT, ybT_b, 8, 4), (ycT, ycT_b, 4, 12)]
        for m in range(8):
            if m == 0:
                acc, accb, gts = gate_box.pop()
            else:
                acc, accb, gts = merge_gates(m)
            slot, sbuf_ = ring_get(l, "pj%d" % m)
            for br in range(3):
                gt, gtb = gts[br]
                src, srcb, nk, base = br_src[br]
                pp_, ppb = PS()
                for kc in range(nk):
                    pe(lambda e, kc=kc, pp_=pp_, src=src, base=base, slot=slot: e.matmul(pp_[:, 0:T], lhsT=wtile(slot, base + kc), rhs=src[:, kc, :], start=(kc == 0), stop=(kc == nk - 1)),
                       [sbuf_, srcb[kc]], [ppb], sig=(kc == nk - 1))
                if br == 0:
                    dve(lambda e, pp_=pp_, gt=gt, acc=acc: e.tensor_tensor(out=acc[:], in0=pp_[:, 0:T], in1=gt[:], op=ALU.mult), [ppb, gtb], [accb])
                else:
                    dve(lambda e, pp_=pp_, gt=gt: e.tensor_tensor(out=gt[:], in0=pp_[:, 0:T], in1=gt[:], op=ALU.mult), [ppb, gtb], [gtb])
                    if br == 1:
                        pool(lambda e, gt=gt, acc=acc: e.tensor_tensor(out=acc[:], in0=acc[:], in1=gt[:], op=ALU.add), [accb, gtb], [accb])
                    else:
                        pool(lambda e, gt=gt, acc=acc, m=m: e.tensor_tensor(out=mergedT[:, m, :], in0=acc[:], in1=gt[:], op=ALU.add), [accb, gtb], [mergedT_b[m]])
        for g in range(2):
            slot, sbuf_ = ring_get(l, "wo%d" % g)
            for mm_ in range(4):
                m = 4 * g + mm_
                pt, pb = PS()
                for kc in range(KC):
                    pe(lambda e, kc=kc, pt=pt, mm_=mm_: e.matmul(pt[:, 0:T], lhsT=wtile(slot, mm_ * KC + kc), rhs=mergedT[:, kc, :], start=(kc == 0), stop=(kc == KC - 1)),
                       [sbuf_, mergedT_b[kc]], [pb], sig=(kc == KC - 1))
                dve(lambda e, m=m, pt=pt: e.tensor_tensor(out=xT[:, m, :], in0=xT[:, m, :], in1=pt[:, 0:T], op=ALU.add), [pb, xT_b[m]], [xT_b[m]])

    S.dry = True
    for l in range(L):
        tile_layer(l)
    S.dry = False
    name2gi = {n: i for i, (n, _) in enumerate(GROUPS)}
    order = [(l, name2gi[n]) for (l, n) in ring_state["dry_list"]] * (nseq * ntile)
    ring_plan(order)

    out_bufs = []
    for s in range(nseq):
        for l in range(L):
            pool(lambda e, l=l: e.memset(stA[l][:], 0.0), (), [stA_b[l]])
            pool(lambda e, l=l: e.memset(stB[l][:], 0.0), (), [stB_b[l]])
            for hf in range(2):
                pool(lambda e, l=l, hf=hf: e.memset(Sf[l][:, 4 * hf:4 * hf + 4, :], 0.0), (), [S_b[l][hf]])
                pool(lambda e, l=l, hf=hf: e.memset(Sbf[l][:, 4 * hf:4 * hf + 4, :], 0.0), (), [Sbf_b[l][hf]])
        for ti in range(ntile):
            tok0 = s * seqlen + ti * T
            for b in range(NB):
                xin, xinb = iobuf.next()
                S.dma("sp", lambda e, xin=xin, b=b: e.dma_start(out=xin[:], in_=x_d[tok0 + b * P: tok0 + (b + 1) * P, :]), (), [xinb])
                for half in range(2):
                    pt, pb = PS()
                    for kk in range(4):
                        kc = half * 4 + kk
                        pe(lambda e, kk=kk, kc=kc, pt=pt, xin=xin: e.matmul(pt[:, kk * P:(kk + 1) * P], lhsT=xin[:, kc * P:(kc + 1) * P], rhs=ident_f[:], start=True, stop=True),
                           [xinb, CB], [pb], sig=(kk == 3))
                    evac(xT[:, half * 4:half * 4 + 4, b * P:(b + 1) * P], pt[:, :].rearrange("p (k t) -> p k t", k=4), [pb], [xT_b[half * 4 + i] for i in range(4)])
            for l in range(L):
                tile_layer(l)
            rms_to(xT, xT_b, lambda kc: fg[:, kc:kc + 1], False)
            for b in range(NB):
                ot, otb = iobuf.next()
                for half in range(2):
                    pt, pb = PS()
                    for kk in range(4):
                        kc = half * 4 + kk
                        pe(lambda e, kk=kk, kc=kc, pt=pt, b=b: e.matmul(pt[:, kk * P:(kk + 1) * P], lhsT=xT[:, kc, b * P:(b + 1) * P], rhs=ident_f[:], start=True, stop=True),
                           [xT_b[kc], CB], [pb], sig=(kk == 3))
                    evac(ot[:, half * 512:(half + 1) * 512], pt[:, :], [pb], [otb])
                S.dma("sp", lambda e, ot=ot, b=b: e.dma_start(out=out_d[tok0 + b * P: tok0 + (b + 1) * P, :], in_=ot[:]), [otb], ())
                out_bufs.append(otb)
    S.wait_all("sp", iobuf.b)
    build.last_sched = S
    return nc


_CACHE = {}


def kernel(**inputs):
    x = np.asarray(inputs["x"], np.float32)
    B, SEQ, _ = x.shape
    L = inputs["w_in"].shape[0]
    nseq = B // NCORES
    lay = host_layout({k: np.asarray(v, np.float32) for k, v in inputs.items()}, L)
    key = (nseq, SEQ, L)
    if key not in _CACHE:
        _CACHE[key] = build(nseq, SEQ, L)
    nc = _CACHE[key]
    in_maps = []
    for c in range(NCORES):
        m = dict(lay)
        m["x"] = np.ascontiguousarray(x[c * nseq:(c + 1) * nseq].reshape(nseq * SEQ, D))
        in_maps.append(m)
    res = run_bass_kernel_spmd(nc, in_maps, core_ids=list(range(NCORES)))
    out = np.stack([np.asarray(r["out"]).reshape(nseq, SEQ, D) for r in res.results], axis=0)
    return out.reshape(B, SEQ, D).astype(np.float32)
```

```python
import numpy as np
import concourse.bass as bass
import concourse.mybir as mybir
from concourse.bass_utils import run_bass_kernel_spmd

F32 = mybir.dt.float32
BF16 = mybir.dt.bfloat16
AF = mybir.ActivationFunctionType
ALU = mybir.AluOpType

P = 128
D = 1024
KC = 8
H = 8
NIN = 10256
EPS = 1e-6
CK = 31
NCORES = 8


class Buf:
    __slots__ = ("name", "w", "r", "dsem", "dcnt")

    def __init__(self, name):
        self.name = name
        self.w = None
        self.r = {}
        self.dsem = None
        self.dcnt = 0


class Sched:
    EPOCH = 8000

    def __init__(self, nc):
        self.nc = nc
        self.eng = {"pe": nc.tensor, "act": nc.scalar, "dve": nc.vector, "pool": nc.gpsimd, "sp": nc.sync}
        self.sems = {}
        self.cnt = {}
        self.epoch = {}
        self.seen = {e: {} for e in self.eng}
        self.nsem = 0
        for e in self.eng:
            self.epoch[e] = 0
            self._new_epoch_sem(e)
        self.n_ops = {e: 0 for e in self.eng}
        self.dry = False

    def _new_sem(self):
        self.nsem += 1
        return self.nc.alloc_semaphore("s%d" % self.nsem)

    def _new_epoch_sem(self, e):
        key = (e, self.epoch[e])
        self.sems[key] = self._new_sem()
        self.cnt[e] = 0

    def buf(self, name="b"):
        return Buf(name)

    def _deps(self, reads, writes):
        deps = {}
        for b in reads:
            if b.w is not None:
                k, v = b.w
                if deps.get(k, 0) < v:
                    deps[k] = v
        for b in writes:
            if b.w is not None:
                k, v = b.w
                if deps.get(k, 0) < v:
                    deps[k] = v
            for k, v in b.r.items():
                if deps.get(k, 0) < v:
                    deps[k] = v
        return deps

    def _waits(self, e, deps):
        seen = self.seen[e]
        waits = []
        for k, v in deps.items():
            if e == "pe" and k[0] == "pe":
                continue
            if seen.get(k, 0) >= v:
                continue
            waits.append((k, v))
            seen[k] = v
        return waits

    def _mark(self, me, reads, writes):
        k, v = me
        for b in writes:
            b.w = me
            b.r = {}
        for b in reads:
            if b not in writes:
                if b.r.get(k, 0) < v:
                    b.r[k] = v

    def op(self, e, fn, reads=(), writes=(), sig=True):
        if self.dry:
            return None
        waits = self._waits(e, self._deps(reads, writes))
        eng = self.eng[e]
        for (k, v) in waits[:-1]:
            eng.wait_ge(self.sems[k], v)
        ins = fn(eng)
        if waits:
            k, v = waits[-1]
            ins._wait_ge(self.sems[k], v)
        key = (e, self.epoch[e])
        if sig:
            self.cnt[e] += 1
            ins.then_inc(self.sems[key], 1)
            me = (key, self.cnt[e])
            if self.cnt[e] >= self.EPOCH:
                self.epoch[e] += 1
                self._new_epoch_sem(e)
        else:
            me = (key, self.cnt[e] + 1)
        self._mark(me, reads, writes)
        self.n_ops[e] += 1
        return ins

    def dma(self, q, fn, reads=(), writes=(), track=None):
        if self.dry:
            return None
        waits = self._waits(q, self._deps(reads, writes))
        eng = self.eng[q]
        for (k, v) in waits[:-1]:
            eng.wait_ge(self.sems[k], v)
        ins = fn(eng)
        if waits:
            k, v = waits[-1]
            ins._wait_ge(self.sems[k], v)
        tb = track if track is not None else (writes[0] if writes else reads[0])
        if tb.dsem is None:
            tb.dsem = ("dma", id(tb))
            self.sems[tb.dsem] = self._new_sem()
        tb.dcnt += 16
        ins.then_inc(self.sems[tb.dsem], 16)
        self._mark((tb.dsem, tb.dcnt), reads, writes)
        return ins

    def wait_all(self, e, bufs):
        waits = self._waits(e, self._deps(bufs, bufs))
        for (k, v) in waits:
            self.eng[e].wait_ge(self.sems[k], v)


class Rot:
    def __init__(self, nc, S, name, shape, dt, n):
        self.t = [nc.alloc_sbuf_tensor("rot_%s%d" % (name, i), shape, dt) for i in range(n)]
        self.b = [S.buf("%s%d" % (name, i)) for i in range(n)]
        self.i = 0

    def next(self):
        i = self.i
        self.i = (i + 1) % len(self.t)
        return self.t[i], self.b[i]


SPLIT = dict(a_val=0, a_glu=512, a_z=1024, q=1536, k=2560, v=3584, b_z=4608, beta=5632, alpha=5640,
             c_u=5648, c_v=6160, c_z=6672, gate=7184)
SLOT_TILES = 32


def group_defs():
    gs = []

    def win_chunks(name, col0, n=4):
        tiles = []
        for j in range(n):
            for kc in range(KC):
                tiles.append(("w_in", kc, col0 + 128 * j))
        gs.append((name, tiles))

    for g2 in range(2):
        tiles = []
        for j in (2 * g2, 2 * g2 + 1):
            for kc in range(KC):
                tiles.append(("w_in", kc, SPLIT["a_glu"] + 128 * j))
            for kc in range(KC):
                tiles.append(("w_in", kc, SPLIT["a_val"] + 128 * j))
        gs.append(("ag%d" % g2, tiles))
        for j in (2 * g2, 2 * g2 + 1):
            gs.append(("ca%d" % j, [("diag", j, k) for k in range(CK)]))
    win_chunks("az", SPLIT["a_z"])
    win_chunks("cu", SPLIT["c_u"])
    tiles = []
    for kc in range(KC):
        for j in range(4):
            tiles.append(("w_in", kc, SPLIT["c_v"] + 128 * j))
    gs.append(("cv", tiles))
    win_chunks("cz", SPLIT["c_z"])
    for nm in ("q", "k", "v"):
        win_chunks(nm + "0", SPLIT[nm])
        win_chunks(nm + "1", SPLIT[nm] + 512)
    win_chunks("bz0", SPLIT["b_z"])
    win_chunks("bz1", SPLIT["b_z"] + 512)
    for m in range(8):
        tiles = []
        for br in range(3):
            for kc in range(KC):
                tiles.append(("w_in", kc, SPLIT["gate"] + br * 1024 + m * 128))
        gs.append(("mg%d" % m, tiles))
        tiles = []
        for kc in range(4):
            tiles.append(("a_proj", kc, m * 128))
        for kc in range(8):
            tiles.append(("b_proj", kc, m * 128))
        for kc in range(4):
            tiles.append(("c_proj", kc, m * 128))
        gs.append(("pj%d" % m, tiles))
    for g in range(2):
        tiles = []
        for mm_ in range(4):
            for kc in range(KC):
                tiles.append(("w_out", kc, (4 * g + mm_) * 128))
        gs.append(("wo%d" % g, tiles))
    return gs


GROUPS = group_defs()
GOFF = {}
_o = 0
for _n, _t in GROUPS:
    if not _n.startswith("ca"):
        GOFF[_n] = (_o, len(_t))
        _o += len(_t) * 128
TOT = _o
for _n, _t in GROUPS:
    if _n.startswith("ca"):
        GOFF[_n] = (_o, len(_t))
        _o += len(_t) * 128
TOT2 = _o

PP_NG, PP_ADW, PP_ADWB, PP_ALNG, PP_ALNB, PP_BCONV, PP_ONG = 0, 8, 8 + 124, 136, 140, 144, 240
NPP = 241
PR_ALOG, PR_DTB, PR_CLNG, PR_CLNB, PR_CBS = 0, 8, 16, 528, 1040
NPR = 1552


def host_layout(inputs, L):
    wbig = np.empty((L, P, TOT), np.float32)
    for l in range(L):
        mats = {"w_in": inputs["w_in"][l], "a_proj": inputs["a_proj"][l], "b_proj": inputs["b_proj"][l],
                "c_proj": inputs["c_proj"][l], "w_out": inputs["w_out"][l]}
        for name, tiles in GROUPS:
            if name.startswith("ca"):
                continue
            off = GOFF[name][0]
            for ti, (mat, kc, c0) in enumerate(tiles):
                wbig[l, :, off + ti * 128: off + (ti + 1) * 128] = mats[mat][kc * 128:(kc + 1) * 128, c0:c0 + 128]
    wba = np.empty((L, P, KC, 16), np.float32)
    for l in range(L):
        w = inputs["w_in"][l][:, SPLIT["beta"]:SPLIT["beta"] + 16]
        wba[l] = w.reshape(KC, P, 16).transpose(1, 0, 2)
    pp = np.zeros((L, P, NPP), np.float32)
    pr = np.zeros((L, 1, NPR), np.float32)
    wsT = np.empty((L, P, 4, P), np.float32)
    for l in range(L):
        pp[l, :, PP_NG:PP_NG + 8] = inputs["norm_g"][l].reshape(KC, P).T
        pp[l, :, PP_ADW:PP_ADW + 124] = inputs["a_dw"][l].reshape(CK, 4, P).transpose(2, 1, 0).reshape(P, 124)
        pp[l, :, PP_ADWB:PP_ADWB + 4] = inputs["a_dw_b"][l].reshape(4, P).T
        pp[l, :, PP_ALNG:PP_ALNG + 4] = inputs["a_ln_g"][l].reshape(4, P).T
        pp[l, :, PP_ALNB:PP_ALNB + 4] = inputs["a_ln_b"][l].reshape(4, P).T
        pp[l, :, PP_BCONV:PP_BCONV + 96] = inputs["b_conv"][l].reshape(4, 24, P).transpose(2, 1, 0).reshape(P, 96)
        pp[l, :, PP_ONG] = inputs["b_onorm_g"][l]
        pr[l, 0, PR_ALOG:PR_ALOG + 8] = inputs["b_a_log"][l]
        pr[l, 0, PR_DTB:PR_DTB + 8] = inputs["b_dt_bias"][l]
        pr[l, 0, PR_CLNG:PR_CLNG + 512] = inputs["c_ln_g"][l]
        pr[l, 0, PR_CLNB:PR_CLNB + 512] = inputs["c_ln_b"][l]
        pr[l, 0, PR_CBS:PR_CBS + 512] = inputs["c_bs"][l].reshape(512)
        wsT[l] = inputs["c_ws"][l].transpose(2, 0, 1)
    fg = np.ascontiguousarray(inputs["final_g"].reshape(KC, P).T)
    return dict(wbig=wbig, wba=wba, pp=pp, pr=pr, wsT=wsT, fg=fg)


def build(nseq, seqlen, L, T=256, nslot=3):
    NB = T // P
    ntile = seqlen // T
    NTOK = nseq * seqlen
    nc = bass.Bass("TRN2", target_bir_lowering=False)
    S = Sched(nc)

    x_d = nc.dram_tensor("x", [NTOK, D], F32, kind="ExternalInput").ap()
    wbig_d = nc.dram_tensor("wbig", [L, P, TOT], F32, kind="ExternalInput").ap()
    wba_d = nc.dram_tensor("wba", [L, P, KC, 16], F32, kind="ExternalInput").ap()
    pp_d = nc.dram_tensor("pp", [L, P, NPP], F32, kind="ExternalInput").ap()
    pr_d = nc.dram_tensor("pr", [L, 1, NPR], F32, kind="ExternalInput").ap()
    wsT_d = nc.dram_tensor("wsT", [L, P, 4, P], F32, kind="ExternalInput").ap()
    fg_d = nc.dram_tensor("fg", [P, KC], F32, kind="ExternalInput").ap()
    out_d = nc.dram_tensor("out", [NTOK, D], F32, kind="ExternalOutput").ap()
    wbf_d = nc.dram_tensor("wbf", [L, P, TOT2], BF16, kind="Internal").ap()

    def sb(name, shape, dt=F32):
        return nc.alloc_sbuf_tensor("sb_" + name, shape, dt)

    ident_bf = sb("ident_bf", [P, P], BF16)
    ident4_bf = sb("ident4_bf", [P, 4, P], BF16)
    ident_f = sb("ident_f", [P, P])
    ones_f = sb("ones_f", [P, P])
    onesD_bf = sb("onesD_bf", [P, P], BF16)
    ones512_f = sb("ones512_f", [P, P])
    onescol_bf = sb("onescol_bf", [P, 1], BF16)
    Lincl = sb("Lincl", [P, P])
    Ustr = sb("Ustr", [P, P])
    maskS = sb("maskS", [P, 4, P], BF16)
    maskIT = sb("maskIT", [P, 4, P], BF16)
    bd4 = sb("bd4", [P, 4, P], BF16)
    nbd4 = sb("nbd4", [P, 4, P], BF16)
    E4 = sb("E4", [4, P])
    CB = S.buf("consts")

    def pool(fn, reads=(), writes=()):
        return S.op("pool", fn, reads, writes)

    def dve(fn, reads=(), writes=()):
        return S.op("dve", fn, reads, writes)

    def act(fn, reads=(), writes=()):
        return S.op("act", fn, reads, writes)

    def pe(fn, reads=(), writes=(), sig=True):
        return S.op("pe", fn, reads, writes, sig=sig)

    def asel(t, pattern, cm, op, fill, base=0):
        pool(lambda e: e.affine_select(out=t, in_=t, pattern=pattern, compare_op=op, fill=fill, base=base,
                                       channel_multiplier=cm), [CB], [CB])

    pool(lambda e: e.memset(ident_f[:], 0.0), (), [CB])
    asel(ident_f[:], [[-1, P]], 1, ALU.not_equal, 1.0)
    pool(lambda e: e.tensor_copy(out=ident_bf[:], in_=ident_f[:]), [CB], [CB])
    for hh in range(4):
        pool(lambda e, hh=hh: e.tensor_copy(out=ident4_bf[:, hh, :], in_=ident_f[:]), [CB], [CB])
    pool(lambda e: e.memset(ones_f[:], 1.0), (), [CB])
    pool(lambda e: e.memset(onesD_bf[:], 1.0 / D), (), [CB])
    pool(lambda e: e.memset(ones512_f[:], 1.0 / 512), (), [CB])
    pool(lambda e: e.memset(onescol_bf[:], 1.0), (), [CB])
    pool(lambda e: e.memset(Lincl[:], 1.0), (), [CB])
    asel(Lincl[:], [[1, P]], -1, ALU.is_ge, 0.0)
    pool(lambda e: e.memset(Ustr[:], 1.0), (), [CB])
    asel(Ustr[:], [[-1, P]], 1, ALU.is_gt, 0.0)
    pool(lambda e: e.memset(maskS[:], 0.0), (), [CB])
    asel(maskS[:], [[0, 4], [-1, P]], 1, ALU.is_gt, -10000.0)
    pool(lambda e: e.memset(maskIT[:], 0.0), (), [CB])
    asel(maskIT[:], [[0, 4], [1, P]], -1, ALU.is_ge, -10000.0)
    pool(lambda e: e.memset(E4[:], 1.0), (), [CB])
    asel(E4[:], [[1, P]], -32, ALU.is_ge, 0.0)
    asel(E4[:], [[-1, P]], 32, ALU.is_ge, 0.0, base=31)

    NPS = 6
    psum_t = [nc.alloc_psum_tensor("ps%d" % i, [P, 512], F32) for i in range(NPS)]
    psum_b = [S.buf("ps%d" % i) for i in range(NPS)]
    psbf_t = [nc.alloc_psum_tensor("psbf%d" % i, [P, 4, P], BF16) for i in range(2)]
    psbf_b = [S.buf("psbf%d" % i) for i in range(2)]
    pst = {"i": 0, "j": 0}

    ps_held = set()

    def PS(hold=False):
        i = pst["i"]
        while i in ps_held:
            i = (i + 1) % NPS
        pst["i"] = (i + 1) % NPS
        if hold:
            ps_held.add(i)
        return psum_t[i], psum_b[i]

    def PS_release(pb):
        ps_held.discard(psum_b.index(pb))

    def PSB():
        i = pst["j"]
        pst["j"] = (i + 1) % 2
        return psbf_t[i], psbf_b[i]

    pt, pb = PS()
    pe(lambda e: e.matmul(pt[:, 0:P], lhsT=E4[:], rhs=E4[:], start=True, stop=True), [CB], [pb])
    for hh in range(4):
        dve(lambda e, hh=hh: e.tensor_copy(out=bd4[:, hh, :], in_=pt[:, 0:P]), [pb], [CB])
    dve(lambda e: e.tensor_scalar(out=nbd4[:], in0=bd4[:], scalar1=-1.0, scalar2=1.0, op0=ALU.mult, op1=ALU.add), [CB], [CB])

    pp = [sb("pp%d" % l, [P, NPP]) for l in range(L)]
    fg = sb("fg", [P, KC])
    negA = [sb("negA%d" % l, [P, 8]) for l in range(L)]
    dtb = [sb("dtb%d" % l, [P, 8]) for l in range(L)]
    clng = [sb("clng%d" % l, [P, 512]) for l in range(L)]
    clnb = [sb("clnb%d" % l, [P, 512]) for l in range(L)]
    bsrow = [sb("bsrow%d" % l, [1, 512]) for l in range(L)]
    onesrow_f = sb("onesrow_f", [1, P])
    wsT = [sb("wsT%d" % l, [P, 4, P], BF16) for l in range(L)]
    wba = [sb("wba%d" % l, [P, KC, 16], BF16) for l in range(L)]
    vrot = Rot(nc, S, "vrot", [P, 512], F32, 2)
    ldtmp = vrot.t[0][:, :].rearrange("p (g i) -> p g i", g=4)
    ldtmp2 = vrot.t[1][:, 0:KC * 16].rearrange("p (k c) -> p k c", k=KC)
    PB_ = S.buf("params")
    LT = vrot.b[0]
    LT2 = vrot.b[1]
    S.dma("sp", lambda e: e.dma_start(out=fg[:], in_=fg_d), (), [PB_])
    pool(lambda e: e.memset(onesrow_f[:], 1.0), (), [PB_])
    for l in range(L):
        S.dma("sp", lambda e, l=l: e.dma_start(out=pp[l][:], in_=pp_d[l]), (), [PB_])
        S.dma("sp", lambda e, l=l: e.dma_start(out=negA[l][:], in_=pr_d[l, :, PR_ALOG:PR_ALOG + 8].partition_broadcast(P)), (), [PB_])
        S.dma("sp", lambda e, l=l: e.dma_start(out=dtb[l][:], in_=pr_d[l, :, PR_DTB:PR_DTB + 8].partition_broadcast(P)), (), [PB_])
        S.dma("sp", lambda e, l=l: e.dma_start(out=clng[l][:], in_=pr_d[l, :, PR_CLNG:PR_CLNG + 512].partition_broadcast(P)), (), [PB_])
        S.dma("sp", lambda e, l=l: e.dma_start(out=clnb[l][:], in_=pr_d[l, :, PR_CLNB:PR_CLNB + 512].partition_broadcast(P)), (), [PB_])
        S.dma("sp", lambda e, l=l: e.dma_start(out=bsrow[l][:], in_=pr_d[l, :, PR_CBS:PR_CBS + 512]), (), [PB_])
        act(lambda e, l=l: e.activation(out=negA[l][:], in_=negA[l][:], func=AF.Exp), [PB_], [PB_])
        dve(lambda e, l=l: e.tensor_scalar(out=negA[l][:], in0=negA[l][:], scalar1=-1.0, scalar2=None, op0=ALU.mult), [PB_], [PB_])
        S.dma("sp", lambda e, l=l: e.dma_start(out=ldtmp, in_=wsT_d[l]), (), [LT])
        dve(lambda e, l=l: e.tensor_tensor(out=wsT[l][:], in0=ldtmp, in1=Lincl[:].unsqueeze(1).to_broadcast([P, 4, P]), op=ALU.mult),
            [LT, CB], [PB_])
        S.dma("sp", lambda e, l=l: e.dma_start(out=ldtmp2, in_=wba_d[l]), (), [LT2])
        dve(lambda e, l=l: e.tensor_copy(out=wba[l][:], in_=ldtmp2), [LT2], [PB_])

    ring = [sb("ring%d" % i, [P, SLOT_TILES * 128], BF16) for i in range(nslot)]
    ring_b = [S.buf("ring%d" % i) for i in range(nslot)]
    wbfL = [S.buf("wbf%d" % l) for l in range(L)]
    wdgL = [S.buf("wdg%d" % l) for l in range(L)]
    wbfB = [[(wdgL[l] if n.startswith("ca") else wbfL[l]) for (n, _) in GROUPS] for l in range(L)]
    for l in range(L):
        for gi, (n, tiles) in enumerate(GROUPS):
            off, nt = GOFF[n]
            if n.startswith("ca"):
                j = int(n[2:])
                dgt, dgb = ring[0][:, 0:CK * P].rearrange("p (k c) -> p k c", k=CK), ring_b[0]
                for k in range(CK):
                    dve(lambda e, l=l, j=j, k=k, dgt=dgt: e.tensor_scalar(out=dgt[:, k, :], in0=ident_f[:], scalar1=pp[l][:, PP_ADW + j * CK + k:PP_ADW + j * CK + k + 1],
                                                                          scalar2=None, op0=ALU.mult), [CB, PB_], [dgb])
                S.dma("sp", lambda e, l=l, off=off, nt=nt, dgt=dgt: e.dma_start(out=wbf_d[l, :, off:off + nt * 128], in_=ring[0][:, 0:CK * P]),
                      [dgb], [wbfB[l][gi]], track=wbfB[l][gi])
            else:
                S.dma("pool", lambda e, l=l, off=off, nt=nt: e.dma_start(out=wbf_d[l, :, off:off + nt * 128],
                                                                          in_=wbig_d[l, :, off:off + nt * 128]),
                      (), [wbfB[l][gi]])

    xT = sb("xT", [P, KC, T])
    xT_b = [S.buf("xT%d" % k) for k in range(KC)]
    hT = sb("hT", [P, KC, T], BF16)
    hT_b = [S.buf("hT%d" % k) for k in range(KC)]
    iobuf = Rot(nc, S, "io", [P, D], F32, 2)
    stA = [sb("stA%d" % l, [P, 4, CK - 1], BF16) for l in range(L)]
    stA_b = [S.buf("stA%d" % l) for l in range(L)]
    stB = [sb("stB%d" % l, [P, 24, 3]) for l in range(L)]
    stB_b = [S.buf("stB%d" % l) for l in range(L)]
    Sf = [sb("Sf%d" % l, [P, H, P]) for l in range(L)]
    Sbf = [sb("Sbf%d" % l, [P, H, P], BF16) for l in range(L)]
    S_b = [[S.buf("S%d_%d" % (l, hf)) for hf in range(2)] for l in range(L)]
    Sbf_b = [[S.buf("Sbf%d_%d" % (l, hf)) for hf in range(2)] for l in range(L)]

    f32rot = Rot(nc, S, "f32r", [P, T], F32, 4)
    sq16rot = Rot(nc, S, "sq16r", [P, T], BF16, 4)
    workA = Rot(nc, S, "workA", [P, T + CK - 1], BF16, 2)
    aT = sb("aT", [P, 4, T])
    aT_b = [S.buf("aT%d" % j) for j in range(4)]
    azT = sb("azT", [P, 4, T], BF16)
    azT_b = [S.buf("azT%d" % j) for j in range(4)]
    meanA = sb("meanA", [P, T])
    rstdA = sb("rstdA", [P, T])
    mrA_b = S.buf("mrA")
    yaT = sb("yaT", [P, 4, T], BF16)
    yaT_b = [S.buf("yaT%d" % j) for j in range(4)]
    bfrot = Rot(nc, S, "bfr", [P, T], BF16, 2)
    cuT = sb("cuT", [P, 4, T], BF16)
    cuT_b = [S.buf("cuT%d" % j) for j in range(4)]
    uczT = sb("uczT", [P, 4, T], BF16)
    uczT_b = [S.buf("uczT%d" % j) for j in range(4)]
    vs = [sb("vs%d" % b, [P, 512], BF16) for b in range(NB)]
    vs_b = [S.buf("vs%d" % b) for b in range(NB)]
    small = Rot(nc, S, "small", [P, 16], F32, 44)
    smallc = Rot(nc, S, "smallc", [P, 16], F32, 4)
    ycT = sb("ycT", [P, 4, T], BF16)
    ycT_b = [S.buf("ycT%d" % j) for j in range(4)]
    workB = Rot(nc, S, "workB", [P, T + 3], F32, 4)
    qkvT = sb("qkvT", [P, 24, T], BF16)
    qkvT_b = [S.buf("qkvT%d" % c) for c in range(24)]
    bzT = sb("bzT", [P, H, T], BF16)
    bzT_b = [S.buf("bzT%d" % h) for h in range(H)]
    ybT = sb("ybT", [P, H, T], BF16)
    ybT_b = [S.buf("ybT%d" % h) for h in range(H)]
    sqrot = Rot(nc, S, "sqT", [P, 16, P], BF16, 1)
    dn_f = [{n: Rot(nc, S, "dn%d_" % hf + n, [P, 4, P], F32, 1) for n in ("gU", "gL")} for hf in range(2)]
    dn_h = [{n: Rot(nc, S, "dn%d_" % hf + n, [P, 4, P], BF16, 1) for n in
             ("Gs", "GiT", "egR", "Mf", "Mbd", "MbdT", "Moff", "ktok", "Vb", "PmT", "qTs", "R", "Y0", "U", "on")} for hf in range(2)]
    dn_x = [Rot(nc, S, "dn%d_X" % hf, [P, 4, P], BF16, 2) for hf in range(2)]
    dn_xt = [Rot(nc, S, "dn%d_XT" % hf, [P, 4, P], BF16, 2) for hf in range(2)]
    dn_p = [Rot(nc, S, "dn%d_PT" % hf, [P, 4, P], BF16, 2) for hf in range(2)]
    dn_y = [Rot(nc, S, "dn%d_Y" % hf, [P, 4, P], BF16, 2) for hf in range(2)]
    mergedT = sb("mergedT", [P, KC, T], BF16)
    mergedT_b = [S.buf("mergedT%d" % k) for k in range(KC)]

    ring_state = {"next_load": 0, "order": [], "slot_of": {}}

    def ring_plan(order):
        ring_state["order"] = order
        ring_state["next_load"] = 0
        ring_state["slot_of"] = {}

    def ring_issue(upto):
        o = ring_state["order"]
        while ring_state["next_load"] < min(upto, len(o)):
            i = ring_state["next_load"]
            l, gi = o[i]
            n = GROUPS[gi][0]
            off, nt = GOFF[n]
            s = i % nslot
            S.dma("sp", lambda e, l=l, off=off, nt=nt, s=s: e.dma_start(out=ring[s][:, 0:nt * 128], in_=wbf_d[l, :, off:off + nt * 128]),
                  [wbfB[l][gi]], [ring_b[s]])
            ring_state["slot_of"][i] = s
            ring_state["next_load"] += 1

    ring_pos = {"i": 0}

    def ring_get(l, name):
        if S.dry:
            ring_state.setdefault("dry_list", []).append((l, name))
            return ring[0], ring_b[0]
        i = ring_pos["i"]
        o = ring_state["order"]
        assert GROUPS[o[i][1]][0] == name and o[i][0] == l, (o[i], l, name)
        ring_issue(i + nslot)
        ring_pos["i"] = i + 1
        s = ring_state["slot_of"][i]
        return ring[s], ring_b[s]

    def wtile(slot, ti):
        return slot[:, ti * 128:(ti + 1) * 128]

    evac_rr = {"i": 0}

    def evac(out, in_, reads, writes, eng=None):
        if eng is None:
            eng = "act" if evac_rr["i"] % 2 == 0 else "dve"
            evac_rr["i"] += 1
        if eng == "act":
            act(lambda e: e.copy(out=out, in_=in_), reads, writes)
        else:
            dve(lambda e: e.tensor_copy(out=out, in_=in_), reads, writes)

    def rstd_from(out, in_, reads, writes, n_tmp=None):
        act(lambda e: e.activation(out=out, in_=in_, func=AF.Ln, bias=EPS), reads, writes)
        act(lambda e: e.activation(out=out, in_=out, func=AF.Exp, scale=-0.5), writes, writes)

    def proj_chunk(slot, sbuf_, j, ntile8=KC):
        pt, pb = PS()
        for kc in range(KC):
            pe(lambda e, kc=kc: e.matmul(pt[:, 0:T], lhsT=wtile(slot, j * KC + kc), rhs=hT[:, kc, :], start=(kc == 0), stop=(kc == KC - 1)),
               [sbuf_, hT_b[kc]], [pb], sig=(kc == KC - 1))
        return pt, pb

    def rms_to(dst, dst_bufs, gcol, dst_is_h):
        pt, pb = PS()
        for kc in range(KC):
            sq, sqb = sq16rot.next()
            act(lambda e, kc=kc, sq=sq: e.activation(out=sq[:], in_=xT[:, kc, :], func=AF.Square), [xT_b[kc]], [sqb])
            pe(lambda e, kc=kc, sq=sq: e.matmul(pt[:, 0:T], lhsT=onesD_bf[:], rhs=sq[:], start=(kc == 0), stop=(kc == KC - 1)),
               [sqb, CB], [pb])
        rs, rsb = f32rot.next()
        rstd_from(rs[:], pt[:, 0:T], [pb], [rsb])
        for kc in range(KC):
            dve(lambda e, kc=kc: e.scalar_tensor_tensor(out=dst[:, kc, :], in0=xT[:, kc, :], scalar=gcol(kc), in1=rs[:], op0=ALU.mult, op1=ALU.mult),
                [xT_b[kc], rsb, PB_], [dst_bufs[kc] if isinstance(dst_bufs, list) else dst_bufs])

    def tile_layer(l):
        ppl = pp[l]
        rms_to(hT, hT_b, lambda kc: ppl[:, PP_NG + kc:PP_NG + kc + 1], True)

        pending = []

        def flush_silu():
            for (c, wk, wkb, acc, accb, wcol) in pending:
                act(lambda e, c=c, acc=acc: e.activation(out=qkvT[:, c, :], in_=acc[:], func=AF.Silu), [accb], [qkvT_b[c]])
            del pending[:]

        for gi, nm in enumerate(("q0", "q1", "k0", "k1", "v0", "v1")):
            slot, sbuf_ = ring_get(l, nm)
            for jp in range(2):
                items = []
                for jj in (2 * jp, 2 * jp + 1):
                    c = gi * 4 + jj
                    pt, pb = proj_chunk(slot, sbuf_, jj)
                    wk, wkb = workB.next()
                    wcol = lambda k, c=c: ppl[:, PP_BCONV + c * 4 + k:PP_BCONV + c * 4 + k + 1]
                    pool(lambda e, c=c, wk=wk: e.tensor_copy(out=wk[:, 0:3], in_=stB[l][:, c, :]), [stB_b[l]], [wkb])
                    act(lambda e, wk=wk, pt=pt: e.copy(out=wk[:, 3:3 + T], in_=pt[:, 0:T]), [pb], [wkb])
                    acc, accb = f32rot.next()
                    act(lambda e, acc=acc, pt=pt, wcol=wcol: e.activation(out=acc[:], in_=pt[:, 0:T], func=AF.Copy, scale=wcol(3)), [pb, PB_], [accb])
                    pool(lambda e, c=c, wk=wk: e.tensor_copy(out=stB[l][:, c, :], in_=wk[:, T:T + 3]), [wkb], [stB_b[l]])
                    items.append((c, wk, wkb, acc, accb, wcol))
                flush_silu()
                for k in range(3):
                    for (c, wk, wkb, acc, accb, wcol) in items:
                        dve(lambda e, wk=wk, acc=acc, k=k, wcol=wcol: e.scalar_tensor_tensor(out=acc[:], in0=wk[:, k:k + T], scalar=wcol(k), in1=acc[:],
                                                                                             op0=ALU.mult, op1=ALU.add), [wkb, accb, PB_], [accb])
                pending.extend(items)
        flush_silu()
        for gi, nm in enumerate(("bz0", "bz1")):
            slot, sbuf_ = ring_get(l, nm)
            for jj in range(4):
                h = gi * 4 + jj
                pt, pb = proj_chunk(slot, sbuf_, jj)
                act(lambda e, h=h, pt=pt: e.activation(out=bzT[:, h, :], in_=pt[:, 0:T], func=AF.Silu), [pb], [bzT_b[h]])

        gate_box = []

        def merge_gates(m):
            slot, sbuf_ = ring_get(l, "mg%d" % m)
            acc, accb = f32rot.next()
            gts = []
            for br in range(3):
                pt, pb = proj_chunk(slot, sbuf_, br)
                gt, gtb = f32rot.next()
                act(lambda e, pt=pt, gt=gt: e.activation(out=gt[:], in_=pt[:, 0:T], func=AF.Sigmoid), [pb], [gtb])
                gts.append((gt, gtb))
            return acc, accb, gts

        def ac_gen():
            for g2 in range(2):
                slot, sbuf_ = ring_get(l, "ag%d" % g2)
                wks = []
                for jj in range(2):
                    j = 2 * g2 + jj
                    pt, pb = proj_chunk(slot, sbuf_, 2 * jj)
                    sg, sgb = f32rot.next()
                    act(lambda e, pt=pt, sg=sg: e.activation(out=sg[:], in_=pt[:, 0:T], func=AF.Sigmoid), [pb], [sgb])
                    pt, pb = proj_chunk(slot, sbuf_, 2 * jj + 1)
                    wk, wkb = workA.next()
                    pool(lambda e, j=j, wk=wk: e.tensor_copy(out=wk[:, 0:CK - 1], in_=stA[l][:, j, :]), [stA_b[l]], [wkb])
                    dve(lambda e, wk=wk, pt=pt, sg=sg: e.tensor_tensor(out=wk[:, CK - 1:CK - 1 + T], in0=pt[:, 0:T], in1=sg[:], op=ALU.mult), [pb, sgb], [wkb])
                    pool(lambda e, j=j, wk=wk: e.tensor_copy(out=stA[l][:, j, :], in_=wk[:, T:T + CK - 1]), [wkb], [stA_b[l]])
                    wks.append((j, wk, wkb))
                    yield
                for (j, wk, wkb) in wks:
                    slot, sbuf_ = ring_get(l, "ca%d" % j)
                    pt, pb = PS()
                    for k in range(CK):
                        pe(lambda e, k=k, pt=pt, wk=wk, slot=slot: e.matmul(pt[:, 0:T], lhsT=wtile(slot, k), rhs=wk[:, k:k + T], start=(k == 0), stop=(k == CK - 1)),
                           [sbuf_, wkb], [pb], sig=(k == CK - 1))
                    act(lambda e, j=j, pt=pt: e.activation(out=aT[:, j, :], in_=pt[:, 0:T], func=AF.Identity, bias=ppl[:, PP_ADWB + j:PP_ADWB + j + 1]), [pb, PB_], [aT_b[j]])
                    yield
            slot, sbuf_ = ring_get(l, "az")
            for j in range(4):
                pt, pb = proj_chunk(slot, sbuf_, j)
                act(lambda e, j=j, pt=pt: e.activation(out=azT[:, j, :], in_=pt[:, 0:T], func=AF.Silu), [pb], [azT_b[j]])
                yield
            slot, sbuf_ = ring_get(l, "cu")
            for j in range(4):
                pt, pb = proj_chunk(slot, sbuf_, j)
                act(lambda e, j=j, pt=pt: e.activation(out=cuT[:, j, :], in_=pt[:, 0:T], func=AF.Gelu_apprx_tanh), [pb], [cuT_b[j]])
                yield
            slot, sbuf_ = ring_get(l, "cv")
            for b in range(NB):
                pt, pb = PS()
                for kc in range(KC):
                    pe(lambda e, kc=kc, b=b, pt=pt: e.matmul(pt[:, 0:512], lhsT=hT[:, kc, b * P:(b + 1) * P], rhs=slot[:, kc * 512:(kc + 1) * 512],
                                                             start=(kc == 0), stop=(kc == KC - 1)), [sbuf_, hT_b[kc]], [pb], sig=(kc == KC - 1))
                vg, vgb = vrot.next()
                act(lambda e, pt=pt, vg=vg: e.activation(out=vg[:], in_=pt[:, 0:512], func=AF.Gelu_apprx_tanh), [pb], [vgb])
                st, stb = smallc.next()
                mv, mvb = smallc.next()
                dve(lambda e, vg=vg, st=st: e.bn_stats(out=st[:, 0:6], in_=vg[:]), [vgb], [stb])
                dve(lambda e, st=st, mv=mv: e.bn_aggr(out=mv[:, 0:2], in_=st[:, 0:6]), [stb], [mvb])
                rstd_from(mv[:, 2:3], mv[:, 1:2], [mvb], [mvb])
                dve(lambda e, vg=vg, mv=mv: e.tensor_scalar(out=vg[:], in0=vg[:], scalar1=mv[:, 0:1], scalar2=mv[:, 2:3], op0=ALU.subtract, op1=ALU.mult),
                    [vgb, mvb], [vgb])
                dve(lambda e, vg=vg: e.tensor_tensor(out=vg[:], in0=vg[:], in1=clng[l][:], op=ALU.mult), [vgb, PB_], [vgb])
                dve(lambda e, vg=vg, b=b: e.tensor_tensor(out=vs[b][:], in0=vg[:], in1=clnb[l][:], op=ALU.add), [vgb, PB_], [vs_b[b]])
                yield
            slot, sbuf_ = ring_get(l, "cz")
            for j in range(4):
                pt, pb = proj_chunk(slot, sbuf_, j)
                t2, t2b = bfrot.next()
                act(lambda e, pt=pt, t2=t2: e.activation(out=t2[:], in_=pt[:, 0:T], func=AF.Silu), [pb], [t2b])
                dve(lambda e, j=j, t2=t2: e.tensor_tensor(out=uczT[:, j, :], in0=t2[:], in1=cuT[:, j, :], op=ALU.mult), [t2b, cuT_b[j]], [uczT_b[j]])
                yield
            pm, pmb = PS()
            p2, p2b = PS()
            for j in range(4):
                sq, sqb = f32rot.next()
                act(lambda e, j=j, sq=sq: e.activation(out=sq[:], in_=aT[:, j, :], func=AF.Square), [aT_b[j]], [sqb])
                pe(lambda e, j=j: e.matmul(pm[:, 0:T], lhsT=ones512_f[:], rhs=aT[:, j, :], start=(j == 0), stop=(j == 3)), [aT_b[j], CB], [pmb])
                pe(lambda e, j=j, sq=sq: e.matmul(p2[:, 0:T], lhsT=ones512_f[:], rhs=sq[:], start=(j == 0), stop=(j == 3)), [sqb, CB], [p2b])
            act(lambda e: e.copy(out=meanA[:], in_=pm[:, 0:T]), [pmb], [mrA_b])
            msq, msqb = f32rot.next()
            dve(lambda e: e.tensor_tensor(out=msq[:], in0=meanA[:], in1=meanA[:], op=ALU.mult), [mrA_b], [msqb])
            dve(lambda e: e.tensor_tensor(out=rstdA[:], in0=p2[:, 0:T], in1=msq[:], op=ALU.subtract), [p2b, msqb], [mrA_b])
            rstd_from(rstdA[:], rstdA[:], [mrA_b], [mrA_b])
            yield
            for j in range(4):
                t1, t1b = f32rot.next()
                dve(lambda e, j=j, t1=t1: e.tensor_tensor(out=t1[:], in0=aT[:, j, :], in1=meanA[:], op=ALU.subtract), [aT_b[j], mrA_b], [t1b])
                dve(lambda e, t1=t1: e.tensor_tensor(out=t1[:], in0=t1[:], in1=rstdA[:], op=ALU.mult), [t1b, mrA_b], [t1b])
                t2, t2b = bfrot.next()
                act(lambda e, j=j, t1=t1, t2=t2: e.activation(out=t2[:], in_=t1[:], func=AF.Silu, scale=ppl[:, PP_ALNG + j:PP_ALNG + j + 1],
                                                              bias=ppl[:, PP_ALNB + j:PP_ALNB + j + 1]), [t1b, PB_], [t2b])
                dve(lambda e, j=j, t2=t2: e.tensor_tensor(out=yaT[:, j, :], in0=t2[:], in1=azT[:, j, :], op=ALU.mult), [t2b, azT_b[j]], [yaT_b[j]])
                yield

            for g in range(4):
                pt, pb = PS()
                for b in range(NB):
                    pe(lambda e, g=g, b=b, pt=pt: e.matmul(pt[:, b * P:(b + 1) * P], lhsT=vs[b][:, g * P:(g + 1) * P], rhs=wsT[l][:, g, :], start=True, stop=False),
                       [vs_b[b], PB_], [pb], sig=False)
                    pe(lambda e, g=g, b=b, pt=pt: e.matmul(pt[:, b * P:(b + 1) * P], lhsT=onesrow_f[:], rhs=bsrow[l][:, g * P:(g + 1) * P], start=False, stop=True),
                       [PB_], [pb], sig=(b == NB - 1))
                dve(lambda e, g=g, pt=pt: e.tensor_tensor(out=ycT[:, g, :], in0=pt[:, 0:T], in1=uczT[:, g, :], op=ALU.mult), [pb, uczT_b[g]], [ycT_b[g]])
                yield


            gate_box.append(merge_gates(0))
            yield

        def dn_gen():
            def dn_prep(b):
                t0 = b * P
                tsl = slice(t0, t0 + P)
                sqT, sqT_b = sqrot.next()
                pba, pbab = PS()
                for kc in range(KC):
                    pe(lambda e, kc=kc: e.matmul(pba[:, 0:16], lhsT=hT[:, kc, tsl], rhs=wba[l][:, kc, :], start=(kc == 0), stop=(kc == KC - 1)),
                       [hT_b[kc], PB_], [pbab], sig=(kc == KC - 1))
                act(lambda e: e.activation(out=sqT[:, :, :], in_=qkvT[:, 0:16, tsl], func=AF.Square), [qkvT_b[c16] for c16 in range(16)], [sqT_b])
                for c16 in range(16):
                    pe(lambda e, c16=c16: e.matmul(pba[:, 16 + c16:17 + c16], lhsT=sqT[:, c16, :], rhs=onescol_bf[:], start=True, stop=True),
                       [sqT_b, CB], [pbab], sig=(c16 == 15))

                def sm():
                    t, bb = small.next()
                    return t, bb

                ba, bab = sm()
                act(lambda e: e.copy(out=ba[:, 0:16], in_=pba[:, 0:16]), [pbab], [bab])
                z, zb = sm()
                dve(lambda e: e.tensor_tensor(out=z[:, 0:8], in0=ba[:, 8:16], in1=dtb[l][:], op=ALU.add), [bab, PB_], [zb])
                act(lambda e: e.activation(out=z[:, 0:8], in_=z[:, 0:8], func=AF.Exp), [zb], [zb])
                act(lambda e: e.activation(out=z[:, 0:8], in_=z[:, 0:8], func=AF.Ln, bias=1.0), [zb], [zb])
                g_, gb = sm()
                dve(lambda e: e.tensor_tensor(out=g_[:, 0:8], in0=z[:, 0:8], in1=negA[l][:], op=ALU.mult), [zb, PB_], [gb])
                beta, betab = sm()
                act(lambda e: e.activation(out=beta[:, 0:8], in_=ba[:, 0:8], func=AF.Exp, scale=-1.0), [bab], [betab])
                dve(lambda e: e.tensor_scalar(out=beta[:, 0:8], in0=beta[:, 0:8], scalar1=1.0, scalar2=None, op0=ALU.add), [betab], [betab])
                dve(lambda e: e.reciprocal(out=beta[:, 0:8], in_=beta[:, 0:8]), [betab], [betab])
                rinv, rinvb = sm()
                rstd_from(rinv[:, 0:16], pba[:, 16:32], [pbab], [rinvb])
                a2, a2b = sm()
                dve(lambda e: e.tensor_tensor(out=a2[:, 0:8], in0=rinv[:, 8:16], in1=rinv[:, 8:16], op=ALU.mult), [rinvb], [a2b])
                dve(lambda e: e.scalar_tensor_tensor(out=a2[:, 0:8], in0=beta[:, 0:8], scalar=-1.0, in1=a2[:, 0:8], op0=ALU.mult, op1=ALU.mult), [betab, a2b], [a2b])
                cV, cVb = sm()
                dve(lambda e: e.tensor_tensor(out=cV[:, 0:8], in0=beta[:, 0:8], in1=rinv[:, 8:16], op=ALU.mult), [betab, rinvb], [cVb])
                osc, oscb = sm()
                dve(lambda e: e.tensor_scalar(out=osc[:, 0:8], in0=rinv[:, 0:8], scalar1=float(P) ** -0.5, scalar2=None, op0=ALU.mult), [rinvb], [oscb])
                pg, pgb = PS()
                pe(lambda e: e.matmul(pg[:, 0:8], lhsT=Lincl[:], rhs=g_[:, 0:8], start=True, stop=True), [gb, CB], [pgb], sig=False)
                pe(lambda e: e.matmul(pg[:, 8:16], lhsT=ones_f[:], rhs=g_[:, 0:8], start=True, stop=True), [gb, CB], [pgb])
                gc, gcb = sm()
                act(lambda e: e.copy(out=gc[:, 0:16], in_=pg[:, 0:16]), [pgb], [gcb])
                egc, egcb = sm()
                act(lambda e: e.activation(out=egc[:, 0:8], in_=pg[:, 0:8], func=AF.Exp), [pgb], [egcb])
                edl, edlb = sm()
                act(lambda e: e.activation(out=edl[:, 0:8], in_=pg[:, 8:16], func=AF.Exp), [pgb], [edlb])
                edec, edecb = sm()
                dve(lambda e: e.tensor_tensor(out=edec[:, 0:8], in0=gc[:, 8:16], in1=gc[:, 0:8], op=ALU.subtract), [gcb], [edecb])
                act(lambda e: e.activation(out=edec[:, 0:8], in_=edec[:, 0:8], func=AF.Exp), [edecb], [edecb])
                cKS, cKSb = sm()
                dve(lambda e: e.tensor_tensor(out=cKS[:, 0:8], in0=a2[:, 0:8], in1=egc[:, 0:8], op=ALU.mult), [a2b, egcb], [cKSb])

                return dict(locals())

            sc_next = dn_prep(0)
            yield
            for b in range(NB):
                sc = sc_next
                if b + 1 < NB:
                    sc_next = dn_prep(b + 1)
                (tsl, g_, gb, a2, a2b, cV, cVb, osc, oscb, edl, edlb, edec, edecb, cKS, cKSb) = [sc[k] for k in (
                    "tsl", "g_", "gb", "a2", "a2b", "cV", "cVb", "osc", "oscb", "edl", "edlb", "edec", "edecb", "cKS", "cKSb")]

                def sm():
                    t, bb = small.next()
                    return t, bb

                def dn_unit(hf):
                    h0 = 4 * hf
                    gU, gUb = dn_f[hf]["gU"].next()
                    gL, gLb = dn_f[hf]["gL"].next()
                    dve(lambda e, h0=h0, gU=gU: e.tensor_tensor(out=gU[:], in0=Ustr[:].unsqueeze(1).to_broadcast([P, 4, P]),
                                                                in1=g_[:, h0:h0 + 4].unsqueeze(2).to_broadcast([P, 4, P]), op=ALU.mult), [gb, CB], [gUb])
                    pool(lambda e, h0=h0, gL=gL: e.tensor_tensor(out=gL[:], in0=Lincl[:].unsqueeze(1).to_broadcast([P, 4, P]),
                                                                 in1=g_[:, h0:h0 + 4].unsqueeze(2).to_broadcast([P, 4, P]), op=ALU.mult), [gb, CB], [gLb])
                    flat = lambda t: t[:].rearrange("p h j -> p (h j)")
                    Gs, Gsb = dn_h[hf]["Gs"].next()
                    GiT, GiTb = dn_h[hf]["GiT"].next()
                    egR, egRb = dn_h[hf]["egR"].next()
                    pt, pb = PS()
                    pe(lambda e, pt=pt, gU=gU: e.matmul(pt[:, :], lhsT=Lincl[:], rhs=flat(gU), start=True, stop=False), [gUb, CB], [pb], sig=False)
                    pe(lambda e, pt=pt: e.matmul(pt[:, :], lhsT=ident_bf[:], rhs=flat(maskS), start=False, stop=True), [CB], [pb])
                    act(lambda e, pt=pt, Gs=Gs: e.activation(out=flat(Gs), in_=pt[:, :], func=AF.Exp), [pb], [Gsb])
                    pt, pb = PS()
                    pe(lambda e, pt=pt, gL=gL: e.matmul(pt[:, :], lhsT=Ustr[:], rhs=flat(gL), start=True, stop=False), [gLb, CB], [pb], sig=False)
                    pe(lambda e, pt=pt: e.matmul(pt[:, :], lhsT=ident_bf[:], rhs=flat(maskIT), start=False, stop=True), [CB], [pb])
                    act(lambda e, pt=pt, GiT=GiT: e.activation(out=flat(GiT), in_=pt[:, :], func=AF.Exp), [pb], [GiTb])
                    pt, pb = PS()
                    pe(lambda e, pt=pt, gL=gL: e.matmul(pt[:, :], lhsT=ones_f[:], rhs=flat(gL), start=True, stop=True), [gLb, CB], [pb])
                    act(lambda e, pt=pt, egR=egR: e.activation(out=flat(egR), in_=pt[:, :], func=AF.Exp), [pb], [egRb])
                    yield
                    qTs, qTsb = dn_h[hf]["qTs"].next()
                    dve(lambda e, qTs=qTs, egR=egR, h0=h0: e.tensor_tensor(out=qTs[:], in0=qkvT[:, h0:h0 + 4, tsl], in1=egR[:], op=ALU.mult),
                        [qkvT_b[h0 + i] for i in range(4)] + [egRb], [qTsb])
                    ktok, ktokb = dn_h[hf]["ktok"].next()
                    Vb, Vbb = dn_h[hf]["Vb"].next()
                    ptb, pbb = PSB()
                    for hh in range(4):
                        pe(lambda e, hh=hh, ptb=ptb: e.transpose(ptb[:, hh, :], qkvT[:, 8 + h0 + hh, tsl], ident_bf[:]), [qkvT_b[8 + h0 + hh], CB], [pbb], sig=(hh == 3))
                    dve(lambda e, ptb=ptb, ktok=ktok: e.tensor_tensor(out=ktok[:], in0=ptb[:], in1=edec[:, h0:h0 + 4].unsqueeze(2).to_broadcast([P, 4, P]), op=ALU.mult),
                        [pbb, edecb], [ktokb])
                    yield
                    ptb, pbb = PSB()
                    for hh in range(4):
                        pe(lambda e, hh=hh, ptb=ptb: e.transpose(ptb[:, hh, :], qkvT[:, 16 + h0 + hh, tsl], ident_bf[:]), [qkvT_b[16 + h0 + hh], CB], [pbb], sig=(hh == 3))
                    dve(lambda e, ptb=ptb, Vb=Vb: e.tensor_tensor(out=Vb[:], in0=ptb[:], in1=cV[:, h0:h0 + 4].unsqueeze(2).to_broadcast([P, 4, P]), op=ALU.mult),
                        [pbb, cVb], [Vbb])
                    yield
                    Mf, Mfb = dn_h[hf]["Mf"].next()
                    PmT, PmTb = dn_h[hf]["PmT"].next()
                    pt, pb = PS()
                    for hh in range(4):
                        kT = qkvT[:, 8 + h0 + hh, tsl]
                        pe(lambda e, hh=hh, pt=pt, kT=kT: e.matmul(pt[:, hh * P:(hh + 1) * P], lhsT=kT, rhs=kT, start=True, stop=True), [qkvT_b[8 + h0 + hh]], [pb], sig=(hh == 3))
                    for hh in range(4):
                        dve(lambda e, hh=hh, pt=pt, Mf=Mf, Gs=Gs: e.scalar_tensor_tensor(out=Mf[:, hh, :], in0=pt[:, hh * P:(hh + 1) * P], scalar=a2[:, h0 + hh:h0 + hh + 1],
                                                                                         in1=Gs[:, hh, :], op0=ALU.mult, op1=ALU.mult), [pb, a2b, Gsb], [Mfb])
                    pt, pb = PS()
                    for hh in range(4):
                        pe(lambda e, hh=hh, pt=pt: e.matmul(pt[:, hh * P:(hh + 1) * P], lhsT=qkvT[:, 8 + h0 + hh, tsl], rhs=qkvT[:, h0 + hh, tsl], start=True, stop=True),
                           [qkvT_b[8 + h0 + hh], qkvT_b[h0 + hh]], [pb], sig=(hh == 3))
                    dve(lambda e, pt=pt, PmT=PmT, GiT=GiT: e.tensor_tensor(out=flat(PmT), in0=pt[:, :], in1=flat(GiT), op=ALU.mult), [pb, GiTb], [PmTb])
                    yield
                    Mbd, Mbdb = dn_h[hf]["Mbd"].next()
                    MbdT, MbdTb = dn_h[hf]["MbdT"].next()
                    pool(lambda e, Mbd=Mbd, Mf=Mf: e.tensor_tensor(out=Mbd[:], in0=Mf[:], in1=bd4[:], op=ALU.mult), [Mfb, CB], [Mbdb])
                    ptb, pbb = PSB()
                    for hh in range(4):
                        pe(lambda e, hh=hh, ptb=ptb, Mf=Mf: e.transpose(ptb[:, hh, :], Mf[:, hh, :], ident_bf[:]), [Mfb, CB], [pbb], sig=(hh == 3))
                    dve(lambda e, ptb=ptb, MbdT=MbdT: e.tensor_tensor(out=MbdT[:], in0=ptb[:], in1=bd4[:], op=ALU.mult), [pbb, CB], [MbdTb])
                    yield
                    Moff, Moffb = dn_h[hf]["Moff"].next()
                    pool(lambda e, Moff=Moff, Mf=Mf: e.tensor_tensor(out=Moff[:], in0=Mf[:], in1=nbd4[:], op=ALU.mult), [Mfb, CB], [Moffb])

                    def grp(lh, lhb, rh, rhb, extra=None, exb=None):
                        pt_, pb_ = PS()
                        for hh in range(4):
                            if extra is None:
                                pe(lambda e, hh=hh: e.matmul(pt_[:, hh * P:(hh + 1) * P], lhsT=lh[:, hh, :], rhs=rh[:, hh, :], start=True, stop=True),
                                   [lhb, rhb], [pb_], sig=(hh == 3))
                            else:
                                pe(lambda e, hh=hh: e.matmul(pt_[:, hh * P:(hh + 1) * P], lhsT=lh[:, hh, :], rhs=rh[:, hh, :], start=True, stop=False),
                                   [lhb, rhb], [pb_], sig=False)
                                pe(lambda e, hh=hh: e.matmul(pt_[:, hh * P:(hh + 1) * P], lhsT=ident_bf[:], rhs=extra[:, hh, :], start=False, stop=True),
                                   [exb, CB], [pb_], sig=(hh == 3))
                        return pt_, pb_

                    PT, PTb = dn_p[hf].next()
                    pool(lambda e, PT=PT, MbdT=MbdT: e.tensor_tensor(out=PT[:], in0=MbdT[:], in1=ident4_bf[:], op=ALU.add), [MbdTb, CB], [PTb])
                    X, Xb, XT, XTb = Mbd, Mbdb, MbdT, MbdTb
                    X2, X2b = dn_x[hf].next()
                    X2T, X2Tb = dn_xt[hf].next()
                    p1, p1b = grp(XT, XTb, X, Xb)
                    p2, p2b = grp(X, Xb, XT, XTb)
                    evac(flat(X2), p1[:, :], [p1b], [X2b], eng="act")
                    evac(flat(X2T), p2[:, :], [p2b], [X2Tb], eng="dve")
                    yield
                    X, Xb, XT, XTb = X2, X2b, X2T, X2Tb
                    for k in range(3):
                        PT2, PT2b = dn_p[hf].next()
                        p3, p3b = grp(X, Xb, PT, PTb, extra=PT, exb=PTb)
                        Xn, Xnb = dn_x[hf].next()
                        p1, p1b = grp(XT, XTb, X, Xb)
                        if k < 2:
                            XnT, XnTb = dn_xt[hf].next()
                            p2, p2b = grp(X, Xb, XT, XTb)
                        evac(flat(PT2), p3[:, :], [p3b], [PT2b], eng="dve")
                        evac(flat(Xn), p1[:, :], [p1b], [Xnb], eng="act")
                        if k < 2:
                            evac(flat(XnT), p2[:, :], [p2b], [XnTb], eng="dve")
                        else:
                            XnT, XnTb = None, None
                        yield
                        PT, PTb, X, Xb, XT, XTb = PT2, PT2b, Xn, Xnb, XnT, XnTb
                    PT2, PT2b = dn_p[hf].next()
                    p3, p3b = grp(X, Xb, PT, PTb, extra=PT, exb=PTb)
                    evac(flat(PT2), p3[:, :], [p3b], [PT2b])
                    yield
                    TbdT, TbdTb = PT2, PT2b
                    Nm, Nmb = dn_x[hf].next()
                    NT, NTb = dn_xt[hf].next()
                    NT1, NT1b = dn_x[hf].next()
                    p1, p1b = grp(TbdT, TbdTb, Moff, Moffb)
                    p2, p2b = grp(Moff, Moffb, TbdT, TbdTb)
                    evac(flat(Nm), p1[:, :], [p1b], [Nmb], eng="dve")
                    evac(flat(NT), p2[:, :], [p2b], [NTb], eng="act")
                    pool(lambda e, NT=NT, NT1=NT1: e.tensor_tensor(out=NT1[:], in0=NT[:], in1=ident4_bf[:], op=ALU.add), [NTb, CB], [NT1b])
                    yield
                    N2T1, N2T1b = dn_xt[hf].next()
                    p1, p1b = grp(Nm, Nmb, NT, NTb, extra=ident4_bf, exb=CB)
                    evac(flat(N2T1), p1[:, :], [p1b], [N2T1b])
                    yield
                    R, Rb = dn_h[hf]["R"].next()
                    pt, pb = PS()
                    for hh in range(4):
                        pe(lambda e, hh=hh, pt=pt: e.matmul(pt[:, hh * P:(hh + 1) * P], lhsT=qkvT[:, 8 + h0 + hh, tsl], rhs=Sbf[l][:, h0 + hh, :], start=True, stop=True),
                           [qkvT_b[8 + h0 + hh], Sbf_b[l][hf]], [pb], sig=(hh == 3))
                    for hh in range(4):
                        dve(lambda e, hh=hh, pt=pt, R=R, Vb=Vb: e.scalar_tensor_tensor(out=R[:, hh, :], in0=pt[:, hh * P:(hh + 1) * P], scalar=cKS[:, h0 + hh:h0 + hh + 1],
                                                                                       in1=Vb[:, hh, :], op0=ALU.mult, op1=ALU.add), [pb, cKSb, Vbb], [Rb])
                    yield
                    Y0, Y0b = dn_h[hf]["Y0"].next()
                    p1, p1b = grp(TbdT, TbdTb, R, Rb)
                    evac(flat(Y0), p1[:, :], [p1b], [Y0b])
                    yield
                    Y1, Y1b = dn_h[hf]["U"].next()
                    p1, p1b = grp(N2T1, N2T1b, Y0, Y0b)
                    evac(flat(Y1), p1[:, :], [p1b], [Y1b])
                    yield
                    W, Wb = dn_y[hf].next()
                    p1, p1b = grp(NT1, NT1b, Y1, Y1b)
                    evac(flat(W), p1[:, :], [p1b], [Wb])
                    yield
                    po, pob = PS(hold=True)
                    for hh in range(4):
                        pe(lambda e, hh=hh, qTs=qTs: e.matmul(po[:, hh * P:(hh + 1) * P], lhsT=qTs[:, hh, :], rhs=Sbf[l][:, h0 + hh, :], start=True, stop=False),
                           [qTsb, Sbf_b[l][hf]], [pob], sig=False)
                        pe(lambda e, hh=hh, PmT=PmT, W=W: e.matmul(po[:, hh * P:(hh + 1) * P], lhsT=PmT[:, hh, :], rhs=W[:, hh, :], start=False, stop=True),
                           [PmTb, Wb], [pob], sig=(hh == 3))
                    pt, pb = PS()
                    for hh in range(4):
                        pe(lambda e, hh=hh, pt=pt, ktok=ktok, W=W: e.matmul(pt[:, hh * P:(hh + 1) * P], lhsT=ktok[:, hh, :], rhs=W[:, hh, :], start=True, stop=True),
                           [ktokb, Wb], [pb], sig=(hh == 3))
                    for hh in range(4):
                        dve(lambda e, hh=hh, pt=pt: e.scalar_tensor_tensor(out=Sf[l][:, h0 + hh, :], in0=Sf[l][:, h0 + hh, :], scalar=edl[:, h0 + hh:h0 + hh + 1],
                                                                           in1=pt[:, hh * P:(hh + 1) * P], op0=ALU.mult, op1=ALU.add), [pb, edlb, S_b[l][hf]], [S_b[l][hf]])
                    act(lambda e: e.copy(out=Sbf[l][:, h0:h0 + 4, :], in_=Sf[l][:, h0:h0 + 4, :]), [S_b[l][hf]], [Sbf_b[l][hf]])
                    ss, ssb = sm()
                    junk, junkb = vrot.next()
                    act(lambda e, junk=junk: e.activation(out=junk[:, :], in_=po[:, :], func=AF.Square), [pob], [junkb])
                    dve(lambda e, junk=junk: e.reduce_sum(out=ss[:, 0:4], in_=junk[:, :].rearrange("p (h e) -> p h e", h=4), axis=mybir.AxisListType.X), [junkb], [ssb])
                    dve(lambda e: e.tensor_tensor(out=ss[:, 4:8], in0=osc[:, h0:h0 + 4], in1=osc[:, h0:h0 + 4], op=ALU.mult), [oscb, ssb], [ssb])
                    dve(lambda e: e.scalar_tensor_tensor(out=ss[:, 0:4], in0=ss[:, 0:4], scalar=1.0 / P, in1=ss[:, 4:8], op0=ALU.mult, op1=ALU.mult), [ssb], [ssb])
                    rstd_from(ss[:, 0:4], ss[:, 0:4], [ssb], [ssb])
                    dve(lambda e: e.tensor_tensor(out=ss[:, 8:12], in0=ss[:, 0:4], in1=osc[:, h0:h0 + 4], op=ALU.mult), [ssb, oscb], [ssb])
                    on, onb = dn_h[hf]["on"].next()
                    dve(lambda e, on=on: e.tensor_tensor(out=on[:], in0=po[:, :].rearrange("p (h e) -> p h e", h=4), in1=ss[:, 8:12].unsqueeze(2).to_broadcast([P, 4, P]), op=ALU.mult),
                        [pob, ssb], [onb])
                    PS_release(pob)
                    yield
                    ptb, pbb = PSB()
                    for hh in range(4):
                        pe(lambda e, hh=hh, ptb=ptb, on=on: e.transpose(ptb[:, hh, :], on[:, hh, :], ident_bf[:]), [onb, CB], [pbb], sig=(hh == 3))
                    dve(lambda e, ptb=ptb: e.scalar_tensor_tensor(out=ybT[:, h0:h0 + 4, tsl], in0=ptb[:], scalar=ppl[:, PP_ONG:PP_ONG + 1], in1=bzT[:, h0:h0 + 4, tsl],
                                                                  op0=ALU.mult, op1=ALU.mult), [pbb, PB_] + [bzT_b[h0 + i] for i in range(4)], [ybT_b[h0 + i] for i in range(4)])

                gens = [dn_unit(0), dn_unit(1)]
                while gens:
                    nxt = []
                    for gen in gens:
                        try:
                            next(gen)
                            nxt.append(gen)
                        except StopIteration:
                            pass
                    gens = nxt
                    yield


        alive = [dn_gen(), ac_gen()]
        while alive:
            nxt = []
            for gen_ in alive:
                try:
                    next(gen_)
                    nxt.append(gen_)
                except StopIteration:
                    pass
            alive = nxt

        br_src = [(yaT, yaT_b, 4, 0), (ybT, ybT_b, 8, 4), (ycT, ycT_b, 4, 12)]
        for m in range(8):
            if m == 0:
                acc, accb, gts = gate_box.pop()
            else:
                acc, accb, gts = merge_gates(m)
            slot, sbuf_ = ring_get(l, "pj%d" % m)
            for br in range(3):
                gt, gtb = gts[br]
                src, srcb, nk, base = br_src[br]
                pp_, ppb = PS()
                for kc in range(nk):
                    pe(lambda e, kc=kc, pp_=pp_, src=src, base=base, slot=slot: e.matmul(pp_[:, 0:T], lhsT=wtile(slot, base + kc), rhs=src[:, kc, :], start=(kc == 0), stop=(kc == nk - 1)),
                       [sbuf_, srcb[kc]], [ppb], sig=(kc == nk - 1))
                if br == 0:
                    dve(lambda e, pp_=pp_, gt=gt, acc=acc: e.tensor_tensor(out=acc[:], in0=pp_[:, 0:T], in1=gt[:], op=ALU.mult), [ppb, gtb], [accb])
                else:
                    dve(lambda e, pp_=pp_, gt=gt: e.tensor_tensor(out=gt[:], in0=pp_[:, 0:T], in1=gt[:], op=ALU.mult), [ppb, gtb], [gtb])
                    if br == 1:
                        pool(lambda e, gt=gt, acc=acc: e.tensor_tensor(out=acc[:], in0=acc[:], in1=gt[:], op=ALU.add), [accb, gtb], [accb])
                    else:
                        pool(lambda e, gt=gt, acc=acc, m=m: e.tensor_tensor(out=mergedT[:, m, :], in0=acc[:], in1=gt[:], op=ALU.add), [accb, gtb], [mergedT_b[m]])
        for g in range(2):
            slot, sbuf_ = ring_get(l, "wo%d" % g)
            for mm_ in range(4):
                m = 4 * g + mm_
                pt, pb = PS()
                for kc in range(KC):
                    pe(lambda e, kc=kc, pt=pt, mm_=mm_: e.matmul(pt[:, 0:T], lhsT=wtile(slot, mm_ * KC + kc), rhs=mergedT[:, kc, :], start=(kc == 0), stop=(kc == KC - 1)),
                       [sbuf_, mergedT_b[kc]], [pb], sig=(kc == KC - 1))
                dve(lambda e, m=m, pt=pt: e.tensor_tensor(out=xT[:, m, :], in0=xT[:, m, :], in1=pt[:, 0:T], op=ALU.add), [pb, xT_b[m]], [xT_b[m]])

    S.dry = True
    for l in range(L):
        tile_layer(l)
    S.dry = False
    name2gi = {n: i for i, (n, _) in enumerate(GROUPS)}
    order = [(l, name2gi[n]) for (l, n) in ring_state["dry_list"]] * (nseq * ntile)
    ring_plan(order)

    out_bufs = []
    for s in range(nseq):
        for l in range(L):
            pool(lambda e, l=l: e.memset(stA[l][:], 0.0), (), [stA_b[l]])
            pool(lambda e, l=l: e.memset(stB[l][:], 0.0), (), [stB_b[l]])
            for hf in range(2):
                pool(lambda e, l=l, hf=hf: e.memset(Sf[l][:, 4 * hf:4 * hf + 4, :], 0.0), (), [S_b[l][hf]])
                pool(lambda e, l=l, hf=hf: e.memset(Sbf[l][:, 4 * hf:4 * hf + 4, :], 0.0), (), [Sbf_b[l][hf]])
        for ti in range(ntile):
            tok0 = s * seqlen + ti * T
            for b in range(NB):
                xin, xinb = iobuf.next()
                S.dma("sp", lambda e, xin=xin, b=b: e.dma_start(out=xin[:], in_=x_d[tok0 + b * P: tok0 + (b + 1) * P, :]), (), [xinb])
                for half in range(2):
                    pt, pb = PS()
                    for kk in range(4):
                        kc = half * 4 + kk
                        pe(lambda e, kk=kk, kc=kc, pt=pt, xin=xin: e.matmul(pt[:, kk * P:(kk + 1) * P], lhsT=xin[:, kc * P:(kc + 1) * P], rhs=ident_f[:], start=True, stop=True),
                           [xinb, CB], [pb], sig=(kk == 3))
                    evac(xT[:, half * 4:half * 4 + 4, b * P:(b + 1) * P], pt[:, :].rearrange("p (k t) -> p k t", k=4), [pb], [xT_b[half * 4 + i] for i in range(4)])
            for l in range(L):
                tile_layer(l)
            rms_to(xT, xT_b, lambda kc: fg[:, kc:kc + 1], False)
            for b in range(NB):
                ot, otb = iobuf.next()
                for half in range(2):
                    pt, pb = PS()
                    for kk in range(4):
                        kc = half * 4 + kk
                        pe(lambda e, kk=kk, kc=kc, pt=pt, b=b: e.matmul(pt[:, kk * P:(kk + 1) * P], lhsT=xT[:, kc, b * P:(b + 1) * P], rhs=ident_f[:], start=True, stop=True),
                           [xT_b[kc], CB], [pb], sig=(kk == 3))
                    evac(ot[:, half * 512:(half + 1) * 512], pt[:, :], [pb], [otb])
                S.dma("sp", lambda e, ot=ot, b=b: e.dma_start(out=out_d[tok0 + b * P: tok0 + (b + 1) * P, :], in_=ot[:]), [otb], ())
                out_bufs.append(otb)
    S.wait_all("sp", iobuf.b)
    build.last_sched = S
    return nc


_CACHE = {}


def kernel(**inputs):
    x = np.asarray(inputs["x"], np.float32)
    B, SEQ, _ = x.shape
    L = inputs["w_in"].shape[0]
    nseq = B // NCORES
    lay = host_layout({k: np.asarray(v, np.float32) for k, v in inputs.items()}, L)
    key = (nseq, SEQ, L)
    if key not in _CACHE:
        _CACHE[key] = build(nseq, SEQ, L)
    nc = _CACHE[key]
    in_maps = []
    for c in range(NCORES):
        m = dict(lay)
        m["x"] = np.ascontiguousarray(x[c * nseq:(c + 1) * nseq].reshape(nseq * SEQ, D))
        in_maps.append(m)
    res = run_bass_kernel_spmd(nc, in_maps, core_ids=list(range(NCORES)))
    out = np.stack([np.asarray(r["out"]).reshape(nseq, SEQ, D) for r in res.results], axis=0)
    return out.reshape(B, SEQ, D).astype(np.float32)
```

```python
import numpy as np
import concourse.bass as bass
import concourse.mybir as mybir
from concourse.bass_utils import run_bass_kernel_spmd

F32 = mybir.dt.float32
BF16 = mybir.dt.bfloat16
AF = mybir.ActivationFunctionType
ALU = mybir.AluOpType

P = 128
D = 1024
KC = 8
H = 8
NIN = 10256
EPS = 1e-6
CK = 31
NCORES = 8


class Buf:
    __slots__ = ("name", "w", "r", "dsem", "dcnt")

    def __init__(self, name):
        self.name = name
        self.w = None
        self.r = {}
        self.dsem = None
        self.dcnt = 0


class Sched:
    EPOCH = 8000

    def __init__(self, nc):
        self.nc = nc
        self.eng = {"pe": nc.tensor, "act": nc.scalar, "dve": nc.vector, "pool": nc.gpsimd, "sp": nc.sync}
        self.sems = {}
        self.cnt = {}
        self.epoch = {}
        self.seen = {e: {} for e in self.eng}
        self.nsem = 0
        for e in self.eng:
            self.epoch[e] = 0
            self._new_epoch_sem(e)
        self.n_ops = {e: 0 for e in self.eng}
        self.dry = False

    def _new_sem(self):
        self.nsem += 1
        return self.nc.alloc_semaphore("s%d" % self.nsem)

    def _new_epoch_sem(self, e):
        key = (e, self.epoch[e])
        self.sems[key] = self._new_sem()
        self.cnt[e] = 0

    def buf(self, name="b"):
        return Buf(name)

    def _deps(self, reads, writes):
        deps = {}
        for b in reads:
            if b.w is not None:
                k, v = b.w
                if deps.get(k, 0) < v:
                    deps[k] = v
        for b in writes:
            if b.w is not None:
                k, v = b.w
                if deps.get(k, 0) < v:
                    deps[k] = v
            for k, v in b.r.items():
                if deps.get(k, 0) < v:
                    deps[k] = v
        return deps

    def _waits(self, e, deps):
        seen = self.seen[e]
        waits = []
        for k, v in deps.items():
            if e == "pe" and k[0] == "pe":
                continue
            if seen.get(k, 0) >= v:
                continue
            waits.append((k, v))
            seen[k] = v
        return waits

    def _mark(self, me, reads, writes):
        k, v = me
        for b in writes:
            b.w = me
            b.r = {}
        for b in reads:
            if b not in writes:
                if b.r.get(k, 0) < v:
                    b.r[k] = v

    def op(self, e, fn, reads=(), writes=(), sig=True):
        if self.dry:
            return None
        waits = self._waits(e, self._deps(reads, writes))
        eng = self.eng[e]
        for (k, v) in waits[:-1]:
            eng.wait_ge(self.sems[k], v)
        ins = fn(eng)
        if waits:
            k, v = waits[-1]
            ins._wait_ge(self.sems[k], v)
        key = (e, self.epoch[e])
        if sig:
            self.cnt[e] += 1
            ins.then_inc(self.sems[key], 1)
            me = (key, self.cnt[e])
            if self.cnt[e] >= self.EPOCH:
                self.epoch[e] += 1
                self._new_epoch_sem(e)
        else:
            me = (key, self.cnt[e] + 1)
        self._mark(me, reads, writes)
        self.n_ops[e] += 1
        return ins

    def dma(self, q, fn, reads=(), writes=(), track=None):
        if self.dry:
            return None
        waits = self._waits(q, self._deps(reads, writes))
        eng = self.eng[q]
        for (k, v) in waits[:-1]:
            eng.wait_ge(self.sems[k], v)
        ins = fn(eng)
        if waits:
            k, v = waits[-1]
            ins._wait_ge(self.sems[k], v)
        tb = track if track is not None else (writes[0] if writes else reads[0])
        if tb.dsem is None:
            tb.dsem = ("dma", id(tb))
            self.sems[tb.dsem] = self._new_sem()
        tb.dcnt += 16
        ins.then_inc(self.sems[tb.dsem], 16)
        self._mark((tb.dsem, tb.dcnt), reads, writes)
        return ins

    def wait_all(self, e, bufs):
        waits = self._waits(e, self._deps(bufs, bufs))
        for (k, v) in waits:
            self.eng[e].wait_ge(self.sems[k], v)


class Rot:
    def __init__(self, nc, S, name, shape, dt, n):
        self.t = [nc.alloc_sbuf_tensor("rot_%s%d" % (name, i), shape, dt) for i in range(n)]
        self.b = [S.buf("%s%d" % (name, i)) for i in range(n)]
        self.i = 0

    def next(self):
        i = self.i
        self.i = (i + 1) % len(self.t)
        return self.t[i], self.b[i]


SPLIT = dict(a_val=0, a_glu=512, a_z=1024, q=1536, k=2560, v=3584, b_z=4608, beta=5632, alpha=5640,
             c_u=5648, c_v=6160, c_z=6672, gate=7184)
SLOT_TILES = 32


def group_defs():
    gs = []

    def win_chunks(name, col0, n=4):
        tiles = []
        for j in range(n):
            for kc in range(KC):
                tiles.append(("w_in", kc, col0 + 128 * j))
        gs.append((name, tiles))

    for g2 in range(2):
        tiles = []
        for j in (2 * g2, 2 * g2 + 1):
            for kc in range(KC):
                tiles.append(("w_in", kc, SPLIT["a_glu"] + 128 * j))
            for kc in range(KC):
                tiles.append(("w_in", kc, SPLIT["a_val"] + 128 * j))
        gs.append(("ag%d" % g2, tiles))
        for j in (2 * g2, 2 * g2 + 1):
            gs.append(("ca%d" % j, [("diag", j, k) for k in range(CK)]))
    win_chunks("az", SPLIT["a_z"])
    win_chunks("cu", SPLIT["c_u"])
    tiles = []
    for kc in range(KC):
        for j in range(4):
            tiles.append(("w_in", kc, SPLIT["c_v"] + 128 * j))
    gs.append(("cv", tiles))
    win_chunks("cz", SPLIT["c_z"])
    for nm in ("q", "k", "v"):
        win_chunks(nm + "0", SPLIT[nm])
        win_chunks(nm + "1", SPLIT[nm] + 512)
    win_chunks("bz0", SPLIT["b_z"])
    win_chunks("bz1", SPLIT["b_z"] + 512)
    for m in range(8):
        tiles = []
        for br in range(3):
            for kc in range(KC):
                tiles.append(("w_in", kc, SPLIT["gate"] + br * 1024 + m * 128))
        gs.append(("mg%d" % m, tiles))
        tiles = []
        for kc in range(4):
            tiles.append(("a_proj", kc, m * 128))
        for kc in range(8):
            tiles.append(("b_proj", kc, m * 128))
        for kc in range(4):
            tiles.append(("c_proj", kc, m * 128))
        gs.append(("pj%d" % m, tiles))
    for g in range(2):
        tiles = []
        for mm_ in range(4):
            for kc in range(KC):
                tiles.append(("w_out", kc, (4 * g + mm_) * 128))
        gs.append(("wo%d" % g, tiles))
    return gs


GROUPS = group_defs()
GOFF = {}
_o = 0
for _n, _t in GROUPS:
    if not _n.startswith("ca"):
        GOFF[_n] = (_o, len(_t))
        _o += len(_t) * 128
TOT = _o
for _n, _t in GROUPS:
    if _n.startswith("ca"):
        GOFF[_n] = (_o, len(_t))
        _o += len(_t) * 128
TOT2 = _o

PP_NG, PP_ADW, PP_ADWB, PP_ALNG, PP_ALNB, PP_BCONV, PP_ONG = 0, 8, 8 + 124, 136, 140, 144, 240
NPP = 241
PR_ALOG, PR_DTB, PR_CLNG, PR_CLNB, PR_CBS = 0, 8, 16, 528, 1040
NPR = 1552


def host_layout(inputs, L):
    wbig = np.empty((L, P, TOT), np.float32)
    for l in range(L):
        mats = {"w_in": inputs["w_in"][l], "a_proj": inputs["a_proj"][l], "b_proj": inputs["b_proj"][l],
                "c_proj": inputs["c_proj"][l], "w_out": inputs["w_out"][l]}
        for name, tiles in GROUPS:
            if name.startswith("ca"):
                continue
            off = GOFF[name][0]
            for ti, (mat, kc, c0) in enumerate(tiles):
                wbig[l, :, off + ti * 128: off + (ti + 1) * 128] = mats[mat][kc * 128:(kc + 1) * 128, c0:c0 + 128]
    wba = np.empty((L, P, KC, 16), np.float32)
    for l in range(L):
        w = inputs["w_in"][l][:, SPLIT["beta"]:SPLIT["beta"] + 16]
        wba[l] = w.reshape(KC, P, 16).transpose(1, 0, 2)
    pp = np.zeros((L, P, NPP), np.float32)
    pr = np.zeros((L, 1, NPR), np.float32)
    wsT = np.empty((L, P, 4, P), np.float32)
    for l in range(L):
        pp[l, :, PP_NG:PP_NG + 8] = inputs["norm_g"][l].reshape(KC, P).T
        pp[l, :, PP_ADW:PP_ADW + 124] = inputs["a_dw"][l].reshape(CK, 4, P).transpose(2, 1, 0).reshape(P, 124)
        pp[l, :, PP_ADWB:PP_ADWB + 4] = inputs["a_dw_b"][l].reshape(4, P).T
        pp[l, :, PP_ALNG:PP_ALNG + 4] = inputs["a_ln_g"][l].reshape(4, P).T
        pp[l, :, PP_ALNB:PP_ALNB + 4] = inputs["a_ln_b"][l].reshape(4, P).T
        pp[l, :, PP_BCONV:PP_BCONV + 96] = inputs["b_conv"][l].reshape(4, 24, P).transpose(2, 1, 0).reshape(P, 96)
        pp[l, :, PP_ONG] = inputs["b_onorm_g"][l]
        pr[l, 0, PR_ALOG:PR_ALOG + 8] = inputs["b_a_log"][l]
        pr[l, 0, PR_DTB:PR_DTB + 8] = inputs["b_dt_bias"][l]
        pr[l, 0, PR_CLNG:PR_CLNG + 512] = inputs["c_ln_g"][l]
        pr[l, 0, PR_CLNB:PR_CLNB + 512] = inputs["c_ln_b"][l]
        pr[l, 0, PR_CBS:PR_CBS + 512] = inputs["c_bs"][l].reshape(512)
        wsT[l] = inputs["c_ws"][l].transpose(2, 0, 1)
    fg = np.ascontiguousarray(inputs["final_g"].reshape(KC, P).T)
    return dict(wbig=wbig, wba=wba, pp=pp, pr=pr, wsT=wsT, fg=fg)


def build(nseq, seqlen, L, T=256, nslot=3):
    NB = T // P
    ntile = seqlen // T
    NTOK = nseq * seqlen
    nc = bass.Bass("TRN2", target_bir_lowering=False)
    S = Sched(nc)

    x_d = nc.dram_tensor("x", [NTOK, D], F32, kind="ExternalInput").ap()
    wbig_d = nc.dram_tensor("wbig", [L, P, TOT], F32, kind="ExternalInput").ap()
    wba_d = nc.dram_tensor("wba", [L, P, KC, 16], F32, kind="ExternalInput").ap()
    pp_d = nc.dram_tensor("pp", [L, P, NPP], F32, kind="ExternalInput").ap()
    pr_d = nc.dram_tensor("pr", [L, 1, NPR], F32, kind="ExternalInput").ap()
    wsT_d = nc.dram_tensor("wsT", [L, P, 4, P], F32, kind="ExternalInput").ap()
    fg_d = nc.dram_tensor("fg", [P, KC], F32, kind="ExternalInput").ap()
    out_d = nc.dram_tensor("out", [NTOK, D], F32, kind="ExternalOutput").ap()
    wbf_d = nc.dram_tensor("wbf", [L, P, TOT2], BF16, kind="Internal").ap()

    def sb(name, shape, dt=F32):
        return nc.alloc_sbuf_tensor("sb_" + name, shape, dt)

    ident_bf = sb("ident_bf", [P, P], BF16)
    ident4_bf = sb("ident4_bf", [P, 4, P], BF16)
    ident_f = sb("ident_f", [P, P])
    ones_f = sb("ones_f", [P, P])
    onesD_bf = sb("onesD_bf", [P, P], BF16)
    ones512_f = sb("ones512_f", [P, P])
    onescol_bf = sb("onescol_bf", [P, 1], BF16)
    Lincl = sb("Lincl", [P, P])
    Ustr = sb("Ustr", [P, P])
    maskS = sb("maskS", [P, 4, P], BF16)
    maskIT = sb("maskIT", [P, 4, P], BF16)
    bd4 = sb("bd4", [P, 4, P], BF16)
    nbd4 = sb("nbd4", [P, 4, P], BF16)
    E4 = sb("E4", [4, P])
    CB = S.buf("consts")

    def pool(fn, reads=(), writes=()):
        return S.op("pool", fn, reads, writes)

    def dve(fn, reads=(), writes=()):
        return S.op("dve", fn, reads, writes)

    def act(fn, reads=(), writes=()):
        return S.op("act", fn, reads, writes)

    def pe(fn, reads=(), writes=(), sig=True):
        return S.op("pe", fn, reads, writes, sig=sig)

    def asel(t, pattern, cm, op, fill, base=0):
        pool(lambda e: e.affine_select(out=t, in_=t, pattern=pattern, compare_op=op, fill=fill, base=base,
                                       channel_multiplier=cm), [CB], [CB])

    pool(lambda e: e.memset(ident_f[:], 0.0), (), [CB])
    asel(ident_f[:], [[-1, P]], 1, ALU.not_equal, 1.0)
    pool(lambda e: e.tensor_copy(out=ident_bf[:], in_=ident_f[:]), [CB], [CB])
    for hh in range(4):
        pool(lambda e, hh=hh: e.tensor_copy(out=ident4_bf[:, hh, :], in_=ident_f[:]), [CB], [CB])
    pool(lambda e: e.memset(ones_f[:], 1.0), (), [CB])
    pool(lambda e: e.memset(onesD_bf[:], 1.0 / D), (), [CB])
    pool(lambda e: e.memset(ones512_f[:], 1.0 / 512), (), [CB])
    pool(lambda e: e.memset(onescol_bf[:], 1.0), (), [CB])
    pool(lambda e: e.memset(Lincl[:], 1.0), (), [CB])
    asel(Lincl[:], [[1, P]], -1, ALU.is_ge, 0.0)
    pool(lambda e: e.memset(Ustr[:], 1.0), (), [CB])
    asel(Ustr[:], [[-1, P]], 1, ALU.is_gt, 0.0)
    pool(lambda e: e.memset(maskS[:], 0.0), (), [CB])
    asel(maskS[:], [[0, 4], [-1, P]], 1, ALU.is_gt, -10000.0)
    pool(lambda e: e.memset(maskIT[:], 0.0), (), [CB])
    asel(maskIT[:], [[0, 4], [1, P]], -1, ALU.is_ge, -10000.0)
    pool(lambda e: e.memset(E4[:], 1.0), (), [CB])
    asel(E4[:], [[1, P]], -32, ALU.is_ge, 0.0)
    asel(E4[:], [[-1, P]], 32, ALU.is_ge, 0.0, base=31)

    NPS = 6
    psum_t = [nc.alloc_psum_tensor("ps%d" % i, [P, 512], F32) for i in range(NPS)]
    psum_b = [S.buf("ps%d" % i) for i in range(NPS)]
    psbf_t = [nc.alloc_psum_tensor("psbf%d" % i, [P, 4, P], BF16) for i in range(2)]
    psbf_b = [S.buf("psbf%d" % i) for i in range(2)]
    pst = {"i": 0, "j": 0}

    ps_held = set()

    def PS(hold=False):
        i = pst["i"]
        while i in ps_held:
            i = (i + 1) % NPS
        pst["i"] = (i + 1) % NPS
        if hold:
            ps_held.add(i)
        return psum_t[i], psum_b[i]

    def PS_release(pb):
        ps_held.discard(psum_b.index(pb))

    def PSB():
        i = pst["j"]
        pst["j"] = (i + 1) % 2
        return psbf_t[i], psbf_b[i]

    pt, pb = PS()
    pe(lambda e: e.matmul(pt[:, 0:P], lhsT=E4[:], rhs=E4[:], start=True, stop=True), [CB], [pb])
    for hh in range(4):
        dve(lambda e, hh=hh: e.tensor_copy(out=bd4[:, hh, :], in_=pt[:, 0:P]), [pb], [CB])
    dve(lambda e: e.tensor_scalar(out=nbd4[:], in0=bd4[:], scalar1=-1.0, scalar2=1.0, op0=ALU.mult, op1=ALU.add), [CB], [CB])

    pp = [sb("pp%d" % l, [P, NPP]) for l in range(L)]
    fg = sb("fg", [P, KC])
    negA = [sb("negA%d" % l, [P, 8]) for l in range(L)]
    dtb = [sb("dtb%d" % l, [P, 8]) for l in range(L)]
    clng = [sb("clng%d" % l, [P, 512]) for l in range(L)]
    clnb = [sb("clnb%d" % l, [P, 512]) for l in range(L)]
    bsrow = [sb("bsrow%d" % l, [1, 512]) for l in range(L)]
    onesrow_f = sb("onesrow_f", [1, P])
    wsT = [sb("wsT%d" % l, [P, 4, P], BF16) for l in range(L)]
    wba = [sb("wba%d" % l, [P, KC, 16], BF16) for l in range(L)]
    vrot = Rot(nc, S, "vrot", [P, 512], F32, 2)
    ldtmp = vrot.t[0][:, :].rearrange("p (g i) -> p g i", g=4)
    ldtmp2 = vrot.t[1][:, 0:KC * 16].rearrange("p (k c) -> p k c", k=KC)
    PB_ = S.buf("params")
    LT = vrot.b[0]
    LT2 = vrot.b[1]
    S.dma("sp", lambda e: e.dma_start(out=fg[:], in_=fg_d), (), [PB_])
    pool(lambda e: e.memset(onesrow_f[:], 1.0), (), [PB_])
    for l in range(L):
        S.dma("sp", lambda e, l=l: e.dma_start(out=pp[l][:], in_=pp_d[l]), (), [PB_])
        S.dma("sp", lambda e, l=l: e.dma_start(out=negA[l][:], in_=pr_d[l, :, PR_ALOG:PR_ALOG + 8].partition_broadcast(P)), (), [PB_])
        S.dma("sp", lambda e, l=l: e.dma_start(out=dtb[l][:], in_=pr_d[l, :, PR_DTB:PR_DTB + 8].partition_broadcast(P)), (), [PB_])
        S.dma("sp", lambda e, l=l: e.dma_start(out=clng[l][:], in_=pr_d[l, :, PR_CLNG:PR_CLNG + 512].partition_broadcast(P)), (), [PB_])
        S.dma("sp", lambda e, l=l: e.dma_start(out=clnb[l][:], in_=pr_d[l, :, PR_CLNB:PR_CLNB + 512].partition_broadcast(P)), (), [PB_])
        S.dma("sp", lambda e, l=l: e.dma_start(out=bsrow[l][:], in_=pr_d[l, :, PR_CBS:PR_CBS + 512]), (), [PB_])
        act(lambda e, l=l: e.activation(out=negA[l][:], in_=negA[l][:], func=AF.Exp), [PB_], [PB_])
        dve(lambda e, l=l: e.tensor_scalar(out=negA[l][:], in0=negA[l][:], scalar1=-1.0, scalar2=None, op0=ALU.mult), [PB_], [PB_])
        S.dma("sp", lambda e, l=l: e.dma_start(out=ldtmp, in_=wsT_d[l]), (), [LT])
        dve(lambda e, l=l: e.tensor_tensor(out=wsT[l][:], in0=ldtmp, in1=Lincl[:].unsqueeze(1).to_broadcast([P, 4, P]), op=ALU.mult),
            [LT, CB], [PB_])
        S.dma("sp", lambda e, l=l: e.dma_start(out=ldtmp2, in_=wba_d[l]), (), [LT2])
        dve(lambda e, l=l: e.tensor_copy(out=wba[l][:], in_=ldtmp2), [LT2], [PB_])

    ring = [sb("ring%d" % i, [P, SLOT_TILES * 128], BF16) for i in range(nslot)]
    ring_b = [S.buf("ring%d" % i) for i in range(nslot)]
    wbfL = [S.buf("wbf%d" % l) for l in range(L)]
    wdgL = [S.buf("wdg%d" % l) for l in range(L)]
    wbfB = [[(wdgL[l] if n.startswith("ca") else wbfL[l]) for (n, _) in GROUPS] for l in range(L)]
    for l in range(L):
        for gi, (n, tiles) in enumerate(GROUPS):
            off, nt = GOFF[n]
            if n.startswith("ca"):
                j = int(n[2:])
                dgt, dgb = ring[0][:, 0:CK * P].rearrange("p (k c) -> p k c", k=CK), ring_b[0]
                for k in range(CK):
                    dve(lambda e, l=l, j=j, k=k, dgt=dgt: e.tensor_scalar(out=dgt[:, k, :], in0=ident_f[:], scalar1=pp[l][:, PP_ADW + j * CK + k:PP_ADW + j * CK + k + 1],
                                                                          scalar2=0.5, op0=ALU.mult, op1=ALU.mult), [CB, PB_], [dgb])
                S.dma("sp", lambda e, l=l, off=off, nt=nt, dgt=dgt: e.dma_start(out=wbf_d[l, :, off:off + nt * 128], in_=ring[0][:, 0:CK * P]),
                      [dgb], [wbfB[l][gi]], track=wbfB[l][gi])
            else:
                S.dma("pool", lambda e, l=l, off=off, nt=nt: e.dma_start(out=wbf_d[l, :, off:off + nt * 128],
                                                                          in_=wbig_d[l, :, off:off + nt * 128]),
                      (), [wbfB[l][gi]])

    xT = sb("xT", [P, KC, T])
    xT_b = [S.buf("xT%d" % k) for k in range(KC)]
    hT = sb("hT", [P, KC, T], BF16)
    hT_b = [S.buf("hT%d" % k) for k in range(KC)]
    iobuf = Rot(nc, S, "io", [P, D], F32, 2)
    stA = [sb("stA%d" % l, [P, 4, CK - 1], BF16) for l in range(L)]
    stA_b = [S.buf("stA%d" % l) for l in range(L)]
    stB = [sb("stB%d" % l, [P, 24, 3]) for l in range(L)]
    stB_b = [S.buf("stB%d" % l) for l in range(L)]
    Sf = [sb("Sf%d" % l, [P, H, P]) for l in range(L)]
    Sbf = [sb("Sbf%d" % l, [P, H, P], BF16) for l in range(L)]
    S_b = [[S.buf("S%d_%d" % (l, hf)) for hf in range(2)] for l in range(L)]
    Sbf_b = [[S.buf("Sbf%d_%d" % (l, hf)) for hf in range(2)] for l in range(L)]

    f32rot = Rot(nc, S, "f32r", [P, T], F32, 4)
    sq16rot = Rot(nc, S, "sq16r", [P, T], BF16, 4)
    workA = Rot(nc, S, "workA", [P, T + CK - 1], BF16, 2)
    aT = sb("aT", [P, 4, T])
    aT_b = [S.buf("aT%d" % j) for j in range(4)]
    azT = sb("azT", [P, 4, T], BF16)
    azT_b = [S.buf("azT%d" % j) for j in range(4)]
    meanA = sb("meanA", [P, T])
    rstdA = sb("rstdA", [P, T])
    mrA_b = S.buf("mrA")
    yaT = sb("yaT", [P, 4, T], BF16)
    yaT_b = [S.buf("yaT%d" % j) for j in range(4)]
    bfrot = Rot(nc, S, "bfr", [P, T], BF16, 2)
    cuT = sb("cuT", [P, 4, T], BF16)
    cuT_b = [S.buf("cuT%d" % j) for j in range(4)]
    uczT = sb("uczT", [P, 4, T], BF16)
    uczT_b = [S.buf("uczT%d" % j) for j in range(4)]
    vs = [sb("vs%d" % b, [P, 512], BF16) for b in range(NB)]
    vs_b = [S.buf("vs%d" % b) for b in range(NB)]
    small = Rot(nc, S, "small", [P, 16], F32, 44)
    smallc = Rot(nc, S, "smallc", [P, 16], F32, 4)
    ycT = sb("ycT", [P, 4, T], BF16)
    ycT_b = [S.buf("ycT%d" % j) for j in range(4)]
    workB = Rot(nc, S, "workB", [P, T + 3], F32, 4)
    qkvT = sb("qkvT", [P, 24, T], BF16)
    qkvT_b = [S.buf("qkvT%d" % c) for c in range(24)]
    bzT = sb("bzT", [P, H, T], BF16)
    bzT_b = [S.buf("bzT%d" % h) for h in range(H)]
    ybT = sb("ybT", [P, H, T], BF16)
    ybT_b = [S.buf("ybT%d" % h) for h in range(H)]
    sqrot = Rot(nc, S, "sqT", [P, 16, P], BF16, 1)
    dn_f = [{n: Rot(nc, S, "dn%d_" % hf + n, [P, 4, P], F32, 1) for n in ("gU", "gL")} for hf in range(2)]
    dn_h = [{n: Rot(nc, S, "dn%d_" % hf + n, [P, 4, P], BF16, 1) for n in
             ("Gs", "GiT", "egR", "Mf", "Mbd", "MbdT", "Moff", "ktok", "Vb", "PmT", "qTs", "R", "Y0", "U", "on")} for hf in range(2)]
    dn_x = [Rot(nc, S, "dn%d_X" % hf, [P, 4, P], BF16, 2) for hf in range(2)]
    dn_xt = [Rot(nc, S, "dn%d_XT" % hf, [P, 4, P], BF16, 2) for hf in range(2)]
    dn_p = [Rot(nc, S, "dn%d_PT" % hf, [P, 4, P], BF16, 2) for hf in range(2)]
    dn_y = [Rot(nc, S, "dn%d_Y" % hf, [P, 4, P], BF16, 2) for hf in range(2)]
    mergedT = sb("mergedT", [P, KC, T], BF16)
    mergedT_b = [S.buf("mergedT%d" % k) for k in range(KC)]

    ring_state = {"next_load": 0, "order": [], "slot_of": {}}

    def ring_plan(order):
        ring_state["order"] = order
        ring_state["next_load"] = 0
        ring_state["slot_of"] = {}

    def ring_issue(upto):
        o = ring_state["order"]
        while ring_state["next_load"] < min(upto, len(o)):
            i = ring_state["next_load"]
            l, gi = o[i]
            n = GROUPS[gi][0]
            off, nt = GOFF[n]
            s = i % nslot
            S.dma("sp", lambda e, l=l, off=off, nt=nt, s=s: e.dma_start(out=ring[s][:, 0:nt * 128], in_=wbf_d[l, :, off:off + nt * 128]),
                  [wbfB[l][gi]], [ring_b[s]])
            ring_state["slot_of"][i] = s
            ring_state["next_load"] += 1

    ring_pos = {"i": 0}

    def ring_get(l, name):
        if S.dry:
            ring_state.setdefault("dry_list", []).append((l, name))
            return ring[0], ring_b[0]
        i = ring_pos["i"]
        o = ring_state["order"]
        assert GROUPS[o[i][1]][0] == name and o[i][0] == l, (o[i], l, name)
        ring_issue(i + nslot)
        ring_pos["i"] = i + 1
        s = ring_state["slot_of"][i]
        return ring[s], ring_b[s]

    def wtile(slot, ti):
        return slot[:, ti * 128:(ti + 1) * 128]

    evac_rr = {"i": 0}

    def evac(out, in_, reads, writes, eng=None):
        if eng is None:
            eng = "act" if evac_rr["i"] % 2 == 0 else "dve"
            evac_rr["i"] += 1
        if eng == "act":
            act(lambda e: e.copy(out=out, in_=in_), reads, writes)
        else:
            dve(lambda e: e.tensor_copy(out=out, in_=in_), reads, writes)

    def rstd_from(out, in_, reads, writes, n_tmp=None):
        act(lambda e: e.activation(out=out, in_=in_, func=AF.Ln, bias=EPS), reads, writes)
        act(lambda e: e.activation(out=out, in_=out, func=AF.Exp, scale=-0.5), writes, writes)

    def proj_chunk(slot, sbuf_, j, ntile8=KC):
        pt, pb = PS()
        for kc in range(KC):
            pe(lambda e, kc=kc: e.matmul(pt[:, 0:T], lhsT=wtile(slot, j * KC + kc), rhs=hT[:, kc, :], start=(kc == 0), stop=(kc == KC - 1)),
               [sbuf_, hT_b[kc]], [pb], sig=(kc == KC - 1))
        return pt, pb

    def rms_to(dst, dst_bufs, gcol, dst_is_h):
        pt, pb = PS()
        for kc in range(KC):
            sq, sqb = sq16rot.next()
            act(lambda e, kc=kc, sq=sq: e.activation(out=sq[:], in_=xT[:, kc, :], func=AF.Square), [xT_b[kc]], [sqb])
            pe(lambda e, kc=kc, sq=sq: e.matmul(pt[:, 0:T], lhsT=onesD_bf[:], rhs=sq[:], start=(kc == 0), stop=(kc == KC - 1)),
               [sqb, CB], [pb])
        rs, rsb = f32rot.next()
        rstd_from(rs[:], pt[:, 0:T], [pb], [rsb])
        for kc in range(KC):
            dve(lambda e, kc=kc: e.scalar_tensor_tensor(out=dst[:, kc, :], in0=xT[:, kc, :], scalar=gcol(kc), in1=rs[:], op0=ALU.mult, op1=ALU.mult),
                [xT_b[kc], rsb, PB_], [dst_bufs[kc] if isinstance(dst_bufs, list) else dst_bufs])

    def tile_layer(l):
        ppl = pp[l]
        rms_to(hT, hT_b, lambda kc: ppl[:, PP_NG + kc:PP_NG + kc + 1], True)

        pending = []

        def flush_silu():
            for (c, wk, wkb, acc, accb, wcol) in pending:
                act(lambda e, c=c, acc=acc: e.activation(out=qkvT[:, c, :], in_=acc[:], func=AF.Silu), [accb], [qkvT_b[c]])
            del pending[:]

        for gi, nm in enumerate(("q0", "q1", "k0", "k1", "v0", "v1")):
            slot, sbuf_ = ring_get(l, nm)
            for jp in range(2):
                items = []
                for jj in (2 * jp, 2 * jp + 1):
                    c = gi * 4 + jj
                    pt, pb = proj_chunk(slot, sbuf_, jj)
                    wk, wkb = workB.next()
                    wcol = lambda k, c=c: ppl[:, PP_BCONV + c * 4 + k:PP_BCONV + c * 4 + k + 1]
                    pool(lambda e, c=c, wk=wk: e.tensor_copy(out=wk[:, 0:3], in_=stB[l][:, c, :]), [stB_b[l]], [wkb])
                    act(lambda e, wk=wk, pt=pt: e.copy(out=wk[:, 3:3 + T], in_=pt[:, 0:T]), [pb], [wkb])
                    acc, accb = f32rot.next()
                    act(lambda e, acc=acc, pt=pt, wcol=wcol: e.activation(out=acc[:], in_=pt[:, 0:T], func=AF.Copy, scale=wcol(3)), [pb, PB_], [accb])
                    pool(lambda e, c=c, wk=wk: e.tensor_copy(out=stB[l][:, c, :], in_=wk[:, T:T + 3]), [wkb], [stB_b[l]])
                    items.append((c, wk, wkb, acc, accb, wcol))
                flush_silu()
                for k in range(3):
                    for (c, wk, wkb, acc, accb, wcol) in items:
                        dve(lambda e, wk=wk, acc=acc, k=k, wcol=wcol: e.scalar_tensor_tensor(out=acc[:], in0=wk[:, k:k + T], scalar=wcol(k), in1=acc[:],
                                                                                             op0=ALU.mult, op1=ALU.add), [wkb, accb, PB_], [accb])
                pending.extend(items)
        flush_silu()
        for gi, nm in enumerate(("bz0", "bz1")):
            slot, sbuf_ = ring_get(l, nm)
            for jj in range(4):
                h = gi * 4 + jj
                pt, pb = proj_chunk(slot, sbuf_, jj)
                act(lambda e, h=h, pt=pt: e.activation(out=bzT[:, h, :], in_=pt[:, 0:T], func=AF.Silu), [pb], [bzT_b[h]])

        gate_box = []

        def merge_gates(m):
            slot, sbuf_ = ring_get(l, "mg%d" % m)
            acc, accb = f32rot.next()
            gts = []
            for br in range(3):
                pt, pb = proj_chunk(slot, sbuf_, br)
                gt, gtb = f32rot.next()
                act(lambda e, pt=pt, gt=gt: e.activation(out=gt[:], in_=pt[:, 0:T], func=AF.Sigmoid), [pb], [gtb])
                gts.append((gt, gtb))
            return acc, accb, gts

        def ac_gen():
            for g2 in range(2):
                slot, sbuf_ = ring_get(l, "ag%d" % g2)
                wks = []
                for jj in range(2):
                    j = 2 * g2 + jj
                    pt, pb = proj_chunk(slot, sbuf_, 2 * jj)
                    sg, sgb = f32rot.next()
                    act(lambda e, pt=pt, sg=sg: e.activation(out=sg[:], in_=pt[:, 0:T], func=AF.Tanh, scale=0.5), [pb], [sgb])
                    pt, pb = proj_chunk(slot, sbuf_, 2 * jj + 1)
                    wk, wkb = workA.next()
                    pool(lambda e, j=j, wk=wk: e.tensor_copy(out=wk[:, 0:CK - 1], in_=stA[l][:, j, :]), [stA_b[l]], [wkb])
                    dve(lambda e, wk=wk, pt=pt, sg=sg: e.scalar_tensor_tensor(out=wk[:, CK - 1:CK - 1 + T], in0=sg[:], scalar=1.0, in1=pt[:, 0:T], op0=ALU.add, op1=ALU.mult), [pb, sgb], [wkb])
                    pool(lambda e, j=j, wk=wk: e.tensor_copy(out=stA[l][:, j, :], in_=wk[:, T:T + CK - 1]), [wkb], [stA_b[l]])
                    wks.append((j, wk, wkb))
                    yield
                for (j, wk, wkb) in wks:
                    slot, sbuf_ = ring_get(l, "ca%d" % j)
                    pt, pb = PS()
                    for k in range(CK):
                        pe(lambda e, k=k, pt=pt, wk=wk, slot=slot: e.matmul(pt[:, 0:T], lhsT=wtile(slot, k), rhs=wk[:, k:k + T], start=(k == 0), stop=(k == CK - 1)),
                           [sbuf_, wkb], [pb], sig=(k == CK - 1))
                    act(lambda e, j=j, pt=pt: e.activation(out=aT[:, j, :], in_=pt[:, 0:T], func=AF.Identity, bias=ppl[:, PP_ADWB + j:PP_ADWB + j + 1]), [pb, PB_], [aT_b[j]])
                    yield
            slot, sbuf_ = ring_get(l, "az")
            for j in range(4):
                pt, pb = proj_chunk(slot, sbuf_, j)
                act(lambda e, j=j, pt=pt: e.activation(out=azT[:, j, :], in_=pt[:, 0:T], func=AF.Silu), [pb], [azT_b[j]])
                yield
            slot, sbuf_ = ring_get(l, "cu")
            for j in range(4):
                pt, pb = proj_chunk(slot, sbuf_, j)
                act(lambda e, j=j, pt=pt: e.activation(out=cuT[:, j, :], in_=pt[:, 0:T], func=AF.Gelu_apprx_tanh), [pb], [cuT_b[j]])
                yield
            slot, sbuf_ = ring_get(l, "cv")
            for b in range(NB):
                pt, pb = PS()
                for kc in range(KC):
                    pe(lambda e, kc=kc, b=b, pt=pt: e.matmul(pt[:, 0:512], lhsT=hT[:, kc, b * P:(b + 1) * P], rhs=slot[:, kc * 512:(kc + 1) * 512],
                                                             start=(kc == 0), stop=(kc == KC - 1)), [sbuf_, hT_b[kc]], [pb], sig=(kc == KC - 1))
                vg, vgb = vrot.next()
                act(lambda e, pt=pt, vg=vg: e.activation(out=vg[:], in_=pt[:, 0:512], func=AF.Gelu_apprx_tanh), [pb], [vgb])
                st, stb = smallc.next()
                mv, mvb = smallc.next()
                dve(lambda e, vg=vg, st=st: e.bn_stats(out=st[:, 0:6], in_=vg[:]), [vgb], [stb])
                dve(lambda e, st=st, mv=mv: e.bn_aggr(out=mv[:, 0:2], in_=st[:, 0:6]), [stb], [mvb])
                rstd_from(mv[:, 2:3], mv[:, 1:2], [mvb], [mvb])
                dve(lambda e, vg=vg, mv=mv: e.tensor_scalar(out=vg[:], in0=vg[:], scalar1=mv[:, 0:1], scalar2=mv[:, 2:3], op0=ALU.subtract, op1=ALU.mult),
                    [vgb, mvb], [vgb])
                dve(lambda e, vg=vg: e.tensor_tensor(out=vg[:], in0=vg[:], in1=clng[l][:], op=ALU.mult), [vgb, PB_], [vgb])
                dve(lambda e, vg=vg, b=b: e.tensor_tensor(out=vs[b][:], in0=vg[:], in1=clnb[l][:], op=ALU.add), [vgb, PB_], [vs_b[b]])
                yield
            slot, sbuf_ = ring_get(l, "cz")
            for j in range(4):
                pt, pb = proj_chunk(slot, sbuf_, j)
                t2, t2b = bfrot.next()
                act(lambda e, pt=pt, t2=t2: e.activation(out=t2[:], in_=pt[:, 0:T], func=AF.Silu), [pb], [t2b])
                dve(lambda e, j=j, t2=t2: e.tensor_tensor(out=uczT[:, j, :], in0=t2[:], in1=cuT[:, j, :], op=ALU.mult), [t2b, cuT_b[j]], [uczT_b[j]])
                yield
            pm, pmb = PS()
            p2, p2b = PS()
            for j in range(4):
                sq, sqb = f32rot.next()
                act(lambda e, j=j, sq=sq: e.activation(out=sq[:], in_=aT[:, j, :], func=AF.Square), [aT_b[j]], [sqb])
                pe(lambda e, j=j: e.matmul(pm[:, 0:T], lhsT=ones512_f[:], rhs=aT[:, j, :], start=(j == 0), stop=(j == 3)), [aT_b[j], CB], [pmb])
                pe(lambda e, j=j, sq=sq: e.matmul(p2[:, 0:T], lhsT=ones512_f[:], rhs=sq[:], start=(j == 0), stop=(j == 3)), [sqb, CB], [p2b])
            act(lambda e: e.copy(out=meanA[:], in_=pm[:, 0:T]), [pmb], [mrA_b])
            msq, msqb = f32rot.next()
            dve(lambda e: e.tensor_tensor(out=msq[:], in0=meanA[:], in1=meanA[:], op=ALU.mult), [mrA_b], [msqb])
            dve(lambda e: e.tensor_tensor(out=rstdA[:], in0=p2[:, 0:T], in1=msq[:], op=ALU.subtract), [p2b, msqb], [mrA_b])
            rstd_from(rstdA[:], rstdA[:], [mrA_b], [mrA_b])
            yield
            for j in range(4):
                t1, t1b = f32rot.next()
                dve(lambda e, j=j, t1=t1: e.tensor_tensor(out=t1[:], in0=aT[:, j, :], in1=meanA[:], op=ALU.subtract), [aT_b[j], mrA_b], [t1b])
                dve(lambda e, t1=t1: e.tensor_tensor(out=t1[:], in0=t1[:], in1=rstdA[:], op=ALU.mult), [t1b, mrA_b], [t1b])
                t2, t2b = bfrot.next()
                act(lambda e, j=j, t1=t1, t2=t2: e.activation(out=t2[:], in_=t1[:], func=AF.Silu, scale=ppl[:, PP_ALNG + j:PP_ALNG + j + 1],
                                                              bias=ppl[:, PP_ALNB + j:PP_ALNB + j + 1]), [t1b, PB_], [t2b])
                dve(lambda e, j=j, t2=t2: e.tensor_tensor(out=yaT[:, j, :], in0=t2[:], in1=azT[:, j, :], op=ALU.mult), [t2b, azT_b[j]], [yaT_b[j]])
                yield

            for g in range(4):
                pt, pb = PS()
                for b in range(NB):
                    pe(lambda e, g=g, b=b, pt=pt: e.matmul(pt[:, b * P:(b + 1) * P], lhsT=vs[b][:, g * P:(g + 1) * P], rhs=wsT[l][:, g, :], start=True, stop=False),
                       [vs_b[b], PB_], [pb], sig=False)
                    pe(lambda e, g=g, b=b, pt=pt: e.matmul(pt[:, b * P:(b + 1) * P], lhsT=onesrow_f[:], rhs=bsrow[l][:, g * P:(g + 1) * P], start=False, stop=True),
                       [PB_], [pb], sig=(b == NB - 1))
                dve(lambda e, g=g, pt=pt: e.tensor_tensor(out=ycT[:, g, :], in0=pt[:, 0:T], in1=uczT[:, g, :], op=ALU.mult), [pb, uczT_b[g]], [ycT_b[g]])
                yield


            gate_box.append(merge_gates(0))
            yield

        def dn_gen():
            def dn_prep(b):
                t0 = b * P
                tsl = slice(t0, t0 + P)
                sqT, sqT_b = sqrot.next()
                pba, pbab = PS()
                for kc in range(KC):
                    pe(lambda e, kc=kc: e.matmul(pba[:, 0:16], lhsT=hT[:, kc, tsl], rhs=wba[l][:, kc, :], start=(kc == 0), stop=(kc == KC - 1)),
                       [hT_b[kc], PB_], [pbab], sig=(kc == KC - 1))
                act(lambda e: e.activation(out=sqT[:, :, :], in_=qkvT[:, 0:16, tsl], func=AF.Square), [qkvT_b[c16] for c16 in range(16)], [sqT_b])
                for c16 in range(16):
                    pe(lambda e, c16=c16: e.matmul(pba[:, 16 + c16:17 + c16], lhsT=sqT[:, c16, :], rhs=onescol_bf[:], start=True, stop=True),
                       [sqT_b, CB], [pbab], sig=(c16 == 15))

                def sm():
                    t, bb = small.next()
                    return t, bb

                ba, bab = sm()
                act(lambda e: e.copy(out=ba[:, 0:16], in_=pba[:, 0:16]), [pbab], [bab])
                z, zb = sm()
                dve(lambda e: e.tensor_tensor(out=z[:, 0:8], in0=ba[:, 8:16], in1=dtb[l][:], op=ALU.add), [bab, PB_], [zb])
                act(lambda e: e.activation(out=z[:, 0:8], in_=z[:, 0:8], func=AF.Exp), [zb], [zb])
                act(lambda e: e.activation(out=z[:, 0:8], in_=z[:, 0:8], func=AF.Ln, bias=1.0), [zb], [zb])
                g_, gb = sm()
                dve(lambda e: e.tensor_tensor(out=g_[:, 0:8], in0=z[:, 0:8], in1=negA[l][:], op=ALU.mult), [zb, PB_], [gb])
                beta, betab = sm()
                act(lambda e: e.activation(out=beta[:, 0:8], in_=ba[:, 0:8], func=AF.Exp, scale=-1.0), [bab], [betab])
                dve(lambda e: e.tensor_scalar(out=beta[:, 0:8], in0=beta[:, 0:8], scalar1=1.0, scalar2=None, op0=ALU.add), [betab], [betab])
                dve(lambda e: e.reciprocal(out=beta[:, 0:8], in_=beta[:, 0:8]), [betab], [betab])
                rinv, rinvb = sm()
                rstd_from(rinv[:, 0:16], pba[:, 16:32], [pbab], [rinvb])
                a2, a2b = sm()
                dve(lambda e: e.tensor_tensor(out=a2[:, 0:8], in0=rinv[:, 8:16], in1=rinv[:, 8:16], op=ALU.mult), [rinvb], [a2b])
                dve(lambda e: e.scalar_tensor_tensor(out=a2[:, 0:8], in0=beta[:, 0:8], scalar=-1.0, in1=a2[:, 0:8], op0=ALU.mult, op1=ALU.mult), [betab, a2b], [a2b])
                cV, cVb = sm()
                dve(lambda e: e.tensor_tensor(out=cV[:, 0:8], in0=beta[:, 0:8], in1=rinv[:, 8:16], op=ALU.mult), [betab, rinvb], [cVb])
                osc, oscb = sm()
                dve(lambda e: e.tensor_scalar(out=osc[:, 0:8], in0=rinv[:, 0:8], scalar1=float(P) ** -0.5, scalar2=None, op0=ALU.mult), [rinvb], [oscb])
                pg, pgb = PS()
                pe(lambda e: e.matmul(pg[:, 0:8], lhsT=Lincl[:], rhs=g_[:, 0:8], start=True, stop=True), [gb, CB], [pgb], sig=False)
                pe(lambda e: e.matmul(pg[:, 8:16], lhsT=ones_f[:], rhs=g_[:, 0:8], start=True, stop=True), [gb, CB], [pgb])
                gc, gcb = sm()
                act(lambda e: e.copy(out=gc[:, 0:16], in_=pg[:, 0:16]), [pgb], [gcb])
                egc, egcb = sm()
                act(lambda e: e.activation(out=egc[:, 0:8], in_=pg[:, 0:8], func=AF.Exp), [pgb], [egcb])
                edl, edlb = sm()
                act(lambda e: e.activation(out=edl[:, 0:8], in_=pg[:, 8:16], func=AF.Exp), [pgb], [edlb])
                edec, edecb = sm()
                dve(lambda e: e.tensor_tensor(out=edec[:, 0:8], in0=gc[:, 8:16], in1=gc[:, 0:8], op=ALU.subtract), [gcb], [edecb])
                act(lambda e: e.activation(out=edec[:, 0:8], in_=edec[:, 0:8], func=AF.Exp), [edecb], [edecb])
                cKS, cKSb = sm()
                dve(lambda e: e.tensor_tensor(out=cKS[:, 0:8], in0=a2[:, 0:8], in1=egc[:, 0:8], op=ALU.mult), [a2b, egcb], [cKSb])

                return dict(locals())

            sc_next = dn_prep(0)
            yield
            for b in range(NB):
                sc = sc_next
                if b + 1 < NB:
                    sc_next = dn_prep(b + 1)
                (tsl, g_, gb, a2, a2b, cV, cVb, osc, oscb, edl, edlb, edec, edecb, cKS, cKSb) = [sc[k] for k in (
                    "tsl", "g_", "gb", "a2", "a2b", "cV", "cVb", "osc", "oscb", "edl", "edlb", "edec", "edecb", "cKS", "cKSb")]

                def sm():
                    t, bb = small.next()
                    return t, bb

                def dn_unit(hf):
                    h0 = 4 * hf
                    gU, gUb = dn_f[hf]["gU"].next()
                    gL, gLb = dn_f[hf]["gL"].next()
                    dve(lambda e, h0=h0, gU=gU: e.tensor_tensor(out=gU[:], in0=Ustr[:].unsqueeze(1).to_broadcast([P, 4, P]),
                                                                in1=g_[:, h0:h0 + 4].unsqueeze(2).to_broadcast([P, 4, P]), op=ALU.mult), [gb, CB], [gUb])
                    pool(lambda e, h0=h0, gL=gL: e.tensor_tensor(out=gL[:], in0=Lincl[:].unsqueeze(1).to_broadcast([P, 4, P]),
                                                                 in1=g_[:, h0:h0 + 4].unsqueeze(2).to_broadcast([P, 4, P]), op=ALU.mult), [gb, CB], [gLb])
                    flat = lambda t: t[:].rearrange("p h j -> p (h j)")
                    Gs, Gsb = dn_h[hf]["Gs"].next()
                    GiT, GiTb = dn_h[hf]["GiT"].next()
                    egR, egRb = dn_h[hf]["egR"].next()
                    pt, pb = PS()
                    pe(lambda e, pt=pt, gU=gU: e.matmul(pt[:, :], lhsT=Lincl[:], rhs=flat(gU), start=True, stop=False), [gUb, CB], [pb], sig=False)
                    pe(lambda e, pt=pt: e.matmul(pt[:, :], lhsT=ident_bf[:], rhs=flat(maskS), start=False, stop=True), [CB], [pb])
                    act(lambda e, pt=pt, Gs=Gs: e.activation(out=flat(Gs), in_=pt[:, :], func=AF.Exp), [pb], [Gsb])
                    pt, pb = PS()
                    pe(lambda e, pt=pt, gL=gL: e.matmul(pt[:, :], lhsT=Ustr[:], rhs=flat(gL), start=True, stop=False), [gLb, CB], [pb], sig=False)
                    pe(lambda e, pt=pt: e.matmul(pt[:, :], lhsT=ident_bf[:], rhs=flat(maskIT), start=False, stop=True), [CB], [pb])
                    act(lambda e, pt=pt, GiT=GiT: e.activation(out=flat(GiT), in_=pt[:, :], func=AF.Exp), [pb], [GiTb])
                    pt, pb = PS()
                    pe(lambda e, pt=pt, gL=gL: e.matmul(pt[:, :], lhsT=ones_f[:], rhs=flat(gL), start=True, stop=True), [gLb, CB], [pb])
                    act(lambda e, pt=pt, egR=egR: e.activation(out=flat(egR), in_=pt[:, :], func=AF.Exp), [pb], [egRb])
                    yield
                    qTs, qTsb = dn_h[hf]["qTs"].next()
                    dve(lambda e, qTs=qTs, egR=egR, h0=h0: e.tensor_tensor(out=qTs[:], in0=qkvT[:, h0:h0 + 4, tsl], in1=egR[:], op=ALU.mult),
                        [qkvT_b[h0 + i] for i in range(4)] + [egRb], [qTsb])
                    ktok, ktokb = dn_h[hf]["ktok"].next()
                    Vb, Vbb = dn_h[hf]["Vb"].next()
                    ptb, pbb = PSB()
                    for hh in range(4):
                        pe(lambda e, hh=hh, ptb=ptb: e.transpose(ptb[:, hh, :], qkvT[:, 8 + h0 + hh, tsl], ident_bf[:]), [qkvT_b[8 + h0 + hh], CB], [pbb], sig=(hh == 3))
                    dve(lambda e, ptb=ptb, ktok=ktok: e.tensor_tensor(out=ktok[:], in0=ptb[:], in1=edec[:, h0:h0 + 4].unsqueeze(2).to_broadcast([P, 4, P]), op=ALU.mult),
                        [pbb, edecb], [ktokb])
                    yield
                    ptb, pbb = PSB()
                    for hh in range(4):
                        pe(lambda e, hh=hh, ptb=ptb: e.transpose(ptb[:, hh, :], qkvT[:, 16 + h0 + hh, tsl], ident_bf[:]), [qkvT_b[16 + h0 + hh], CB], [pbb], sig=(hh == 3))
                    dve(lambda e, ptb=ptb, Vb=Vb: e.tensor_tensor(out=Vb[:], in0=ptb[:], in1=cV[:, h0:h0 + 4].unsqueeze(2).to_broadcast([P, 4, P]), op=ALU.mult),
                        [pbb, cVb], [Vbb])
                    yield
                    Mf, Mfb = dn_h[hf]["Mf"].next()
                    PmT, PmTb = dn_h[hf]["PmT"].next()
                    pt, pb = PS()
                    for hh in range(4):
                        kT = qkvT[:, 8 + h0 + hh, tsl]
                        pe(lambda e, hh=hh, pt=pt, kT=kT: e.matmul(pt[:, hh * P:(hh + 1) * P], lhsT=kT, rhs=kT, start=True, stop=True), [qkvT_b[8 + h0 + hh]], [pb], sig=(hh == 3))
                    for hh in range(4):
                        dve(lambda e, hh=hh, pt=pt, Mf=Mf, Gs=Gs: e.scalar_tensor_tensor(out=Mf[:, hh, :], in0=pt[:, hh * P:(hh + 1) * P], scalar=a2[:, h0 + hh:h0 + hh + 1],
                                                                                         in1=Gs[:, hh, :], op0=ALU.mult, op1=ALU.mult), [pb, a2b, Gsb], [Mfb])
                    pt, pb = PS()
                    for hh in range(4):
                        pe(lambda e, hh=hh, pt=pt: e.matmul(pt[:, hh * P:(hh + 1) * P], lhsT=qkvT[:, 8 + h0 + hh, tsl], rhs=qkvT[:, h0 + hh, tsl], start=True, stop=True),
                           [qkvT_b[8 + h0 + hh], qkvT_b[h0 + hh]], [pb], sig=(hh == 3))
                    dve(lambda e, pt=pt, PmT=PmT, GiT=GiT: e.tensor_tensor(out=flat(PmT), in0=pt[:, :], in1=flat(GiT), op=ALU.mult), [pb, GiTb], [PmTb])
                    yield
                    Mbd, Mbdb = dn_h[hf]["Mbd"].next()
                    MbdT, MbdTb = dn_h[hf]["MbdT"].next()
                    pool(lambda e, Mbd=Mbd, Mf=Mf: e.tensor_tensor(out=Mbd[:], in0=Mf[:], in1=bd4[:], op=ALU.mult), [Mfb, CB], [Mbdb])
                    ptb, pbb = PSB()
                    for hh in range(4):
                        pe(lambda e, hh=hh, ptb=ptb, Mf=Mf: e.transpose(ptb[:, hh, :], Mf[:, hh, :], ident_bf[:]), [Mfb, CB], [pbb], sig=(hh == 3))
                    dve(lambda e, ptb=ptb, MbdT=MbdT: e.tensor_tensor(out=MbdT[:], in0=ptb[:], in1=bd4[:], op=ALU.mult), [pbb, CB], [MbdTb])
                    yield
                    Moff, Moffb = dn_h[hf]["Moff"].next()
                    pool(lambda e, Moff=Moff, Mf=Mf: e.tensor_tensor(out=Moff[:], in0=Mf[:], in1=nbd4[:], op=ALU.mult), [Mfb, CB], [Moffb])

                    def grp(lh, lhb, rh, rhb, extra=None, exb=None):
                        pt_, pb_ = PS()
                        for hh in range(4):
                            if extra is None:
                                pe(lambda e, hh=hh: e.matmul(pt_[:, hh * P:(hh + 1) * P], lhsT=lh[:, hh, :], rhs=rh[:, hh, :], start=True, stop=True),
                                   [lhb, rhb], [pb_], sig=(hh == 3))
                            else:
                                pe(lambda e, hh=hh: e.matmul(pt_[:, hh * P:(hh + 1) * P], lhsT=lh[:, hh, :], rhs=rh[:, hh, :], start=True, stop=False),
                                   [lhb, rhb], [pb_], sig=False)
                                pe(lambda e, hh=hh: e.matmul(pt_[:, hh * P:(hh + 1) * P], lhsT=ident_bf[:], rhs=extra[:, hh, :], start=False, stop=True),
                                   [exb, CB], [pb_], sig=(hh == 3))
                        return pt_, pb_

                    PT, PTb = dn_p[hf].next()
                    pool(lambda e, PT=PT, MbdT=MbdT: e.tensor_tensor(out=PT[:], in0=MbdT[:], in1=ident4_bf[:], op=ALU.add), [MbdTb, CB], [PTb])
                    X, Xb, XT, XTb = Mbd, Mbdb, MbdT, MbdTb
                    X2, X2b = dn_x[hf].next()
                    X2T, X2Tb = dn_xt[hf].next()
                    p1, p1b = grp(XT, XTb, X, Xb)
                    p2, p2b = grp(X, Xb, XT, XTb)
                    evac(flat(X2), p1[:, :], [p1b], [X2b], eng="act")
                    evac(flat(X2T), p2[:, :], [p2b], [X2Tb], eng="dve")
                    yield
                    X, Xb, XT, XTb = X2, X2b, X2T, X2Tb
                    for k in range(3):
                        PT2, PT2b = dn_p[hf].next()
                        p3, p3b = grp(X, Xb, PT, PTb, extra=PT, exb=PTb)
                        Xn, Xnb = dn_x[hf].next()
                        p1, p1b = grp(XT, XTb, X, Xb)
                        if k < 2:
                            XnT, XnTb = dn_xt[hf].next()
                            p2, p2b = grp(X, Xb, XT, XTb)
                        evac(flat(PT2), p3[:, :], [p3b], [PT2b], eng="act")
                        evac(flat(Xn), p1[:, :], [p1b], [Xnb], eng="dve")
                        if k < 2:
                            evac(flat(XnT), p2[:, :], [p2b], [XnTb], eng="act")
                        else:
                            XnT, XnTb = None, None
                        yield
                        PT, PTb, X, Xb, XT, XTb = PT2, PT2b, Xn, Xnb, XnT, XnTb
                    PT2, PT2b = dn_p[hf].next()
                    p3, p3b = grp(X, Xb, PT, PTb, extra=PT, exb=PTb)
                    evac(flat(PT2), p3[:, :], [p3b], [PT2b])
                    yield
                    TbdT, TbdTb = PT2, PT2b
                    Nm, Nmb = dn_x[hf].next()
                    NT, NTb = dn_xt[hf].next()
                    NT1, NT1b = dn_x[hf].next()
                    p1, p1b = grp(TbdT, TbdTb, Moff, Moffb)
                    p2, p2b = grp(Moff, Moffb, TbdT, TbdTb)
                    evac(flat(Nm), p1[:, :], [p1b], [Nmb], eng="act")
                    evac(flat(NT), p2[:, :], [p2b], [NTb], eng="act")
                    pool(lambda e, NT=NT, NT1=NT1: e.tensor_tensor(out=NT1[:], in0=NT[:], in1=ident4_bf[:], op=ALU.add), [NTb, CB], [NT1b])
                    yield
                    N2T1, N2T1b = dn_xt[hf].next()
                    p1, p1b = grp(Nm, Nmb, NT, NTb, extra=ident4_bf, exb=CB)
                    evac(flat(N2T1), p1[:, :], [p1b], [N2T1b])
                    yield
                    R, Rb = dn_h[hf]["R"].next()
                    pt, pb = PS()
                    for hh in range(4):
                        pe(lambda e, hh=hh, pt=pt: e.matmul(pt[:, hh * P:(hh + 1) * P], lhsT=qkvT[:, 8 + h0 + hh, tsl], rhs=Sbf[l][:, h0 + hh, :], start=True, stop=True),
                           [qkvT_b[8 + h0 + hh], Sbf_b[l][hf]], [pb], sig=(hh == 3))
                    for hh in range(4):
                        dve(lambda e, hh=hh, pt=pt, R=R, Vb=Vb: e.scalar_tensor_tensor(out=R[:, hh, :], in0=pt[:, hh * P:(hh + 1) * P], scalar=cKS[:, h0 + hh:h0 + hh + 1],
                                                                                       in1=Vb[:, hh, :], op0=ALU.mult, op1=ALU.add), [pb, cKSb, Vbb], [Rb])
                    yield
                    Y0, Y0b = dn_h[hf]["Y0"].next()
                    p1, p1b = grp(TbdT, TbdTb, R, Rb)
                    evac(flat(Y0), p1[:, :], [p1b], [Y0b])
                    yield
                    Y1, Y1b = dn_h[hf]["U"].next()
                    p1, p1b = grp(N2T1, N2T1b, Y0, Y0b)
                    evac(flat(Y1), p1[:, :], [p1b], [Y1b])
                    yield
                    W, Wb = dn_y[hf].next()
                    p1, p1b = grp(NT1, NT1b, Y1, Y1b)
                    evac(flat(W), p1[:, :], [p1b], [Wb])
                    yield
                    po, pob = PS(hold=True)
                    for hh in range(4):
                        pe(lambda e, hh=hh, qTs=qTs: e.matmul(po[:, hh * P:(hh + 1) * P], lhsT=qTs[:, hh, :], rhs=Sbf[l][:, h0 + hh, :], start=True, stop=False),
                           [qTsb, Sbf_b[l][hf]], [pob], sig=False)
                        pe(lambda e, hh=hh, PmT=PmT, W=W: e.matmul(po[:, hh * P:(hh + 1) * P], lhsT=PmT[:, hh, :], rhs=W[:, hh, :], start=False, stop=True),
                           [PmTb, Wb], [pob], sig=(hh == 3))
                    pt, pb = PS()
                    for hh in range(4):
                        pe(lambda e, hh=hh, pt=pt, ktok=ktok, W=W: e.matmul(pt[:, hh * P:(hh + 1) * P], lhsT=ktok[:, hh, :], rhs=W[:, hh, :], start=True, stop=True),
                           [ktokb, Wb], [pb], sig=(hh == 3))
                    for hh in range(4):
                        dve(lambda e, hh=hh, pt=pt: e.scalar_tensor_tensor(out=Sf[l][:, h0 + hh, :], in0=Sf[l][:, h0 + hh, :], scalar=edl[:, h0 + hh:h0 + hh + 1],
                                                                           in1=pt[:, hh * P:(hh + 1) * P], op0=ALU.mult, op1=ALU.add), [pb, edlb, S_b[l][hf]], [S_b[l][hf]])
                    act(lambda e: e.copy(out=Sbf[l][:, h0:h0 + 4, :], in_=Sf[l][:, h0:h0 + 4, :]), [S_b[l][hf]], [Sbf_b[l][hf]])
                    ss, ssb = sm()
                    junk, junkb = vrot.next()
                    act(lambda e, junk=junk: e.activation(out=junk[:, :], in_=po[:, :], func=AF.Square), [pob], [junkb])
                    dve(lambda e, junk=junk: e.reduce_sum(out=ss[:, 0:4], in_=junk[:, :].rearrange("p (h e) -> p h e", h=4), axis=mybir.AxisListType.X), [junkb], [ssb])
                    dve(lambda e: e.tensor_tensor(out=ss[:, 4:8], in0=osc[:, h0:h0 + 4], in1=osc[:, h0:h0 + 4], op=ALU.mult), [oscb, ssb], [ssb])
                    dve(lambda e: e.scalar_tensor_tensor(out=ss[:, 0:4], in0=ss[:, 0:4], scalar=1.0 / P, in1=ss[:, 4:8], op0=ALU.mult, op1=ALU.mult), [ssb], [ssb])
                    rstd_from(ss[:, 0:4], ss[:, 0:4], [ssb], [ssb])
                    dve(lambda e: e.tensor_tensor(out=ss[:, 8:12], in0=ss[:, 0:4], in1=osc[:, h0:h0 + 4], op=ALU.mult), [ssb, oscb], [ssb])
                    on, onb = dn_h[hf]["on"].next()
                    dve(lambda e, on=on: e.tensor_tensor(out=on[:], in0=po[:, :].rearrange("p (h e) -> p h e", h=4), in1=ss[:, 8:12].unsqueeze(2).to_broadcast([P, 4, P]), op=ALU.mult),
                        [pob, ssb], [onb])
                    PS_release(pob)
                    yield
                    ptb, pbb = PSB()
                    for hh in range(4):
                        pe(lambda e, hh=hh, ptb=ptb, on=on: e.transpose(ptb[:, hh, :], on[:, hh, :], ident_bf[:]), [onb, CB], [pbb], sig=(hh == 3))
                    dve(lambda e, ptb=ptb: e.scalar_tensor_tensor(out=ybT[:, h0:h0 + 4, tsl], in0=ptb[:], scalar=ppl[:, PP_ONG:PP_ONG + 1], in1=bzT[:, h0:h0 + 4, tsl],
                                                                  op0=ALU.mult, op1=ALU.mult), [pbb, PB_] + [bzT_b[h0 + i] for i in range(4)], [ybT_b[h0 + i] for i in range(4)])

                gens = [dn_unit(0), dn_unit(1)]
                while gens:
                    nxt = []
                    for gen in gens:
                        try:
                            next(gen)
                            nxt.append(gen)
                        except StopIteration:
                            pass
                    gens = nxt
                    yield


        alive = [dn_gen(), ac_gen()]
        while alive:
            nxt = []
            for gen_ in alive:
                try:
                    next(gen_)
                    nxt.append(gen_)
                except StopIteration:
                    pass
            alive = nxt

        br_src = [(yaT, yaT_b, 4, 0), (ybT, ybT_b, 8, 4), (ycT, ycT_b, 4, 12)]
        for m in range(8):
            if m == 0:
                acc, accb, gts = gate_box.pop()
            else:
                acc, accb, gts = merge_gates(m)
            slot, sbuf_ = ring_get(l, "pj%d" % m)
            for br in range(3):
                gt, gtb = gts[br]
                src, srcb, nk, base = br_src[br]
                pp_, ppb = PS()
                for kc in range(nk):
                    pe(lambda e, kc=kc, pp_=pp_, src=src, base=base, slot=slot: e.matmul(pp_[:, 0:T], lhsT=wtile(slot, base + kc), rhs=src[:, kc, :], start=(kc == 0), stop=(kc == nk - 1)),
                       [sbuf_, srcb[kc]], [ppb], sig=(kc == nk - 1))
                if br == 0:
                    dve(lambda e, pp_=pp_, gt=gt, acc=acc: e.tensor_tensor(out=acc[:], in0=pp_[:, 0:T], in1=gt[:], op=ALU.mult), [ppb, gtb], [accb])
                else:
                    dve(lambda e, pp_=pp_, gt=gt: e.tensor_tensor(out=gt[:], in0=pp_[:, 0:T], in1=gt[:], op=ALU.mult), [ppb, gtb], [gtb])
                    if br == 1:
                        pool(lambda e, gt=gt, acc=acc: e.tensor_tensor(out=acc[:], in0=acc[:], in1=gt[:], op=ALU.add), [accb, gtb], [accb])
                    else:
                        pool(lambda e, gt=gt, acc=acc, m=m: e.tensor_tensor(out=mergedT[:, m, :], in0=acc[:], in1=gt[:], op=ALU.add), [accb, gtb], [mergedT_b[m]])
        for g in range(2):
            slot, sbuf_ = ring_get(l, "wo%d" % g)
            for mm_ in range(4):
                m = 4 * g + mm_
                pt, pb = PS()
                for kc in range(KC):
                    pe(lambda e, kc=kc, pt=pt, mm_=mm_: e.matmul(pt[:, 0:T], lhsT=wtile(slot, mm_ * KC + kc), rhs=mergedT[:, kc, :], start=(kc == 0), stop=(kc == KC - 1)),
                       [sbuf_, mergedT_b[kc]], [pb], sig=(kc == KC - 1))
                dve(lambda e, m=m, pt=pt: e.tensor_tensor(out=xT[:, m, :], in0=xT[:, m, :], in1=pt[:, 0:T], op=ALU.add), [pb, xT_b[m]], [xT_b[m]])

    S.dry = True
    for l in range(L):
        tile_layer(l)
    S.dry = False
    name2gi = {n: i for i, (n, _) in enumerate(GROUPS)}
    order = [(l, name2gi[n]) for (l, n) in ring_state["dry_list"]] * (nseq * ntile)
    ring_plan(order)

    out_bufs = []
    for s in range(nseq):
        for l in range(L):
            pool(lambda e, l=l: e.memset(stA[l][:], 0.0), (), [stA_b[l]])
            pool(lambda e, l=l: e.memset(stB[l][:], 0.0), (), [stB_b[l]])
            for hf in range(2):
                pool(lambda e, l=l, hf=hf: e.memset(Sf[l][:, 4 * hf:4 * hf + 4, :], 0.0), (), [S_b[l][hf]])
                pool(lambda e, l=l, hf=hf: e.memset(Sbf[l][:, 4 * hf:4 * hf + 4, :], 0.0), (), [Sbf_b[l][hf]])
        for ti in range(ntile):
            tok0 = s * seqlen + ti * T
            for b in range(NB):
                xin, xinb = iobuf.next()
                S.dma("sp", lambda e, xin=xin, b=b: e.dma_start(out=xin[:], in_=x_d[tok0 + b * P: tok0 + (b + 1) * P, :]), (), [xinb])
                for half in range(2):
                    pt, pb = PS()
                    for kk in range(4):
                        kc = half * 4 + kk
                        pe(lambda e, kk=kk, kc=kc, pt=pt, xin=xin: e.matmul(pt[:, kk * P:(kk + 1) * P], lhsT=xin[:, kc * P:(kc + 1) * P], rhs=ident_f[:], start=True, stop=True),
                           [xinb, CB], [pb], sig=(kk == 3))
                    evac(xT[:, half * 4:half * 4 + 4, b * P:(b + 1) * P], pt[:, :].rearrange("p (k t) -> p k t", k=4), [pb], [xT_b[half * 4 + i] for i in range(4)])
            for l in range(L):
                tile_layer(l)
            rms_to(xT, xT_b, lambda kc: fg[:, kc:kc + 1], False)
            for b in range(NB):
                ot, otb = iobuf.next()
                for half in range(2):
                    pt, pb = PS()
                    for kk in range(4):
                        kc = half * 4 + kk
                        pe(lambda e, kk=kk, kc=kc, pt=pt, b=b: e.matmul(pt[:, kk * P:(kk + 1) * P], lhsT=xT[:, kc, b * P:(b + 1) * P], rhs=ident_f[:], start=True, stop=True),
                           [xT_b[kc], CB], [pb], sig=(kk == 3))
                    evac(ot[:, half * 512:(half + 1) * 512], pt[:, :], [pb], [otb])
                S.dma("sp", lambda e, ot=ot, b=b: e.dma_start(out=out_d[tok0 + b * P: tok0 + (b + 1) * P, :], in_=ot[:]), [otb], ())
                out_bufs.append(otb)
    S.wait_all("sp", iobuf.b)
    build.last_sched = S
    return nc


_CACHE = {}


def kernel(**inputs):
    x = np.asarray(inputs["x"], np.float32)
    B, SEQ, _ = x.shape
    L = inputs["w_in"].shape[0]
    nseq = B // NCORES
    lay = host_layout({k: np.asarray(v, np.float32) for k, v in inputs.items()}, L)
    key = (nseq, SEQ, L)
    if key not in _CACHE:
        _CACHE[key] = build(nseq, SEQ, L)
    nc = _CACHE[key]
    in_maps = []
    for c in range(NCORES):
        m = dict(lay)
        m["x"] = np.ascontiguousarray(x[c * nseq:(c + 1) * nseq].reshape(nseq * SEQ, D))
        in_maps.append(m)
    res = run_bass_kernel_spmd(nc, in_maps, core_ids=list(range(NCORES)))
    out = np.stack([np.asarray(r["out"]).reshape(nseq, SEQ, D) for r in res.results], axis=0)
    return out.reshape(B, SEQ, D).astype(np.float32)
```
